# Optimizing a Trainium2 kernel written in Bass

```python
import jax, jax.numpy as jnp
from jax import lax
import numpy as np

D_MODEL = 1024
BATCH = 2
SEQ = 8192
DEPTH = 2

GRID_W = 64
RMS_EPS = 1e-6

ML_WIDTH = D_MODEL
ML_HEADS = 4
ML_HEAD_DIM = ML_WIDTH // ML_HEADS
ML_CHUNK = 64

NA_WIDTH = D_MODEL
NA_HEADS = 16
NA_HEAD_DIM = NA_WIDTH // NA_HEADS
NA_KH = 8
NA_KW = 16

GQ_WIDTH = D_MODEL
GQ_HEADS = 8
GQ_HEAD_DIM = GQ_WIDTH // GQ_HEADS
GQ_KV_HEADS = 2
GQ_KV_WIDTH = GQ_KV_HEADS * GQ_HEAD_DIM
GQ_BLOCK = 128
ROPE_THETA = 10000.0
ROPE_HALF = GQ_HEAD_DIM // 4

CV_WIDTH = D_MODEL
CV_K = 3

EVEN_SPLIT = (ML_WIDTH,) * 5 + (4 * ML_HEADS,) + (NA_WIDTH,) * 4
ODD_SPLIT = (GQ_WIDTH, GQ_KV_WIDTH, GQ_KV_WIDTH, GQ_WIDTH) + (CV_WIDTH,) * 4
EVEN_IN = sum(EVEN_SPLIT)
ODD_IN = sum(ODD_SPLIT)
MIX_EVEN = ML_WIDTH + NA_WIDTH
MIX_ODD = GQ_WIDTH + CV_WIDTH

kernel_name = "hybrid_mlstm_natten_gqa_shortconv_encoder"


def _split(a, sizes):
    return jnp.split(a, np.cumsum(sizes)[:-1].tolist(), axis=-1)


def rmsnorm(x, g):
    x32 = x.astype(jnp.float32)
    y = x32 * lax.rsqrt(jnp.mean(x32 * x32, axis=-1, keepdims=True) + RMS_EPS)
    return (y * g.astype(jnp.float32)).astype(x.dtype)


def mlstm_scan(q, k, v, ig, lf):
    B, H, S, dh = q.shape
    L = ML_CHUNK
    nc = S // L

    def chunks(a):
        return jnp.moveaxis(a.reshape(B, H, nc, L, *a.shape[3:]), 2, 0)

    tril = jnp.tril(jnp.ones((L, L), dtype=bool))

    def step(carry, xs):
        c_mat, n_vec, m = carry
        qc, kc, vc, igc, lfc = xs
        b = jnp.cumsum(lfc, axis=-1)
        g = b[..., -1]
        d = jnp.where(tril, b[..., :, None] - b[..., None, :] + igc[..., None, :], -jnp.inf)
        inter = b + m[..., None]
        m_t = jnp.maximum(inter, jnp.max(d, axis=-1))
        w = jnp.exp(d - m_t[..., None]) * jnp.einsum('bhtd,bhsd->bhts', qc, kc)
        decay = jnp.exp(inter - m_t)
        num = decay[..., None] * jnp.einsum('bhvk,bhtk->bhtv', c_mat, qc) + jnp.einsum('bhts,bhsv->bhtv', w, vc)
        den = decay * jnp.einsum('bhk,bhtk->bht', n_vec, qc) + jnp.sum(w, axis=-1)
        h = num / jnp.maximum(jnp.abs(den), jnp.exp(-m_t))[..., None]
        a = g[..., None] - b + igc
        m_new = jnp.maximum(g + m, jnp.max(a, axis=-1))
        carry_scale = jnp.exp(g + m - m_new)
        wk = jnp.exp(a - m_new[..., None])
        c_new = carry_scale[..., None, None] * c_mat + jnp.einsum('bhsv,bhsk->bhvk', wk[..., None] * vc, kc)
        n_new = carry_scale[..., None] * n_vec + jnp.einsum('bhs,bhsk->bhk', wk, kc)
        return (c_new, n_new, m_new), h

    f32 = jnp.float32
    init = (jnp.zeros((B, H, dh, dh), f32), jnp.zeros((B, H, dh), f32), jnp.zeros((B, H), f32))
    _, h = lax.scan(step, init, (chunks(q), chunks(k), chunks(v), chunks(ig), chunks(lf)))
    return jnp.moveaxis(h, 0, 2).reshape(B, H, S, dh)


def mlstm_branch(q, k, v, o, z, gates, gate_b, norm_g):
    B, S, _ = q.shape
    f32 = jnp.float32

    def heads(a):
        return a.reshape(B, S, ML_HEADS, ML_HEAD_DIM).transpose(0, 2, 1, 3).astype(f32)

    qh, kh, vh = heads(q), heads(k) * (ML_HEAD_DIM ** -0.5), heads(v)
    g = (gates.astype(f32) + gate_b.astype(f32)).reshape(B, S, 4, ML_HEADS).transpose(2, 0, 3, 1)
    ig_f, lf_f = g[0], jax.nn.log_sigmoid(g[1])
    ig_b, lf_b = g[2], jax.nn.log_sigmoid(g[3])
    h_f = mlstm_scan(qh, kh, vh, ig_f, lf_f)
    rev = lambda a: jnp.flip(a, axis=2)
    h_b = rev(mlstm_scan(rev(qh), rev(kh), rev(vh), rev(ig_b), rev(lf_b)))
    h = h_f + h_b
    h = h * lax.rsqrt(jnp.mean(h * h, axis=-1, keepdims=True) + RMS_EPS)
    h = h.transpose(0, 2, 1, 3).reshape(B, S, ML_WIDTH) * norm_g.astype(f32)
    out = h * jax.nn.sigmoid(o.astype(f32)) * jax.nn.silu(z.astype(f32))
    return out.astype(q.dtype)


def na_branch(q, k, v, z, rpb):
    B, S, _ = q.shape
    rows = S // GRID_W
    kh = min(NA_KH, rows)

    def grid(a):
        return a.reshape(B, rows, GRID_W, NA_HEADS, NA_HEAD_DIM).transpose(0, 3, 1, 2, 4)

    qg, kg, vg = grid(q) * (NA_HEAD_DIM ** -0.5), grid(k), grid(v)
    c = jnp.arange(GRID_W)
    c0 = jnp.clip(c - NA_KW // 2, 0, GRID_W - NA_KW)
    col_idx = c0[:, None] + jnp.arange(NA_KW)[None, :]
    dc = col_idx - c[:, None] + (NA_KW - 1)

    def row(r):
        r0 = jnp.clip(r - kh // 2, 0, rows - kh)
        kr = lax.dynamic_slice_in_dim(kg, r0, kh, axis=2)
        vr = lax.dynamic_slice_in_dim(vg, r0, kh, axis=2)
        kw = jnp.take(kr, col_idx, axis=3)
        vw = jnp.take(vr, col_idx, axis=3)
        qr = lax.dynamic_index_in_dim(qg, r, axis=2, keepdims=False)
        s = jnp.einsum('bhcd,bhicjd->bhcij', qr, kw).astype(jnp.float32)
        dr = r0 + jnp.arange(kh) - r + (NA_KH - 1)
        bias = rpb[:, dr[None, :, None], dc[:, None, :]]
        s = s + bias.astype(jnp.float32)
        p = jax.nn.softmax(s.reshape(B, NA_HEADS, GRID_W, kh * NA_KW), axis=-1).reshape(s.shape)
        return jnp.einsum('bhcij,bhicjd->bhcd', p.astype(vw.dtype), vw)

    out = lax.map(row, jnp.arange(rows))
    out = out.transpose(1, 0, 3, 2, 4).reshape(B, S, NA_WIDTH)
    return out * jax.nn.silu(z)


def axial_rope_tables(S):
    t = jnp.arange(S)
    pos = jnp.stack([t // GRID_W, t % GRID_W], axis=-1).astype(jnp.float32)
    inv = ROPE_THETA ** (-jnp.arange(ROPE_HALF, dtype=jnp.float32) / ROPE_HALF)
    ang = pos[:, :, None] * inv
    return jnp.cos(ang)[:, None], jnp.sin(ang)[:, None]


def apply_rope(x, cos, sin):
    xs = x.astype(jnp.float32).reshape(*x.shape[:-1], 2, 2, ROPE_HALF)
    x1, x2 = xs[..., 0, :], xs[..., 1, :]
    out = jnp.stack([x1 * cos - x2 * sin, x1 * sin + x2 * cos], axis=-2)
    return out.reshape(x.shape).astype(x.dtype)


def gqa_branch(q, k, v, z, qn_g, kn_g):
    B, S, _ = q.shape
    qh = rmsnorm(q.reshape(B, S, GQ_HEADS, GQ_HEAD_DIM), qn_g)
    kh = rmsnorm(k.reshape(B, S, GQ_KV_HEADS, GQ_HEAD_DIM), kn_g)
    vh = v.reshape(B, S, GQ_KV_HEADS, GQ_HEAD_DIM)
    cos, sin = axial_rope_tables(S)
    qh, kh = apply_rope(qh, cos, sin), apply_rope(kh, cos, sin)
    grp = GQ_HEADS // GQ_KV_HEADS
    nb = S // GQ_BLOCK
    qb = qh.reshape(B, nb, GQ_BLOCK, GQ_KV_HEADS, grp, GQ_HEAD_DIM).transpose(1, 0, 3, 4, 2, 5)
    qb = qb * (GQ_HEAD_DIM ** -0.5)
    kt = kh.transpose(0, 2, 1, 3)
    vt = vh.transpose(0, 2, 1, 3)

    def block(qblk):
        s = jnp.einsum('bkgqd,bksd->bkgqs', qblk, kt).astype(jnp.float32)
        p = jax.nn.softmax(s, axis=-1).astype(vt.dtype)
        return jnp.einsum('bkgqs,bksd->bkgqd', p, vt)

    out = lax.map(block, qb)
    out = out.transpose(1, 0, 4, 2, 3, 5).reshape(B, S, GQ_WIDTH)
    return out * jax.nn.silu(z)


def conv_branch(bg, cg, xin, z, w, b):
    xc = cg * xin
    xp = jnp.pad(xc, ((0, 0), (1, 1), (0, 0)))
    y = xp[:, :-2] * w[0] + xp[:, 1:-1] * w[1] + xp[:, 2:] * w[2] + b
    return bg * y * jax.nn.silu(z)


def even_mixer(u, w_in, gate_b, w_out, ml_norm_g, na_rpb):
    p = u @ w_in
    ml_q, ml_k, ml_v, ml_o, ml_z, ml_gates, na_q, na_k, na_v, na_z = _split(p, EVEN_SPLIT)
    a = mlstm_branch(ml_q, ml_k, ml_v, ml_o, ml_z, ml_gates, gate_b, ml_norm_g)
    bb = na_branch(na_q, na_k, na_v, na_z, na_rpb)
    return jnp.concatenate([a, bb], axis=-1) @ w_out


def odd_mixer(u, w_in, w_out, q_norm_g, k_norm_g, conv_w, conv_b):
    p = u @ w_in
    g_q, g_k, g_v, g_z, c_b, c_c, c_x, c_z = _split(p, ODD_SPLIT)
    a = gqa_branch(g_q, g_k, g_v, g_z, q_norm_g, k_norm_g)
    d = conv_branch(c_b, c_c, c_x, c_z, conv_w, conv_b)
    return jnp.concatenate([a, d], axis=-1) @ w_out


def setup_inputs(seed: int = 0) -> dict:
    key = jax.random.key(seed)
    ks = jax.random.split(key, 14)
    n_even = (DEPTH + 1) // 2
    n_odd = DEPTH // 2
    f32 = jnp.float32

    def nrm(k, shape, scale):
        return jax.random.normal(k, shape, f32) * scale

    f_base = jnp.linspace(3.0, 6.0, ML_HEADS, dtype=f32)
    z_h = jnp.zeros((ML_HEADS,), f32)
    gate_base = jnp.concatenate([z_h, f_base, z_h, f_base])
    return {
        "x": nrm(ks[0], (BATCH, SEQ, D_MODEL), 1.0),
        "norm_g": 1.0 + nrm(ks[1], (DEPTH, D_MODEL), 0.02),
        "final_g": 1.0 + nrm(ks[2], (D_MODEL,), 0.02),
        "ev_w_in": nrm(ks[3], (n_even, D_MODEL, EVEN_IN), D_MODEL ** -0.5),
        "ev_gate_b": gate_base + nrm(ks[4], (n_even, 4 * ML_HEADS), 0.1),
        "ev_w_out": nrm(ks[5], (n_even, MIX_EVEN, D_MODEL), MIX_EVEN ** -0.5),
        "ev_ml_norm_g": 1.0 + nrm(ks[6], (n_even, ML_WIDTH), 0.02),
        "ev_na_rpb": nrm(ks[7], (n_even, NA_HEADS, 2 * NA_KH - 1, 2 * NA_KW - 1), 0.1),
        "od_w_in": nrm(ks[8], (n_odd, D_MODEL, ODD_IN), D_MODEL ** -0.5),
        "od_w_out": nrm(ks[9], (n_odd, MIX_ODD, D_MODEL), MIX_ODD ** -0.5),
        "od_q_norm_g": 1.0 + nrm(ks[10], (n_odd, GQ_HEAD_DIM), 0.02),
        "od_k_norm_g": 1.0 + nrm(ks[11], (n_odd, GQ_HEAD_DIM), 0.02),
        "od_conv_w": nrm(ks[12], (n_odd, CV_K, CV_WIDTH), CV_K ** -0.5),
        "od_conv_b": nrm(ks[13], (n_odd, CV_WIDTH), 0.02),
    }


def reference(x, norm_g, final_g, ev_w_in, ev_gate_b, ev_w_out, ev_ml_norm_g, ev_na_rpb,
              od_w_in, od_w_out, od_q_norm_g, od_k_norm_g, od_conv_w, od_conv_b):
    h = x
    for layer in range(DEPTH):
        u = rmsnorm(h, norm_g[layer])
        j = layer // 2
        if layer % 2 == 0:
            y = even_mixer(u, ev_w_in[j], ev_gate_b[j], ev_w_out[j], ev_ml_norm_g[j], ev_na_rpb[j])
        else:
            y = odd_mixer(u, od_w_in[j], od_w_out[j], od_q_norm_g[j], od_k_norm_g[j],
                          od_conv_w[j], od_conv_b[j])
        h = h + y.astype(h.dtype)
    return rmsnorm(h, final_g)
```

```python
import numpy as np
import ml_dtypes
from contextlib import ExitStack
import concourse.bass as bass
import concourse.mybir as mybir
from concourse.bass_utils import run_bass_kernel_spmd

F32 = mybir.dt.float32
BF16 = mybir.dt.bfloat16
AF = mybir.ActivationFunctionType
ALU = mybir.AluOpType
AX = mybir.AxisListType

S = 8192
D = 1024
NCORES = 8
GROUPS = [[0, 1, 2, 3], [4, 5, 6, 7]]
EPS = 1e-6


class Tok:
    __slots__ = ("w", "r", "x")

    def __init__(self, x=False):
        self.w = None
        self.r = {}
        self.x = x


def XTok():
    return Tok(True)


class Prog:
    ENG = ("pe", "act", "dve", "pool", "sp")

    def __init__(self, nc):
        self.nc = nc
        self.ins = {e: [] for e in self.ENG}
        self.seen = {e: {} for e in self.ENG}
        self.dma_cnt = {}
        self.stack = ExitStack()
        self._n = 0

    SB_BASE = 16512
    SB_TOP = 229376

    def sb(self, shape, dt, name=None):
        self._n += 1
        nb = int(np.prod(shape[1:])) * (4 if dt == F32 else 2)
        nb = (nb + 31) // 32 * 32
        off = getattr(self, "sbp", self.SB_BASE)
        assert off + nb <= self.SB_TOP, f"SBUF overflow: {off + nb}"
        self.sbp = off + nb
        return self.nc.alloc_sbuf_tensor_at(name or f"sb{self._n}", list(shape), dt, offset=off)

    def mark(self):
        return getattr(self, "sbp", self.SB_BASE)

    def release(self, m):
        self.sbp = m

    def ps(self, shape, dt=F32, name=None):
        self._n += 1
        return self.stack.enter_context(self.nc.psum_tensor(name or f"ps{self._n}", list(shape), dt))

    def _deps(self, eng, reads, writes):
        deps = {}
        def add(ev):
            key = ev[1]
            if key not in deps or deps[key][2] < ev[2]:
                deps[key] = ev
        for t in reads:
            if t.w is not None:
                add(t.w)
        for t in writes:
            if t.w is not None:
                add(t.w)
            for ev in t.r.values():
                add(ev)
        waits = []
        seen = self.seen[eng]
        for key, ev in deps.items():
            if ev[0] == "c" and key == eng and eng == "pe":
                continue
            if seen.get(key, -1) >= ev[2]:
                continue
            seen[key] = ev[2]
            waits.append(ev)
        return waits

    def op(self, eng, fn, reads=(), writes=()):
        xs = [t for t in reads if t.x]
        if xs:
            reads = [t for t in reads if not t.x]
            writes = list(writes) + xs
        waits = self._deps(eng, reads, writes)
        idx = len(self.ins[eng])
        ev = ("c", eng, idx)
        for t in reads:
            t.r[eng] = ev
        for t in writes:
            t.w = ev
            t.r = {}
        self.ins[eng].append([fn, waits, False, None])

    def _alias(self, sem):
        al = self.__dict__.setdefault("sem_alias", {})
        if sem not in al:
            al[sem] = f"d{len(al)}"
        return al[sem]

    def dma(self, eng, out, in_, sem, reads=(), writes=(), **kw):
        if eng == "sp" and type(out.tensor).__name__ == "DRamTensorHandle":
            eng = "pool"
        sem = self._alias(sem)
        waits = self._deps(eng, reads, writes)
        self.dma_cnt[sem] = self.dma_cnt.get(sem, 0) + 16
        ev = ("d", sem, self.dma_cnt[sem])
        for t in reads:
            t.r[sem] = ev
        for t in writes:
            t.w = ev
            t.r = {}
        fn = lambda e: e.dma_start(out=out, in_=in_, **kw)
        self.ins[eng].append([fn, waits, None, sem])

    def dma_fn(self, eng, fn, sem, reads=(), writes=()):
        sem = self._alias(sem)
        waits = self._deps(eng, reads, writes)
        self.dma_cnt[sem] = self.dma_cnt.get(sem, 0) + 16
        ev = ("d", sem, self.dma_cnt[sem])
        for t in reads:
            t.r[sem] = ev
        for t in writes:
            t.w = ev
            t.r = {}
        self.ins[eng].append([fn, waits, None, sem])

    def coll(self, kind, groups, in_ap, out_ap, sem, reads=(), writes=()):
        eng = "pool"
        sem = "coll_" + sem
        self.__dict__.setdefault("coll_sems", set()).add(sem)
        waits = self._deps(eng, reads, writes)
        self.dma_cnt[sem] = self.dma_cnt.get(sem, 0) + 1
        ev = ("d", sem, self.dma_cnt[sem])
        for t in reads:
            t.r[sem] = ev
        for t in writes:
            t.w = ev
            t.r = {}
        fn = lambda e: e.collective_compute(kind, ALU.bypass, replica_groups=groups, ins=[in_ap.opt()], outs=[out_ap.opt()])
        self.ins[eng].append([fn, waits, None, (sem, 1)])

    def barrier(self):
        last = {}
        for e in self.ENG:
            for i in range(len(self.ins[e]) - 1, -1, -1):
                if self.ins[e][i][0] is not None and self.ins[e][i][3] is None:
                    last[e] = i
                    break
        for e in self.ENG:
            waits = []
            seen = self.seen[e]
            for src, idx in last.items():
                if src == e or seen.get(src, -1) >= idx:
                    continue
                seen[src] = idx
                waits.append(("c", src, idx))
            for sem, val in self.dma_cnt.items():
                if sem in self.__dict__.get("coll_sems", ()):
                    continue
                if seen.get(sem, -1) >= val:
                    continue
                seen[sem] = val
                waits.append(("d", sem, val))
            self.ins[e].append([None, waits, False, None])
        self.sem_alias = {}

    def build(self):
        nc = self.nc
        for e in self.ENG:
            for rec in self.ins[e]:
                for ev in rec[1]:
                    if ev[0] == "c":
                        self.ins[ev[1]][ev[2]][2] = True
        semval = {}
        for e in self.ENG:
            c = 0
            for i, rec in enumerate(self.ins[e]):
                if rec[2]:
                    c += 1
                    semval[(e, i)] = c
        names = [e for e in self.ENG if e != "sp"] + sorted(self.dma_cnt)
        sems = {n: self.stack.enter_context(nc.semaphore("s_" + n)) for n in names}
        final_dma = dict(self.dma_cnt)

        def replay(ename, eobj, last=False):
            for i, (fn, waits, sig, dsem) in enumerate(self.ins[ename]):
                for ev in waits:
                    if ev[0] == "c":
                        eobj.wait_ge(sems[ev[1]], semval[(ev[1], ev[2])])
                    else:
                        eobj.wait_ge(sems[ev[1]], ev[2])
                if fn is None:
                    continue
                inst = fn(eobj)
                if isinstance(dsem, tuple):
                    inst.then_inc(sems[dsem[0]], dsem[1])
                elif dsem is not None:
                    inst.then_inc(sems[dsem], 16)
                elif sig:
                    inst.then_inc(sems[ename], 1)
            if last:
                for n, v in final_dma.items():
                    eobj.wait_ge(sems[n], v)

        with nc.Block() as block:
            @block.tensor
            def _(e):
                replay("pe", e)

            @block.scalar
            def _(e):
                replay("act", e)

            @block.vector
            def _(e):
                replay("dve", e)

            @block.gpsimd
            def _(e):
                replay("pool", e)

            @block.sync
            def _(e):
                replay("sp", e, last=True)
        self.stack.close()


def _bf(a):
    return np.ascontiguousarray(a)


class NormCtx:
    def __init__(self, P, hTv, ngs, t_ng, ones_bf, t_ones, epsb, t_eps, ps_ss, t_ps_ss, blk=512):
        self.P = P
        self.blk = blk
        self.hTv = hTv
        self.ngs, self.t_ng = ngs, t_ng
        self.ones_bf, self.t_ones = ones_bf, t_ones
        self.epsb, self.t_eps = epsb, t_eps
        self.ps_ss, self.t_ps_ss = ps_ss, t_ps_ss
        self.xt = [P.sb([128, 8, blk], F32) for _ in range(2)]
        self.t_xt = [Tok(), Tok()]
        self.xsq = P.sb([128, 8, blk], BF16)
        self.t_xsq = Tok()
        self.lnv = P.sb([128, blk], F32)
        self.rstd = P.sb([128, blk], F32)
        self.t_lnv, self.t_rstd = Tok(), Tok()
        self.uT = [P.sb([128, 8, blk], BF16) for _ in range(2)]
        self.t_uT = [Tok(), Tok()]

    def load(self, j):
        k = j % 2
        if self.hTv is None:
            self.loader(j, self.xt[k], f"xt{k}", self.t_xt[k])
        else:
            self.P.dma("sp", self.xt[k][:], self.hTv[:, :, j * self.blk:(j + 1) * self.blk], f"xt{k}", writes=[self.t_xt[k]])

    def norm(self, j):
        P = self.P
        k = j % 2
        xt, xsq, uT = self.xt[k], self.xsq, self.uT[k]
        P.op("act", lambda e: e.activation(xsq[:], xt[:], AF.Square), reads=[self.t_xt[k]], writes=[self.t_xsq])
        for c in range(8):
            P.op("pe", (lambda e, c=c: e.matmul(self.ps_ss[:, 0:self.blk], self.ones_bf[:], xsq[:, c, :], start=(c == 0), stop=(c == 7))),
                 reads=[self.t_ones, self.t_xsq], writes=[self.t_ps_ss])
        P.op("act", lambda e: e.activation(self.lnv[:], self.ps_ss[:, 0:self.blk], AF.Ln, bias=self.epsb[:, 0:1], scale=1.0 / D),
             reads=[self.t_ps_ss, self.t_eps], writes=[self.t_lnv])
        P.op("act", lambda e: e.activation(self.rstd[:], self.lnv[:], AF.Exp, scale=-0.5), reads=[self.t_lnv], writes=[self.t_rstd])
        for c in range(8):
            P.op("dve", (lambda e, c=c: e.scalar_tensor_tensor(uT[:, c, :], xt[:, c, :], self.ngs[:, c:c + 1], self.rstd[:],
                                                              ALU.mult, ALU.mult)),
                 reads=[self.t_xt[k], self.t_ng, self.t_rstd], writes=[self.t_uT[k]])
        return uT, self.t_uT[k]


NWC = 1792


def emit_mixer1(nc, P, PSB, t_PSB, PXg, t_PXg, io, nblk=16, do_v=True, do_g=True, sub=99):
    hgl, t_hg, w, ng, qkg, cs, cwb, ident = (io[k] for k in ("hg", "t_hg", "w", "ng", "qkg", "cs", "cwb", "ident"))
    mp, mg = io["mp"], io["mg"]
    dint = lambda n, s_, d: nc.dram_tensor(n, s_, d, kind="Internal").ap()
    qT_s = dint("qT_s", [2, 128, S], BF16)
    z_s = dint("z_s", [S, 256], F32)
    xc_s = dint("xc_s", [256, S], F32)
    gz_s = dint("gz_s", [256, S], F32)
    t_mpp = [[Tok() for _ in range(6)] for _ in range(8)]

    wb = P.sb([128, 8, NWC], BF16)
    t_wb = Tok()
    wv = w.rearrange("(c p) n -> p c n", p=128)
    for c in range(8):
        P.dma("pool", wb[:, c, :], wv[:, c, :], "wb", writes=[t_wb])
    ngs = P.sb([128, 8], F32); t_ng = Tok()
    P.dma("sp", ngs[:], ng[:, :], "c_ng", writes=[t_ng])
    qg = P.sb([128, 2, 128], F32); t_qg = Tok()
    P.dma("sp", qg[:, 0, :], qkg[0:1, :].to_broadcast([128, 128]), "c_qg", writes=[t_qg])
    P.dma("sp", qg[:, 1, :], qkg[1:2, :].to_broadcast([128, 128]), "c_qg", writes=[t_qg])
    P.op("dve", lambda e: e.tensor_scalar(qg[:, 0, :], qg[:, 0, :], float(128 ** -0.5), None, ALU.mult), reads=[t_qg], writes=[t_qg])
    cw = P.sb([128, 8], F32); t_cw = Tok()
    P.dma("sp", cw[:], cwb[:, :], "c_cw", writes=[t_cw])
    idb = P.sb([128, 128], BF16); t_id = Tok()
    P.dma("sp", idb[:], ident[:, :], "c_id", writes=[t_id])
    ones_bf = P.sb([128, 128], BF16); t_ones = Tok()
    P.op("pool", lambda e: e.memset(ones_bf[:], 1.0), writes=[t_ones])
    epsb = P.sb([128, 2], F32); t_eps = Tok()
    P.op("pool", lambda e: e.memset(epsb[:], EPS), writes=[t_eps])
    kT = P.sb([128, S], BF16)
    t_kT = [Tok() for _ in range(64)]
    vx = P.sb([128, 64, 129], BF16)
    t_vx = [Tok() for _ in range(64)]
    t_vones = Tok()
    P.op("pool", lambda e: e.memset(vx[:, :, 128:129], 1.0), writes=[t_vones])
    pT = [[PSB[1], PSB[2]], [PSB[3], PSB[4]]]
    t_pT = [[t_PSB[1], t_PSB[2]], [t_PSB[3], t_PSB[4]]]
    pB = [PSB[0], PSB[5], PSB[6]]
    t_pB = [t_PSB[0], t_PSB[5], t_PSB[6]]
    pX = PXg
    t_pX = [t_PXg, t_PXg]

    nctx = NormCtx(P, None, ngs, t_ng, ones_bf, t_ones, epsb, t_eps, pB[0], t_pB[0])

    def _loader(j, xt, sem, tok):
        i, r0 = j // 2, (j % 2) * 2
        hv = hgl[i].rearrange("(r c p) t -> p r c t", r=4, p=128)
        for rr in range(2):
            P.dma("sp", xt[:, :, rr * 256:(rr + 1) * 256], hv[:, r0 + rr, :, :], sem, reads=[t_hg[i]], writes=[tok])
    nctx.loader = _loader
    csb = [P.sb([128, 4, 128], F32) for _ in range(2)]; t_cs = [Tok(), Tok()]
    st = [P.sb([128, 12], F32) for _ in range(2)]; t_st = [Tok(), Tok()]
    junk = P.sb([128, 128], F32); t_junk = Tok()
    xn = [P.sb([128, 3, 128], F32) for _ in range(2)]; t_xn = [Tok(), Tok()]
    tmp = [P.sb([128, 3, 64], F32) for _ in range(4)]; t_tmp = [Tok() for _ in range(4)]
    rot = [P.sb([128, 3, 128], BF16) for _ in range(2)]; t_rot = [Tok(), Tok()]
    qTst = [P.sb([128, 2, 512], BF16) for _ in range(2)]; t_qTst = [Tok(), Tok()]
    zst = [P.sb([128, 4, 256], F32) for _ in range(2)]; t_zst = [Tok(), Tok()]
    ccs = P.sb([128, 512], F32); t_ccs = Tok()
    sis = P.sb([128, 512], F32); t_sis = Tok()
    xcst = [P.sb([128, 2, 512], F32) for _ in range(2)]; t_xcst = [Tok(), Tok()]
    gzst = [P.sb([128, 2, 512], F32) for _ in range(2)]; t_gzst = [Tok(), Tok()]
    t_qs = [Tok() for _ in range(16)]
    t_zs = [Tok() for _ in range(16)]
    t_xcs = [Tok() for _ in range(16)]
    t_gzs = [Tok() for _ in range(16)]
    csv = cs.rearrange("(n p) f -> p n f", p=128)
    z_sv = z_s.rearrange("(n p) f -> p n f", p=128)
    qT_sv = qT_s.rearrange("h d t -> d h t")
    xc_sv = xc_s.rearrange("(c p) t -> p c t", p=128)
    gz_sv = gz_s.rearrange("(c p) t -> p c t", p=128)

    def r4(ap):
        return ap.rearrange("p h (a r i) -> p h a r i", a=2, r=2)

    jorder = [2 * p + q for p in io.get("order", list(range(8))) for q in range(2)][:nblk] if nblk == 16 else list(range(nblk))
    nctx.load(jorder[0])
    pend_tr = None
    for jn, j in enumerate(jorder):
        kb = j % 2
        jnext = jorder[jn + 1] if jn + 1 < len(jorder) else None
        if jnext is not None:
            nctx.load(jnext)
        P.dma("sp", csb[kb][:], csv[:, 4 * j:4 * j + 4, :], f"cs{kb}", writes=[t_cs[kb]])
        if jn == 0:
            pend = nctx.norm(j)
        uT, t_uT = pend
        def fm(m, pi):
            for c in range(8):
                P.op("pe", (lambda e, c=c, m=m, pi=pi, uT=uT: e.matmul(pB[pi][:], wb[:, c, 768 + m * 128:768 + (m + 1) * 128], uT[:, c, :],
                                                                 start=(c == 0), stop=(c == 7))),
                     reads=[t_uT, t_wb], writes=[t_pB[pi]])

        def fm_part(part):
            cc = part // 2
            if part % 2 == 0:
                fm(2 + cc, 1)
                P.op("act", lambda e: e.activation(ccs[:], pB[1][:], AF.Copy), reads=[t_pB[1]], writes=[t_ccs])
                fm(4 + cc, 2)
                P.op("dve", (lambda e, cc=cc, kb=kb: e.tensor_tensor(xcst[kb][:, cc, :], pB[2][:], ccs[:], ALU.mult)),
                     reads=[t_pB[2], t_ccs], writes=[t_xcst[kb]])
            else:
                fm(6 + cc, 1)
                P.op("act", lambda e: e.activation(sis[:], pB[1][:], AF.Silu), reads=[t_pB[1]], writes=[t_sis])
                fm(0 + cc, 2)
                P.op("dve", (lambda e, cc=cc, kb=kb: e.tensor_tensor(gzst[kb][:, cc, :], pB[2][:], sis[:], ALU.mult)),
                     reads=[t_pB[2], t_sis], writes=[t_gzst[kb]])
        for tc in range(4):
            ch = 4 * j + tc
            k = ch % 2
            ps = pT[k]
            if tc == 1 and jnext is not None:
                pend = nctx.norm(jnext)
            for (half, c0, c1) in ((0, 0, 512), (1, 512, 768)):
                for c in range(8):
                    P.op("pe", (lambda e, c=c, half=half, c0=c0, c1=c1, ps=ps, tc=tc, uT=uT:
                                e.matmul(ps[half][:, 0:c1 - c0], uT[:, c, tc * 128:(tc + 1) * 128], wb[:, c, c0:c1],
                                         start=(c == 0), stop=(c == 7))),
                         reads=[t_uT, t_wb], writes=[t_pT[k][half]])
            fm_part(tc)
            if pend_tr is not None:
                pend_tr()
            for h in range(3):
                P.op("act", (lambda e, h=h, ps=ps, k=k: e.activation(junk[:], ps[0][:, h * 128:(h + 1) * 128], AF.Square,
                                                                     accum_out=st[k][:, h:h + 1])),
                     reads=[t_pT[k][0]], writes=[t_junk, t_st[k]])
            P.op("act", (lambda e, k=k: e.activation(st[k][:, 4:7], st[k][:, 0:3], AF.Ln, bias=epsb[:, 0:1], scale=1.0 / 128)),
                 reads=[t_st[k], t_eps], writes=[t_st[k]])
            P.op("act", (lambda e, k=k: e.activation(st[k][:, 8:11], st[k][:, 4:7], AF.Exp, scale=-0.5)),
                 reads=[t_st[k]], writes=[t_st[k]])
            for h in range(3):
                P.op("dve", (lambda e, h=h, ps=ps, k=k: e.scalar_tensor_tensor(
                    xn[k][:, h, :], ps[0][:, h * 128:(h + 1) * 128], st[k][:, 8 + h:9 + h], qg[:, 1 if h == 2 else 0, :],
                    ALU.mult, ALU.mult)), reads=[t_pT[k][0], t_st[k], t_qg], writes=[t_xn[k]])
            P.op("act", (lambda e, ps=ps, ch=ch: e.activation(vx[:, ch, 0:128], ps[0][:, 384:512], AF.Copy)),
                 reads=[t_pT[k][0], t_vones], writes=[t_vx[ch]])
            P.op("act", (lambda e, ps=ps, tc=tc, kb=kb: e.activation(zst[kb][:, tc, :], ps[1][:, 0:256], AF.Silu)),
                 reads=[t_pT[k][1]], writes=[t_zst[kb]])
            x4 = r4(xn[k][:])
            o4 = r4(rot[k][:])
            x1, x2 = x4[:, :, :, 0, :], x4[:, :, :, 1, :]
            cosv = csb[kb][:, tc, 0:64].rearrange("p (a i) -> p a i", a=2).unsqueeze(1).to_broadcast([128, 3, 2, 32])
            sinv = csb[kb][:, tc, 64:128].rearrange("p (a i) -> p a i", a=2).unsqueeze(1).to_broadcast([128, 3, 2, 32])
            tv = [t[:].rearrange("p h (a i) -> p h a i", a=2) for t in tmp]
            P.op("dve", (lambda e, x1=x1, cosv=cosv, tv=tv: e.tensor_tensor(tv[0], x1, cosv, ALU.mult)),
                 reads=[t_xn[k], t_cs[kb]], writes=[t_tmp[0]])
            P.op("dve", (lambda e, x2=x2, sinv=sinv, tv=tv: e.tensor_tensor(tv[1], x2, sinv, ALU.mult)),
                 reads=[t_xn[k], t_cs[kb]], writes=[t_tmp[1]])
            P.op("dve", (lambda e, o4=o4, tv=tv: e.tensor_tensor(o4[:, :, :, 0, :], tv[0], tv[1], ALU.subtract)),
                 reads=[t_tmp[0], t_tmp[1]], writes=[t_rot[k]])
            P.op("dve", (lambda e, x1=x1, sinv=sinv, tv=tv: e.tensor_tensor(tv[2], x1, sinv, ALU.mult)),
                 reads=[t_xn[k], t_cs[kb]], writes=[t_tmp[2]])
            P.op("dve", (lambda e, x2=x2, cosv=cosv, tv=tv: e.tensor_tensor(tv[3], x2, cosv, ALU.mult)),
                 reads=[t_xn[k], t_cs[kb]], writes=[t_tmp[3]])
            P.op("dve", (lambda e, o4=o4, tv=tv: e.tensor_tensor(o4[:, :, :, 1, :], tv[2], tv[3], ALU.add)),
                 reads=[t_tmp[2], t_tmp[3]], writes=[t_rot[k]])
            def tr(k=k, ch=ch, kb=kb, tc=tc, j=j):
                for h in range(3):
                    P.op("pe", (lambda e, h=h, k=k: e.transpose(pX[:, k * 512 + h * 128:k * 512 + (h + 1) * 128], rot[k][:, h, :], idb[:])),
                         reads=[t_rot[k], t_id], writes=[t_pX[k]])
                P.op("act", (lambda e, k=k, ch=ch: e.activation(kT[:, ch * 128:(ch + 1) * 128], pX[:, k * 512 + 256:k * 512 + 384], AF.Copy)),
                     reads=[t_pX[k]], writes=[t_kT[ch]])
                P.op("dve", (lambda e, k=k, kb=kb, tc=tc: e.tensor_copy(
                    qTst[kb][:, :, tc * 128:(tc + 1) * 128], pX[:, k * 512:k * 512 + 256].rearrange("p (h t) -> p h t", h=2))),
                     reads=[t_pX[k]], writes=[t_qTst[kb]])
                if tc == 3:
                    P.dma("sp", qT_sv[:, :, j * 512:(j + 1) * 512], qTst[kb][:], f"qo{kb}", reads=[t_qTst[kb]], writes=[t_qs[j]])
            pend_tr = tr
        P.dma("sp", z_sv[:, 4 * j:4 * j + 4, :], zst[kb][:], f"zo{kb}", reads=[t_zst[kb]], writes=[t_zs[j]])
        P.dma("sp", xc_sv[:, :, j * 512:(j + 1) * 512], xcst[kb][:], f"xo{kb}", reads=[t_xcst[kb]], writes=[t_xcs[j]])
        P.dma("sp", gz_sv[:, :, j * 512:(j + 1) * 512], gzst[kb][:], f"go{kb}", reads=[t_gzst[kb]], writes=[t_gzs[j]])
    pend_tr()

    xcv = [P.sb([128, 2, 514], F32) for _ in range(2)]; t_xcv = [Tok(), Tok()]
    gzv = [P.sb([128, 2, 512], F32) for _ in range(2)]; t_gzv = [Tok(), Tok()]
    cv1 = P.sb([128, 512], F32); t_cv1 = Tok()
    cv2 = P.sb([128, 512], F32); t_cv2 = Tok()
    cvo = [P.sb([128, 2, 512], BF16) for _ in range(2)]; t_cvo = [Tok(), Tok()]
    for j in range(16 if do_v else 0):
        kb = j % 2
        lo = max(j * 512 - 1, 0)
        hi = min(j * 512 + 513, S)
        o0 = lo - (j * 512 - 1)
        rd = [t_xcs[j]] + ([t_xcs[j - 1]] if j > 0 else []) + ([t_xcs[j + 1]] if j < 15 else [])
        if j == 0:
            P.op("pool", lambda e: e.memset(xcv[0][:, :, 0:1], 0.0), writes=[t_xcv[0]])
        if j == 15:
            P.op("pool", lambda e: e.memset(xcv[1][:, :, 513:514], 0.0), writes=[t_xcv[1]])
        P.dma("pool", xcv[kb][:, :, o0:o0 + hi - lo], xc_sv[:, :, lo:hi], f"xv{kb}", reads=rd, writes=[t_xcv[kb]])
        P.dma("pool", gzv[kb][:], gz_sv[:, :, j * 512:(j + 1) * 512], f"gv{kb}", reads=[t_gzs[j]], writes=[t_gzv[kb]])
        for cc in range(2):
            w0, w1, w2, bb = (cw[:, cc * 4 + i:cc * 4 + i + 1] for i in range(4))
            P.op("dve", (lambda e, kb=kb, cc=cc, w0=w0, bb=bb: e.tensor_scalar(cv1[:], xcv[kb][:, cc, 0:512], w0, bb, ALU.mult, ALU.add)),
                 reads=[t_xcv[kb], t_cw], writes=[t_cv1])
            P.op("dve", (lambda e, kb=kb, cc=cc, w1=w1: e.scalar_tensor_tensor(cv2[:], xcv[kb][:, cc, 1:513], w1, cv1[:], ALU.mult, ALU.add)),
                 reads=[t_xcv[kb], t_cw, t_cv1], writes=[t_cv2])
            P.op("dve", (lambda e, kb=kb, cc=cc, w2=w2: e.scalar_tensor_tensor(cv1[:], xcv[kb][:, cc, 2:514], w2, cv2[:], ALU.mult, ALU.add)),
                 reads=[t_xcv[kb], t_cw, t_cv2], writes=[t_cv1])
            P.op("dve", (lambda e, kb=kb, cc=cc: e.tensor_tensor(cvo[kb][:, cc, :], cv1[:], gzv[kb][:, cc, :], ALU.mult)),
                 reads=[t_cv1, t_gzv[kb]], writes=[t_cvo[kb]])
        P.dma("sp", mp[j // 2].rearrange("(a p) t -> p a t", p=128)[:, 2:4, (j % 2) * 512:(j % 2 + 1) * 512], cvo[kb][:], f"co{kb}",
              reads=[t_cvo[kb]], writes=[t_mpp[j // 2][4 + j % 2]])

    qTb = [P.sb([128, 512], BF16) for _ in range(2)]; t_qTb = [Tok(), Tok()]
    szb = [P.sb([128, 4, 128], F32) for _ in range(2)]; t_szb = [Tok(), Tok()]
    pt = [P.sb([128, 512], BF16) for _ in range(3)]; t_pt = [Tok() for _ in range(3)]
    rden = P.sb([128, 4], F32); t_rden = Tok()
    aout = [P.sb([128, 4, 128], BF16) for _ in range(2)]; t_aout = [Tok(), Tok()]
    pS = [pT[0][0][:], pT[0][1][:]]
    t_pS = t_pT[0]
    pO = [pT[1][0][:], pT[1][1][:], pB[1][:], pB[2][:]]
    t_pO = [t_pT[1][0], t_pT[1][1], t_pB[1], t_pB[2]]
    aoT = [P.sb([128, 4, 128], BF16) for _ in range(2)]; t_aoT = [Tok(), Tok()]
    it = 0
    for hd in range(2 if do_g else 0):
        for qb in range(min(16, sub)):
            kq = it % 2
            it += 1
            P.dma("sp", qTb[kq][:], qT_s[hd][:, qb * 512:(qb + 1) * 512], f"ql{kq}", reads=[t_qs[qb]], writes=[t_qTb[kq]])
            P.dma("sp", szb[kq][:], z_sv[:, 4 * qb:4 * qb + 4, hd * 128:(hd + 1) * 128], f"zl{kq}", reads=[t_zs[qb]], writes=[t_szb[kq]])

            def smm(kc):
                P.op("pe", (lambda e, kc=kc, kq=kq: e.matmul(pS[kc % 2], kT[:, kc * 128:(kc + 1) * 128], qTb[kq][:], start=True, stop=True)),
                     reads=[t_kT[kc], t_qTb[kq]], writes=[t_pS[kc % 2]])
            smm(0)
            for kc in range(64):
                if kc + 1 < 64:
                    smm(kc + 1)
                pk = kc % 3
                P.op("act", (lambda e, kc=kc, pk=pk: e.activation(pt[pk][:], pS[kc % 2], AF.Exp)),
                     reads=[t_pS[kc % 2]], writes=[t_pt[pk]])
                for qs in range(4):
                    P.op("pe", (lambda e, kc=kc, pk=pk, qs=qs: e.matmul(pO[qs][:, 0:129], pt[pk][:, qs * 128:(qs + 1) * 128], vx[:, kc, :],
                                                                       start=(kc == 0), stop=(kc == 63))),
                         reads=[t_pt[pk], t_vx[kc], t_vones], writes=[t_pO[qs]])
            for qs in range(4):
                P.op("dve", (lambda e, qs=qs: e.reciprocal(rden[:, qs:qs + 1], pO[qs][:, 128:129])), reads=[t_pO[qs]], writes=[t_rden])
                P.op("dve", (lambda e, qs=qs, kq=kq: e.scalar_tensor_tensor(aout[kq][:, qs, :], pO[qs][:, 0:128], rden[:, qs:qs + 1],
                                                                          szb[kq][:, qs, :], ALU.mult, ALU.mult)),
                     reads=[t_pO[qs], t_rden, t_szb[kq]], writes=[t_aout[kq]])
            for qs in range(4):
                P.op("pe", (lambda e, qs=qs, kq=kq: e.transpose(pX[:, qs * 128:(qs + 1) * 128], aout[kq][:, qs, :], idb[:])),
                     reads=[t_aout[kq], t_id], writes=[t_pX[0]])
            P.op("act", (lambda e, kq=kq: e.activation(aoT[kq][:], pX[:, 0:512].rearrange("p (a t) -> p a t", a=4), AF.Copy)),
                 reads=[t_pX[0]], writes=[t_aoT[kq]])
            P.dma("sp", mp[qb // 2][hd * 128:(hd + 1) * 128, (qb % 2) * 512:(qb % 2 + 1) * 512], aoT[kq][:].rearrange("p a t -> p (a t)"), f"ao{kq}",
                  reads=[t_aoT[kq]], writes=[t_mpp[qb // 2][hd * 2 + qb % 2]])
            if hd == 1 and qb % 2 == 1:
                P.coll("AllGather", GROUPS, mp[qb // 2], mg[qb // 2], f"cc{(qb // 2) % 4}", reads=t_mpp[qb // 2], writes=[io["t_mg"][qb // 2]])


def rope_tables():
    t = np.arange(S)
    pos = np.stack([t // 64, t % 64], axis=-1).astype(np.float32)
    inv = (np.float32(10000.0) ** (-np.arange(32, dtype=np.float32) / np.float32(32))).astype(np.float32)
    ang = (pos[:, :, None] * inv).astype(np.float32)
    return np.concatenate([np.cos(ang).reshape(S, 64), np.sin(ang).reshape(S, 64)], axis=1).astype(np.float32)


NWA = 2308
NA_CFG_TILES = (0, 1, 2, 62, 63)


def na_tile_cfg(i):
    if i < 2:
        return i, 0, 4
    if i >= 62:
        return 3 + (i - 62), 120, 4
    return 2, 2 * i - 4, 5


def emit_mixer0(nc, P, PSB, t_PSB, PXg, t_PXg, io, do_m=True, do_n=True):
    BLK = 256
    NB = S // BLK
    hT, w, ng, gateb, mlg, tri, nab, nam, ident = (io[k] for k in ("hT", "w", "ng", "gateb", "mlg", "tri", "nab", "nam", "ident"))
    mp, mg = io["mp"], io["mg"]
    dint = lambda n, s, d: nc.dram_tensor(n, s, d, kind="Internal").ap()
    fm_s = dint("fm_s", [64, 128, 768], BF16)
    tm_s = dint("tm_s", [64, 128, 769], BF16)
    g_s = dint("g_s", [64, 128, 768], F32)
    na_s = dint("na_s", [4, 128, S], BF16)
    vna_s = dint("vna_s", [64, 128, 260], BF16)
    h_s = dint("h_s", [2, 64, 128, 256], F32)
    t_mpp = [[Tok() for _ in range(16)] for _ in range(8)]

    wb = P.sb([128, 8, NWA], BF16); t_wb = Tok()
    wv = w.rearrange("(c p) n -> p c n", p=128)
    for c in range(8):
        for (a, b_) in ((0, 1024), (1024, 2048), (2048, NWA)):
            P.dma("pool", wb[:, c, a:b_], wv[:, c, a:b_], "wb", writes=[t_wb])
    ngs = P.sb([128, 8], F32); t_ng = Tok()
    P.dma("sp", ngs[:], ng[:, :], "c_ng", writes=[t_ng])
    gbb = P.sb([128, 4], F32); t_gbb = Tok()
    P.dma("sp", gbb[:], gateb[0:1, :].to_broadcast([128, 4]), "c_gb", writes=[t_gbb])
    mlgb = P.sb([128, 256], F32); t_mlg = Tok()
    P.dma("sp", mlgb[:], mlg[0:1, :].to_broadcast([128, 256]), "c_mlg", writes=[t_mlg])
    trs = P.sb([128, 2, 128], F32); t_tri = Tok()
    P.dma("sp", trs[:], tri.rearrange("a s t -> s a t"), "c_tri", writes=[t_tri])
    idb = P.sb([128, 128], BF16); t_id = Tok()
    P.dma("sp", idb[:], ident[:, :], "c_id", writes=[t_id])
    ones_bf = P.sb([128, 128], BF16); t_ones = Tok()
    P.op("pool", lambda e: e.memset(ones_bf[:], 1.0), writes=[t_ones])
    ones32 = P.sb([128, 128], F32); t_ones32 = Tok()
    P.op("pool", lambda e: e.memset(ones32[:], 1.0), writes=[t_ones32])
    epsb = P.sb([128, 2], F32); t_eps = Tok()
    P.op("pool", lambda e: e.memset(epsb[:, 0:1], EPS), writes=[t_eps])
    P.op("pool", lambda e: e.memset(epsb[:, 1:2], 1.0), writes=[t_eps])
    EA = P.sb([128, 64, 2], F32); t_EA = [Tok() for _ in range(64)]
    EG = P.sb([128, 64, 2], F32); t_EG = [Tok() for _ in range(64)]
    pb, t_pb, pX, t_pX = PSB, t_PSB, PXg, t_PXg

    nctx = NormCtx(P, hT.rearrange("(c p) t -> p c t", p=128), ngs, t_ng, ones_bf, t_ones, epsb, t_eps, pb[0], t_pb[0], blk=BLK)
    g4 = [P.sb([128, 4], F32) for _ in range(2)]; t_g4 = [Tok(), Tok()]
    sm = [P.sb([128, 16], F32) for _ in range(2)]; t_sm = [Tok(), Tok()]
    X5 = [P.sb([128, 5, 256], BF16) for _ in range(2)]; t_X5 = [Tok(), Tok()]
    FMst = [P.sb([128, 768], BF16) for _ in range(2)]; t_FMst = [Tok(), Tok()]
    vml = [P.sb([128, 257], BF16) for _ in range(2)]; t_vml = [Tok(), Tok()]
    GS = [P.sb([128, 3, 256], F32) for _ in range(2)]; t_GS = [Tok(), Tok()]
    Y4 = [P.sb([128, 512], BF16) for _ in range(2)]; t_Y4 = [Tok(), Tok()]
    NAst = [P.sb([128, 4, 128], BF16) for _ in range(2)]; t_NAst = [Tok(), Tok()]
    vst = [P.sb([128, 4, 65], BF16) for _ in range(2)]; t_vst = [Tok(), Tok()]
    t_fm = [Tok() for _ in range(64)]
    t_tm = [Tok() for _ in range(64)]
    t_gs = [Tok() for _ in range(64)]
    t_nas = [Tok() for _ in range(64)]
    t_vna = [Tok() for _ in range(64)]
    for k in range(2):
        P.op("pool", (lambda e, k=k: e.memset(vml[k][:, 256:257], 1.0)), writes=[t_vml[k]])
        P.op("pool", (lambda e, k=k: e.memset(vst[k][:, :, 64:65], 1.0)), writes=[t_vst[k]])
    na_sv = na_s.rearrange("a p t -> p a t")

    def grp(uT, t_uT, tc, c0, c1, bank):
        for c in range(8):
            P.op("pe", (lambda e, c=c: e.matmul(pb[bank][:, 0:c1 - c0], uT[:, c, tc * 128:(tc + 1) * 128], wb[:, c, c0:c1],
                                                start=(c == 0), stop=(c == 7))),
                 reads=[t_uT, t_wb], writes=[t_pb[bank]])

    nctx.load(0)
    pend_nat = None
    for j in range(NB):
        if j + 1 < NB:
            nctx.load(j + 1)
        if j == 0:
            pend = nctx.norm(0)
        uT, t_uT = pend
        for tc in range(BLK // 128):
            ch = j * (BLK // 128) + tc
            k = ch % 2
            if tc == 1 and j + 1 < NB:
                pend = nctx.norm(j + 1)
            if pend_nat is not None:
                pend_nat()
            grp(uT, t_uT, tc, 2048, 2308, 5)
            P.op("dve", (lambda e, k=k: e.tensor_tensor(g4[k][:], pb[5][:, 256:260], gbb[:], ALU.add)),
                 reads=[t_pb[5], t_gbb], writes=[t_g4[k]])
            P.op("act", (lambda e, k=k: e.activation(vst[k][:, :, 0:64], pb[5][:, 0:256].rearrange("p (h d) -> p h d", h=4), AF.Copy)),
                 reads=[t_pb[5]], writes=[t_vst[k]])
            P.dma("sp", vna_s[ch], vst[k][:].rearrange("p h d -> p (h d)"), f"vn{k}", reads=[t_vst[k]], writes=[t_vna[ch]])
            P.op("act", (lambda e, k=k: e.activation(sm[k][:, 0:2], g4[k][:, 1:4:2], AF.Exp, scale=-1.0)),
                 reads=[t_g4[k]], writes=[t_sm[k]])
            P.op("act", (lambda e, k=k: e.activation(sm[k][:, 2:4], sm[k][:, 0:2], AF.Ln, bias=epsb[:, 1:2])),
                 reads=[t_sm[k], t_eps], writes=[t_sm[k]])
            grp(uT, t_uT, tc, 0, 512, 1)
            grp(uT, t_uT, tc, 512, 1024, 2)
            P.op("act", (lambda e, k=k: e.activation(vml[k][:, 0:256], pb[2][:, 0:256], AF.Copy)), reads=[t_pb[2]], writes=[t_vml[k]])
            P.dma("sp", tm_s[ch][:, 512:769], vml[k][:], f"vo{k}", reads=[t_vml[k]], writes=[t_tm[ch]])
            P.op("pe", (lambda e, k=k: e.matmul(pb[6][:, 0:1], trs[:, 0, :], sm[k][:, 2:3], start=True, stop=True)),
                 reads=[t_tri, t_sm[k]], writes=[t_pb[6]])
            P.op("pe", (lambda e, k=k: e.matmul(pb[6][:, 1:2], trs[:, 1, :], sm[k][:, 3:4], start=True, stop=True)),
                 reads=[t_tri, t_sm[k]], writes=[t_pb[6]])
            P.op("pe", (lambda e, k=k: e.matmul(pb[6][:, 2:4], ones32[:], sm[k][:, 2:4], start=True, stop=True)),
                 reads=[t_ones32, t_sm[k]], writes=[t_pb[6]])
            P.op("act", (lambda e, k=k: e.activation(sm[k][:, 4:6], pb[6][:, 0:2], AF.Exp, scale=-1.0)),
                 reads=[t_pb[6]], writes=[t_sm[k]])
            P.op("act", (lambda e, ch=ch: e.activation(EG[:, ch, :], pb[6][:, 2:4], AF.Exp, scale=-1.0)),
                 reads=[t_pb[6]], writes=[t_EG[ch]])
            P.op("dve", (lambda e, k=k: e.tensor_tensor(sm[k][:, 6:8], pb[6][:, 0:2], g4[k][:, 0:4:2], ALU.add)),
                 reads=[t_pb[6], t_g4[k]], writes=[t_sm[k]])
            P.op("act", (lambda e, k=k: e.activation(sm[k][:, 8:10], sm[k][:, 6:8], AF.Exp)), reads=[t_sm[k]], writes=[t_sm[k]])
            P.op("dve", (lambda e, k=k, ch=ch: e.tensor_scalar(EA[:, ch, :], sm[k][:, 8:10], 1.0 / 16.0, None, ALU.mult)),
                 reads=[t_sm[k]], writes=[t_EA[ch]])
            grp(uT, t_uT, tc, 1024, 1536, 3)
            P.op("dve", (lambda e, k=k: e.tensor_scalar(X5[k][:, 0, :], pb[1][:, 0:256], sm[k][:, 4:5], None, ALU.mult)),
                 reads=[t_pb[1], t_sm[k]], writes=[t_X5[k]])
            P.op("dve", (lambda e, k=k: e.tensor_scalar(X5[k][:, 1, :], pb[1][:, 0:256], sm[k][:, 5:6], None, ALU.mult)),
                 reads=[t_pb[1], t_sm[k]], writes=[t_X5[k]])
            P.op("act", (lambda e, k=k: e.activation(X5[k][:, 2, :], pb[1][:, 256:512], AF.Copy)), reads=[t_pb[1]], writes=[t_X5[k]])
            P.op("dve", (lambda e, k=k, ch=ch: e.tensor_scalar(X5[k][:, 3, :], pb[1][:, 256:512], EA[:, ch, 0:1], None, ALU.mult)),
                 reads=[t_pb[1], t_EA[ch]], writes=[t_X5[k]])
            P.op("dve", (lambda e, k=k, ch=ch: e.tensor_scalar(X5[k][:, 4, :], pb[1][:, 256:512], EA[:, ch, 1:2], None, ALU.mult)),
                 reads=[t_pb[1], t_EA[ch]], writes=[t_X5[k]])
            P.dma("sp", tm_s[ch][:, 0:512], X5[k][:, 3:5, :].rearrange("p a d -> p (a d)"), f"to{k}", reads=[t_X5[k]], writes=[t_tm[ch]])
            grp(uT, t_uT, tc, 1536, 2048, 4)
            P.op("act", (lambda e, k=k: e.activation(GS[k][:, 0, :], pb[2][:, 256:512], AF.Tanh, scale=0.5)), reads=[t_pb[2]], writes=[t_GS[k]])
            P.op("dve", (lambda e, k=k: e.tensor_scalar(GS[k][:, 0, :], GS[k][:, 0, :], 0.5, 0.5, ALU.mult, ALU.add)), reads=[], writes=[t_GS[k]])
            P.op("act", (lambda e, k=k: e.activation(GS[k][:, 1:3, :], pb[3][:].rearrange("p (a d) -> p a d", a=2), AF.Silu)),
                 reads=[t_pb[3]], writes=[t_GS[k]])
            P.dma("sp", g_s[ch], GS[k][:].rearrange("p a d -> p (a d)"), f"go{k}", reads=[t_GS[k]], writes=[t_gs[ch]])
            P.op("act", (lambda e, k=k: e.activation(Y4[k][:], pb[4][:], AF.Copy)), reads=[t_pb[4]], writes=[t_Y4[k]])
            for s_ in range(3):
                for hf in range(2):
                    P.op("pe", (lambda e, k=k, s_=s_, hf=hf: e.transpose(pX[:, (s_ * 2 + hf) * 128:(s_ * 2 + hf + 1) * 128],
                                                                        X5[k][:, s_, hf * 128:(hf + 1) * 128], idb[:])),
                         reads=[t_X5[k], t_id], writes=[t_pX])
            P.op("act", (lambda e, k=k: e.activation(FMst[k][:], pX[:, 0:768], AF.Copy)), reads=[t_pX], writes=[t_FMst[k]])
            P.dma("sp", fm_s[ch], FMst[k][:], f"fo{k}", reads=[t_FMst[k]], writes=[t_fm[ch]])

            def nat(k=k, ch=ch):
                for a in range(4):
                    P.op("pe", (lambda e, k=k, a=a: e.transpose(pX[:, a * 128:(a + 1) * 128], Y4[k][:, a * 128:(a + 1) * 128], idb[:])),
                         reads=[t_Y4[k], t_id], writes=[t_pX])
                P.op("dve", (lambda e, k=k: e.tensor_copy(NAst[k][:], pX[:, 0:512].rearrange("p (a t) -> p a t", a=4))),
                     reads=[t_pX], writes=[t_NAst[k]])
                P.dma("sp", na_sv[:, :, ch * 128:(ch + 1) * 128], NAst[k][:], f"no{k}", reads=[t_NAst[k]], writes=[t_nas[ch]])
            pend_nat = nat
    pend_nat()

    if True:
        fmB = [[P.sb([128, 768], BF16) for _ in range(2)] for _ in range(2)]
        tmB = [[P.sb([128, 769], BF16) for _ in range(2)] for _ in range(2)]
        t_fmB = [[Tok(), Tok()], [Tok(), Tok()]]
        t_tmB = [[Tok(), Tok()], [Tok(), Tok()]]
        wT = [[P.sb([128, 128], BF16) for _ in range(2)] for _ in range(2)]; t_wT = [[Tok(), Tok()], [Tok(), Tok()]]
        Cf = [P.sb([128, 2, 257], F32) for _ in range(2)]; t_Cf = [Tok(), Tok()]
        Cb = [P.sb([128, 2, 257], BF16) for _ in range(2)]; t_Cb = [Tok(), Tok()]
        ctmp = [[P.sb([128, 2, 257], F32) for _ in range(2)] for _ in range(2)]; t_ctmp = [[Tok(), Tok()], [Tok(), Tok()]]
        dn = [P.sb([128, 4], F32) for _ in range(2)]; t_dn = [Tok(), Tok()]
        hst = [[P.sb([128, 256], F32) for _ in range(2)] for _ in range(2)]
        t_hst = [[Tok(), Tok()], [Tok(), Tok()]]
        t_hs = [[Tok() for _ in range(64)] for _ in range(2)]
        for d_ in range(2):
            P.op("pool", (lambda e, d_=d_: e.memset(Cf[d_][:], 0.0)), writes=[t_Cf[d_]])
        def m_front(c):
            kk = c % 2
            chs = (c, 63 - c)
            for d_ in range(2):
                ch = chs[d_]
                P.dma("sp", fmB[d_][kk][:], fm_s[ch], f"fl{d_}{kk}", reads=[t_fm[ch]], writes=[t_fmB[d_][kk]])
                P.dma("sp", tmB[d_][kk][:], tm_s[ch], f"tl{d_}{kk}", reads=[t_tm[ch]], writes=[t_tmB[d_][kk]])
            for d_ in range(2):
                ch = chs[d_]
                fmv = fmB[d_][kk][:].rearrange("p (a h t) -> p a h t", a=3, h=2)
                for hf in range(2):
                    P.op("pe", (lambda e, d_=d_, hf=hf, fmv=fmv: e.matmul(pb[0][:, d_ * 128:(d_ + 1) * 128], fmv[:, 2, hf, :], fmv[:, d_, hf, :],
                                                                         start=(hf == 0), stop=(hf == 1))),
                         reads=[t_fmB[d_][kk]], writes=[t_pb[0]])
                P.op("dve", (lambda e, d_=d_, ch=ch, kk=kk: e.scalar_tensor_tensor(wT[d_][kk][:], pb[0][:, d_ * 128:(d_ + 1) * 128], EA[:, ch, d_:d_ + 1], trs[:, d_, :],
                                                                           ALU.mult, ALU.mult)),
                     reads=[t_pb[0], t_EA[ch], t_tri], writes=[t_wT[d_][kk]])
            for d_ in range(2):
                ch = chs[d_]
                tmv = tmB[d_][kk]
                for hf in range(2):
                    P.op("pe", (lambda e, d_=d_, hf=hf, tmv=tmv: e.matmul(pb[3 + hf][:, 0:257], tmv[:, d_ * 256 + hf * 128:d_ * 256 + (hf + 1) * 128],
                                                                         tmv[:, 512:769], start=True, stop=True)),
                         reads=[t_tmB[d_][kk]], writes=[t_pb[3 + hf]])
                    P.op("act", (lambda e, d_=d_, hf=hf, ch=ch, kk=kk: e.activation(ctmp[d_][kk][:, hf, :], pb[3 + hf][:, 0:257], AF.Copy, scale=EG[:, ch, d_:d_ + 1])),
                         reads=[t_pb[3 + hf], t_EG[ch]], writes=[t_ctmp[d_][kk]])

        def m_back(c):
            kk = c % 2
            chs = (c, 63 - c)
            for d_ in range(2):
                ch = chs[d_]
                fmv = fmB[d_][kk][:].rearrange("p (a h t) -> p a h t", a=3, h=2)
                tmv = tmB[d_][kk]
                if c > 0:
                    for hf in range(2):
                        P.op("pe", (lambda e, d_=d_, hf=hf, fmv=fmv: e.matmul(pb[1 + d_][:, 0:257], fmv[:, d_, hf, :], Cb[d_][:, hf, :],
                                                                             start=(hf == 0), stop=False)),
                             reads=[t_fmB[d_][kk], t_Cb[d_]], writes=[t_pb[1 + d_]])
                P.op("pe", (lambda e, d_=d_, tmv=tmv, c=c, kk=kk: e.matmul(pb[1 + d_][:, 0:257], wT[d_][kk][:], tmv[:, 512:769], start=(c == 0), stop=True)),
                     reads=[t_wT[d_][kk], t_tmB[d_][kk]], writes=[t_pb[1 + d_]])
                P.op("act", (lambda e, d_=d_: e.activation(dn[d_][:, 0:1], pb[1 + d_][:, 256:257], AF.Abs)), reads=[t_pb[1 + d_]], writes=[t_dn[d_]])
                P.op("dve", (lambda e, d_=d_: e.tensor_scalar(dn[d_][:, 1:2], dn[d_][:, 0:1], 1.0, None, ALU.max)), reads=[t_dn[d_]], writes=[t_dn[d_]])
                P.op("dve", (lambda e, d_=d_: e.reciprocal(dn[d_][:, 2:3], dn[d_][:, 1:2])), reads=[t_dn[d_]], writes=[t_dn[d_]])
                P.op("dve", (lambda e, d_=d_, kk=kk: e.tensor_scalar(hst[d_][kk][:], pb[1 + d_][:, 0:256], dn[d_][:, 2:3], None, ALU.mult)),
                     reads=[t_pb[1 + d_], t_dn[d_]], writes=[t_hst[d_][kk]])
                P.dma("sp", h_s[d_][ch], hst[d_][kk][:], f"ho{d_}{kk}", reads=[t_hst[d_][kk]], writes=[t_hs[d_][ch]])
                P.op("dve", (lambda e, d_=d_, ch=ch, kk=kk: e.scalar_tensor_tensor(Cf[d_][:], Cf[d_][:], EG[:, ch, d_:d_ + 1], ctmp[d_][kk][:], ALU.mult, ALU.add)),
                     reads=[t_EG[ch], t_ctmp[d_][kk]], writes=[t_Cf[d_]])
                P.op("dve", (lambda e, d_=d_: e.tensor_copy(Cb[d_][:], Cf[d_][:])), reads=[t_Cf[d_]], writes=[t_Cb[d_]])

        hfb = P.sb([128, 2, 4, 256], F32); t_hfb = Tok()
        gfb = P.sb([128, 4, 512], F32); t_gfb = Tok()
        hsum = P.sb([128, 4, 256], F32); t_hsum = Tok()
        junk = P.sb([128, 256], F32); t_junk = Tok()
        fst = P.sb([128, 12], F32); t_fst = Tok()
        mo = P.sb([128, 4, 256], BF16); t_mo = Tok()
        moT = P.sb([128, 2, 4, 128], BF16); t_moT = Tok()

        def m_final4(g):
            c0 = 4 * g
            for d_ in range(2):
                P.dma("sp", hfb[:, d_, :, :], h_s[d_][c0:c0 + 4].rearrange("j p f -> p j f"), f"hl{d_}",
                      reads=[t_hs[d_][c0 + j] for j in range(4)], writes=[t_hfb])
            P.dma("sp", gfb[:], g_s[c0:c0 + 4].rearrange("j p f -> p j f")[:, :, 0:512], "gl0", reads=[t_gs[c0 + j] for j in range(4)], writes=[t_gfb])
            P.op("dve", (lambda e: e.tensor_tensor(hsum[:], hfb[:, 0, :, :], hfb[:, 1, :, :], ALU.add)), reads=[t_hfb], writes=[t_hsum])
            for j in range(4):
                P.op("act", (lambda e, j=j: e.activation(junk[:], hsum[:, j, :], AF.Square, accum_out=fst[:, j:j + 1])),
                     reads=[t_hsum], writes=[t_junk, t_fst])
            P.op("act", (lambda e: e.activation(fst[:, 4:8], fst[:, 0:4], AF.Ln, bias=epsb[:, 0:1], scale=1.0 / 256)), reads=[t_fst, t_eps], writes=[t_fst])
            P.op("act", (lambda e: e.activation(fst[:, 8:12], fst[:, 4:8], AF.Exp, scale=-0.5)), reads=[t_fst], writes=[t_fst])
            for j in range(4):
                P.op("dve", (lambda e, j=j: e.scalar_tensor_tensor(hsum[:, j, :], hsum[:, j, :], fst[:, 8 + j:9 + j], mlgb[:], ALU.mult, ALU.mult)),
                     reads=[t_fst, t_mlg], writes=[t_hsum])
            P.op("dve", (lambda e: e.tensor_tensor(hsum[:], hsum[:], gfb[:, :, 0:256], ALU.mult)), reads=[t_gfb], writes=[t_hsum])
            P.op("dve", (lambda e: e.tensor_tensor(mo[:], hsum[:], gfb[:, :, 256:512], ALU.mult)), reads=[t_gfb, t_hsum], writes=[t_mo])
            for hf in range(2):
                for j in range(4):
                    P.op("pe", (lambda e, hf=hf, j=j: e.transpose(pX[:, (hf * 4 + j) * 128:(hf * 4 + j + 1) * 128], mo[:, j, hf * 128:(hf + 1) * 128], idb[:])),
                         reads=[t_mo, t_id], writes=[t_pX])
            P.op("act", (lambda e: e.activation(moT[:].rearrange("p a j t -> p (a j t)"), pX[:, 0:1024], AF.Copy)), reads=[t_pX], writes=[t_moT])
            p_, q_ = c0 // 8, (c0 % 8) * 128
            P.dma("sp", mp[p_].rearrange("(a p) t -> p a t", p=128)[:, 0:2, q_:q_ + 512], moT[:].rearrange("p a j t -> p a (j t)"), "mo0",
                  reads=[t_moT], writes=[t_mpp[p_][(c0 % 8) + j] for j in range(4)])

    if True:
        Et = P.sb([128, 5, 20, 128], BF16); t_E = [Tok() for _ in range(5)]
        btmp = P.sb([128, 20, 128], F32); t_btmp = Tok()
        mtmp = P.sb([128, 5, 128], F32); t_mtmp = Tok()
        for cf in range(5):
            P.dma("sp", btmp[:], nab[cf], "bl", writes=[t_btmp])
            P.dma("sp", mtmp[:], nam[cf], "ml", writes=[t_mtmp])
            P.op("act", (lambda e: e.activation(btmp[:], btmp[:], AF.Exp)), writes=[t_btmp])
            for h in range(4):
                P.op("dve", (lambda e, cf=cf, h=h: e.tensor_tensor(Et[:, cf, h * 5:(h + 1) * 5, :], btmp[:, h * 5:(h + 1) * 5, :], mtmp[:], ALU.mult)),
                     reads=[t_btmp, t_mtmp], writes=[t_E[cf]])
        qn = [P.sb([128, 2, 128], BF16) for _ in range(2)]; t_qn = [Tok(), Tok()]
        kn_ = [P.sb([128, 2, 640], BF16) for _ in range(2)]; t_kn = [Tok(), Tok()]
        vn = [P.sb([128, 5, 260], BF16) for _ in range(2)]; t_vn = [Tok(), Tok()]
        gzn = [P.sb([128, 256], F32) for _ in range(2)]; t_gzn = [Tok(), Tok()]
        pp = [P.sb([128, 5, 128], BF16) for _ in range(3)]; t_pp = [Tok() for _ in range(3)]
        rdn = P.sb([128, 4], F32); t_rdn = Tok()
        no = [P.sb([128, 256], BF16) for _ in range(2)]; t_no = [Tok(), Tok()]
        noT = [P.sb([128, 2, 128], BF16) for _ in range(2)]; t_noT = [Tok(), Tok()]
        vna_v = vna_s.rearrange("n p f -> p n f")
        cnt = [0]

        def n_tile(i):
            k = i % 2
            cf, r0, nb = na_tile_cfg(i)
            c0 = r0 // 2
            nk = 512 if nb == 4 else 576
            P.dma("sp", qn[k][:], na_sv[:, 0:2, i * 128:(i + 1) * 128], f"nq{k}", reads=[t_nas[i]], writes=[t_qn[k]])
            P.dma("sp", kn_[k][:, :, 0:nk], na_sv[:, 2:4, r0 * 64:r0 * 64 + nk], f"nk{k}",
                  reads=[t_nas[cc] for cc in range(c0, c0 + nb)], writes=[t_kn[k]])
            P.dma("sp", vn[k][:, 0:nb, :], vna_v[:, c0:c0 + nb, :], f"nv{k}", reads=[t_vna[cc] for cc in range(c0, c0 + nb)], writes=[t_vn[k]])
            P.dma("sp", gzn[k][:], g_s[i][:, 512:768], f"ng{k}", reads=[t_gs[i]], writes=[t_gzn[k]])
            ob = 6
            sbanks = (5, 7)

            def s_mm(h):
                pr, off = h // 2, (h % 2) * 64
                sb_ = sbanks[h % 2]
                for bl in range(nb):
                    kn = 128 if bl < 4 else 64
                    if bl < 4:
                        dst, tk = pb[sb_][0:kn, bl * 128:(bl + 1) * 128], t_pb[sb_]
                    else:
                        dst, tk = pb[0][0:kn, 256 + 128 * (h % 2):384 + 128 * (h % 2)], t_pb[0]
                    P.op("pe", (lambda e, k=k, pr=pr, off=off, bl=bl, kn=kn, dst=dst: e.matmul(
                        dst, kn_[k][off:off + 64, pr, bl * 128:bl * 128 + kn], qn[k][off:off + 64, pr, :],
                        start=True, stop=True)), reads=[t_kn[k], t_qn[k]], writes=[tk])
            s_mm(0)
            for h in range(4):
                sb_ = sbanks[h % 2]
                pk = cnt[0] % 3
                cnt[0] += 1
                if h + 1 < 4:
                    s_mm(h + 1)
                P.op("act", (lambda e, pk=pk, sb_=sb_: e.activation(pp[pk][:, 0:4, :], pb[sb_][:].rearrange("p (b q) -> p b q", b=4), AF.Exp, scale=0.125)),
                     reads=[t_pb[sb_]], writes=[t_pp[pk]])
                if nb == 5:
                    P.op("act", (lambda e, pk=pk, h=h: e.activation(pp[pk][0:64, 4, :], pb[0][0:64, 256 + 128 * (h % 2):384 + 128 * (h % 2)], AF.Exp, scale=0.125)),
                         reads=[t_pb[0]], writes=[t_pp[pk]])
                P.op("dve", (lambda e, pk=pk, cf=cf, h=h: e.tensor_tensor(pp[pk][:, 0:4, :], pp[pk][:, 0:4, :], Et[:, cf, h * 5:h * 5 + 4, :], ALU.mult)),
                     reads=[t_E[cf]], writes=[t_pp[pk]])
                if nb == 5:
                    P.op("dve", (lambda e, pk=pk, cf=cf, h=h: e.tensor_tensor(pp[pk][0:64, 4, :], pp[pk][0:64, 4, :], Et[0:64, cf, h * 5 + 4, :], ALU.mult)),
                         reads=[t_E[cf]], writes=[t_pp[pk]])
                for bl in range(nb):
                    kn = 128 if bl < 4 else 64
                    P.op("pe", (lambda e, k=k, pk=pk, h=h, bl=bl, kn=kn, ob=ob: e.matmul(
                        pb[ob][:, h * 65:(h + 1) * 65], pp[pk][0:kn, bl, :], vn[k][0:kn, bl, h * 65:(h + 1) * 65],
                        start=(bl == 0), stop=(bl == nb - 1))), reads=[t_pp[pk], t_vn[k]], writes=[t_pb[ob]])
            ov = pb[ob][:, 0:260].rearrange("p (h d) -> p h d", h=4)
            P.op("dve", (lambda e, ov=ov: e.reciprocal(rdn[:], ov[:, :, 64])), reads=[t_pb[ob]], writes=[t_rdn])
            for h in range(4):
                P.op("dve", (lambda e, k=k, h=h, ov=ov: e.scalar_tensor_tensor(no[k][:, h * 64:(h + 1) * 64], ov[:, h, 0:64], rdn[:, h:h + 1],
                                                                             gzn[k][:, h * 64:(h + 1) * 64], ALU.mult, ALU.mult)),
                     reads=[t_pb[ob], t_rdn, t_gzn[k]], writes=[t_no[k]])
            for hf in range(2):
                P.op("pe", (lambda e, k=k, hf=hf: e.transpose(pX[:, hf * 128:(hf + 1) * 128], no[k][:, hf * 128:(hf + 1) * 128], idb[:])),
                     reads=[t_no[k], t_id], writes=[t_pX])
            P.op("act", (lambda e, k=k: e.activation(noT[k][:], pX[:, 0:256].rearrange("p (a t) -> p a t", a=2), AF.Copy)),
                 reads=[t_pX], writes=[t_noT[k]])
            P.dma("sp", mp[i // 8].rearrange("(a p) t -> p a t", p=128)[:, 2:4, (i % 8) * 128:(i % 8 + 1) * 128], noT[k][:], f"nn{k}",
                  reads=[t_noT[k]], writes=[t_mpp[i // 8][8 + i % 8]])


    def ag(p):
        P.coll("AllGather", GROUPS, mp[p], mg[p], f"cc{p % 4}", reads=t_mpp[p], writes=[io["t_mg"][p]])
    m_front(0)
    for c in range(64):
        if c + 1 < 64:
            m_front(c + 1)
        m_back(c)
        n_tile(c)
        if c >= 35 and c % 4 == 3:
            m_final4((c - 3) // 4)
            m_final4((63 - c) // 4)
            if c % 8 == 7:
                q = (c - 39) // 8
                ag(3 - q)
                ag(4 + q)


def na_bias_tables(rpb4):
    kp = np.arange(128)
    q = np.arange(128)
    bias = np.zeros((5, 128, 20, 128), np.float32)
    mask = np.zeros((5, 128, 5, 128), np.float32)
    for ci, i in enumerate(NA_CFG_TILES):
        _, r0k, nb = na_tile_cfg(i)
        qr = 2 * i + q // 64
        qc = q % 64
        rr0 = np.clip(qr - 4, 0, 120)
        cc0 = np.clip(qc - 8, 0, 48)
        for bl in range(nb):
            npart = 128 if bl < 4 else 64
            kr = r0k + bl * 2 + kp // 64
            kc = kp % 64
            valid = ((kr[:, None] >= rr0[None, :]) & (kr[:, None] < rr0[None, :] + 8) &
                     (kc[:, None] >= cc0[None, :]) & (kc[:, None] < cc0[None, :] + 16) & (kp[:, None] < npart))
            dr = np.clip(kr[:, None] - qr[None, :] + 7, 0, 14)
            dc = np.clip(kc[:, None] - qc[None, :] + 15, 0, 30)
            mask[ci, :, bl, :] = valid
            for h in range(4):
                bias[ci, :, h * 5 + bl, :] = rpb4[h][dr, dc]
    return bias, mask


def emit_outproj(nc, P, PSB, t_PSB, io, final):
    mg, t_mg, wout = io["mg"], io["t_mg"], io["wout"]
    m0 = P.mark()
    wb = P.sb([128, 16, D], BF16); t_wb = Tok()
    for c in range(16):
        r, q = c // 4, c % 4
        row0 = (r * 256 + q * 128) if q < 2 else (1024 + r * 256 + (q - 2) * 128)
        P.dma("pool", wb[:, c, :], wout[row0:row0 + 128, :], "wb", writes=[t_wb])
    NBUF = 4
    mt = [P.sb([128, 16, 256], BF16) for _ in range(NBUF)]; t_mt = [Tok() for _ in range(NBUF)]
    xr = [P.sb([128, 8, 256], F32) for _ in range(NBUF)]; t_xr = [Tok() for _ in range(NBUF)]
    hT = [P.sb([128, 8, 256], F32) for _ in range(2)]; t_hT = [Tok(), Tok()]
    if final:
        ones_bf = P.sb([128, 128], BF16); t_ones = Tok()
        P.op("pool", lambda e: e.memset(ones_bf[:], 1.0), writes=[t_ones])
        epsb = P.sb([128, 1], F32); t_eps = Tok()
        P.op("pool", lambda e: e.memset(epsb[:], EPS), writes=[t_eps])
        gfs = P.sb([128, 8], F32); t_gf = Tok()
        P.dma("sp", gfs[:], io["gfin"][:, :], "c_ng", writes=[t_gf])
        xsq = P.sb([128, 8, 256], BF16); t_xsq = Tok()
        lnv = P.sb([128, 256], F32); t_lnv = Tok()
        rstd = P.sb([128, 256], F32); t_rstd = Tok()
        ob = [P.sb([128, 8, 256], F32) for _ in range(2)]; t_ob = [Tok(), Tok()]
    order = io.get("order", list(range(8)))
    for n_, i in enumerate(order):
        k = n_ % 2
        kl = n_ % NBUF

        def ld(e, i=i, k=kl):
            if "rank" not in P.__dict__:
                P.rank = e.partition_id() % 4
            r = P.rank
            return e.dma_start(out=mt[k][:], in_=mg[i].rearrange("(c p) t -> p c t", p=128)[:, :, bass.ts(r, 256)])
        P.dma_fn("sp", ld, f"ml{kl}", reads=[t_mg[i]], writes=[t_mt[kl]])
        if "xres" in io:
            P.dma("sp", xr[kl][:], io["xres"][i].rearrange("(c p) t -> p c t", p=128), f"xr{kl}", writes=[t_xr[kl]])
        else:
            P.dma("sp", xr[kl][:], io["hp_in"][i].rearrange("(c p) t -> p c t", p=128), f"xr{kl}", reads=[io["t_hp_in"][i]], writes=[t_xr[kl]])
        for n in range(8):
            bank = 1 + n % 4
            for c in range(16):
                P.op("pe", (lambda e, c=c, n=n, kl=kl, bank=bank: e.matmul(PSB[bank][:, 0:256], wb[:, c, n * 128:(n + 1) * 128], mt[kl][:, c, :],
                                                                      start=(c == 0), stop=(c == 15))),
                     reads=[t_wb, t_mt[kl]], writes=[t_PSB[bank]])
            P.op("dve", (lambda e, n=n, k=k, kl=kl, bank=bank: e.tensor_tensor(hT[k][:, n, :], PSB[bank][:, 0:256], xr[kl][:, n, :], ALU.add)),
                 reads=[t_PSB[bank], t_xr[kl]], writes=[t_hT[k]])
        if not final:
            hp, t_hp, hgl, t_hg = io["hp"], io["t_hp"], io["hg"], io["t_hg"]
            P.dma("sp", hp[i].rearrange("(c p) t -> p c t", p=128), hT[k][:], f"ho{k}", reads=[t_hT[k]], writes=[t_hp[i]])
            P.coll("AllGather", GROUPS, hp[i], hgl[i], f"cc{4 + i % 4}", reads=[t_hp[i]], writes=[t_hg[i]])
        else:
            P.op("act", (lambda e, k=k: e.activation(xsq[:], hT[k][:], AF.Square)), reads=[t_hT[k]], writes=[t_xsq])
            for c in range(8):
                P.op("pe", (lambda e, c=c: e.matmul(PSB[0][:, 0:256], ones_bf[:], xsq[:, c, :], start=(c == 0), stop=(c == 7))),
                     reads=[t_ones, t_xsq], writes=[t_PSB[0]])
            P.op("act", (lambda e: e.activation(lnv[:], PSB[0][:, 0:256], AF.Ln, bias=epsb[:, 0:1], scale=1.0 / D)),
                 reads=[t_PSB[0], t_eps], writes=[t_lnv])
            P.op("act", (lambda e: e.activation(rstd[:], lnv[:], AF.Exp, scale=-0.5)), reads=[t_lnv], writes=[t_rstd])
            for c in range(8):
                P.op("dve", (lambda e, c=c, k=k: e.scalar_tensor_tensor(ob[k][:, c, :], hT[k][:, c, :], gfs[:, c:c + 1], rstd[:], ALU.mult, ALU.mult)),
                     reads=[t_hT[k], t_gf, t_rstd], writes=[t_ob[k]])
            P.dma("sp", io["outT"][i].rearrange("(c p) t -> p c t", p=128), ob[k][:], f"oo{k}", reads=[t_ob[k]])
    P.barrier()
    P.release(m0)


def build_fused(nstages=4, a_kw=None, c_kw=None):
    nc = bass.Bass("TRN2", target_bir_lowering=False)
    din = lambda n, s_, d=F32: nc.dram_tensor(n, s_, d, kind="ExternalInput").ap()
    dint = lambda n, s_, d: nc.dram_tensor(n, s_, d, kind="Internal").ap()
    a_hT = din("a_hT", [D, S]); a_w = din("a_w", [D, NWA]); a_ng = din("a_ng", [128, 8])
    gateb = din("gateb", [1, 4]); mlg = din("mlg", [1, 256]); tri = din("tri", [2, 128, 128])
    nab = din("nab", [5, 128, 20, 128]); nam = din("nam", [5, 128, 5, 128]); ident = din("ident", [128, 128], BF16)
    b_wout = din("b_wout", [2048, D]); b_xres = din("b_xres", [8, D, 256])
    c_w = din("c_w", [D, NWC]); c_ng = din("c_ng", [128, 8]); qkg = din("qkg", [2, 128]); cs = din("cs", [S, 128]); cwb = din("cwb", [128, 8])
    d_wout = din("d_wout", [2048, D]); gfin = din("gfin", [128, 8])
    outT = nc.dram_tensor("outT", [8, D, 256], F32, kind="ExternalOutput").ap()
    mp0 = [dint(f"mp0_{i}", [512, 1024], BF16) for i in range(8)]
    mg0 = [dint(f"mg0_{i}", [2048, 1024], BF16) for i in range(8)]
    mp1 = [dint(f"mp1_{i}", [512, 1024], BF16) for i in range(8)]
    mg1 = [dint(f"mg1_{i}", [2048, 1024], BF16) for i in range(8)]
    hp = [dint(f"hp_{i}", [D, 256], F32) for i in range(8)]
    hgl = [dint(f"hg_{i}", [4 * D, 256], F32) for i in range(8)]
    t_mg0 = [Tok() for _ in range(8)]; t_mg1 = [Tok() for _ in range(8)]
    t_hp = [Tok() for _ in range(8)]; t_hg = [Tok() for _ in range(8)]

    P = Prog(nc)
    PSB = [P.ps([128, 512]) for _ in range(8)]
    t_PSB = [XTok() for _ in range(8)]
    PX = PSB[7][:].bitcast(BF16); t_PX = t_PSB[7]

    m = P.mark()
    emit_mixer0(nc, P, PSB, t_PSB, PX, t_PX, dict(hT=a_hT, w=a_w, ng=a_ng, gateb=gateb, mlg=mlg, tri=tri, nab=nab, nam=nam, ident=ident,
                                                 mp=mp0, mg=mg0, t_mg=t_mg0), **(a_kw or {}))
    P.barrier(); P.release(m)
    if nstages >= 2:
        emit_outproj(nc, P, PSB, t_PSB, dict(mg=mg0, t_mg=t_mg0, wout=b_wout, xres=b_xres, hp=hp, t_hp=t_hp, hg=hgl, t_hg=t_hg,
                                             order=[3, 4, 2, 5, 1, 6, 0, 7]), final=False)
    m = P.mark()
    if nstages >= 3:
        emit_mixer1(nc, P, PSB, t_PSB, PX, t_PX, dict(hg=hgl, t_hg=t_hg, w=c_w, ng=c_ng, qkg=qkg, cs=cs, cwb=cwb, ident=ident,
                                                     mp=mp1, mg=mg1, t_mg=t_mg1, order=[3, 4, 2, 5, 1, 6, 0, 7]), **(c_kw or {}))
    P.barrier(); P.release(m)
    if nstages >= 4:
        emit_outproj(nc, P, PSB, t_PSB, dict(mg=mg1, t_mg=t_mg1, wout=d_wout, hp_in=hp, t_hp_in=t_hp, gfin=gfin, outT=outT), final=True)
    P.build()
    return nc


def kernel(x, norm_g, final_g, ev_w_in, ev_gate_b, ev_w_out, ev_ml_norm_g, ev_na_rpb,
           od_w_in, od_w_out, od_q_norm_g, od_k_norm_g, od_conv_w, od_conv_b):
    f = lambda a: np.asarray(a, dtype=np.float32)
    x, norm_g, final_g = f(x), f(norm_g), f(final_g)
    ev_w_in, ev_gate_b, ev_w_out, ev_ml_norm_g, ev_na_rpb = f(ev_w_in)[0], f(ev_gate_b)[0], f(ev_w_out)[0], f(ev_ml_norm_g)[0], f(ev_na_rpb)[0]
    od_w_in, od_w_out, qg, kg, conv_w, conv_b = f(od_w_in)[0], f(od_w_out)[0], f(od_q_norm_g)[0], f(od_k_norm_g)[0], f(od_conv_w)[0], f(od_conv_b)[0]
    nc = build_fused()
    ident = np.eye(128, dtype=np.float32).astype(ml_dtypes.bfloat16)
    s_ = np.arange(128)
    tri = np.stack([(s_[:, None] <= s_[None, :]), (s_[:, None] >= s_[None, :])]).astype(np.float32)
    cs = rope_tables()
    xT = [np.ascontiguousarray(x[b].T) for b in range(2)]
    r256 = np.arange(256)
    in_maps = []
    for core in range(NCORES):
        b, hg = core // 4, core % 4
        cols0 = np.concatenate([
            0 + hg * 256 + r256, 1024 + hg * 256 + r256, 2048 + hg * 256 + r256, 3072 + hg * 256 + r256,
            4096 + hg * 256 + r256, 5136 + 3072 + hg * 256 + r256, 5136 + hg * 256 + r256, 5136 + 1024 + hg * 256 + r256,
            5136 + 2048 + hg * 256 + r256, 5120 + np.array([hg, 4 + hg, 8 + hg, 12 + hg])])
        gb = ev_gate_b[np.array([hg, 4 + hg, 8 + hg, 12 + hg])].reshape(1, 4)
        bias, mask = na_bias_tables(ev_na_rpb[4 * hg:4 * hg + 4])
        kvh = hg // 2
        cols1 = np.concatenate([
            hg * 256 + r256, 1024 + kvh * 128 + np.arange(128), 1280 + kvh * 128 + np.arange(128),
            1536 + hg * 256 + r256, 2560 + hg * 256 + r256, 3584 + hg * 256 + r256, 4608 + hg * 256 + r256, 5632 + hg * 256 + r256])
        cwb = np.zeros((128, 8), np.float32)
        for cc in range(2):
            ch = hg * 256 + cc * 128 + np.arange(128)
            for i in range(3):
                cwb[:, cc * 4 + i] = conv_w[i, ch]
            cwb[:, cc * 4 + 3] = conv_b[ch]
        xres = np.stack([xT[b][:, (4 * i + hg) * 256:(4 * i + hg + 1) * 256] for i in range(8)])
        in_maps.append({
            "a_hT": xT[b], "a_w": np.ascontiguousarray(ev_w_in[:, cols0]), "a_ng": np.ascontiguousarray(norm_g[0].reshape(8, 128).T),
            "gateb": np.ascontiguousarray(gb), "mlg": np.ascontiguousarray(ev_ml_norm_g[hg * 256:(hg + 1) * 256].reshape(1, 256)),
            "tri": tri, "nab": bias, "nam": mask, "ident": ident,
            "b_wout": np.ascontiguousarray(ev_w_out), "b_xres": np.ascontiguousarray(xres),
            "c_w": np.ascontiguousarray(od_w_in[:, cols1]), "c_ng": np.ascontiguousarray(norm_g[1].reshape(8, 128).T),
            "qkg": np.stack([qg, kg]), "cs": cs, "cwb": cwb,
            "d_wout": np.ascontiguousarray(od_w_out), "gfin": np.ascontiguousarray(final_g.reshape(8, 128).T)})
    res = run_bass_kernel_spmd(nc, in_maps, core_ids=list(range(NCORES)))
    out = np.empty((2, S, D), np.float32)
    for core in range(NCORES):
        b, tq = core // 4, core % 4
        o = res.results[core]["outT"]
        for i in range(8):
            out[b, (4 * i + tq) * 256:(4 * i + tq + 1) * 256, :] = o[i].T
    return out
```

```python
import numpy as np
import ml_dtypes
from contextlib import ExitStack
import concourse.bass as bass
import concourse.mybir as mybir
from concourse.bass_utils import run_bass_kernel_spmd

F32 = mybir.dt.float32
BF16 = mybir.dt.bfloat16
AF = mybir.ActivationFunctionType
ALU = mybir.AluOpType
AX = mybir.AxisListType

S = 8192
D = 1024
NCORES = 8
GROUPS = [[0, 1, 2, 3], [4, 5, 6, 7]]
EPS = 1e-6


class Tok:
    __slots__ = ("w", "r", "x")

    def __init__(self, x=False):
        self.w = None
        self.r = {}
        self.x = x


def XTok():
    return Tok(True)


class Prog:
    ENG = ("pe", "act", "dve", "pool", "sp")

    def __init__(self, nc):
        self.nc = nc
        self.ins = {e: [] for e in self.ENG}
        self.seen = {e: {} for e in self.ENG}
        self.dma_cnt = {}
        self.stack = ExitStack()
        self._n = 0

    SB_BASE = 16512
    SB_TOP = 229376

    def sb(self, shape, dt, name=None):
        self._n += 1
        nb = int(np.prod(shape[1:])) * (4 if dt == F32 else 2)
        nb = (nb + 31) // 32 * 32
        off = getattr(self, "sbp", self.SB_BASE)
        assert off + nb <= self.SB_TOP, f"SBUF overflow: {off + nb}"
        self.sbp = off + nb
        return self.nc.alloc_sbuf_tensor_at(name or f"sb{self._n}", list(shape), dt, offset=off)

    def mark(self):
        return getattr(self, "sbp", self.SB_BASE)

    def release(self, m):
        self.sbp = m

    def ps(self, shape, dt=F32, name=None):
        self._n += 1
        return self.stack.enter_context(self.nc.psum_tensor(name or f"ps{self._n}", list(shape), dt))

    def _deps(self, eng, reads, writes):
        deps = {}
        def add(ev):
            key = ev[1]
            if key not in deps or deps[key][2] < ev[2]:
                deps[key] = ev
        for t in reads:
            if t.w is not None:
                add(t.w)
        for t in writes:
            if t.w is not None:
                add(t.w)
            for ev in t.r.values():
                add(ev)
        waits = []
        seen = self.seen[eng]
        for key, ev in deps.items():
            if ev[0] == "c" and key == eng and eng == "pe":
                continue
            if seen.get(key, -1) >= ev[2]:
                continue
            seen[key] = ev[2]
            waits.append(ev)
        return waits

    def op(self, eng, fn, reads=(), writes=()):
        xs = [t for t in reads if t.x]
        if xs:
            reads = [t for t in reads if not t.x]
            writes = list(writes) + xs
        waits = self._deps(eng, reads, writes)
        idx = len(self.ins[eng])
        ev = ("c", eng, idx)
        for t in reads:
            t.r[eng] = ev
        for t in writes:
            t.w = ev
            t.r = {}
        self.ins[eng].append([fn, waits, False, None])

    def _alias(self, sem):
        al = self.__dict__.setdefault("sem_alias", {})
        if sem not in al:
            al[sem] = f"d{len(al)}"
        return al[sem]

    def dma(self, eng, out, in_, sem, reads=(), writes=(), **kw):
        if eng == "sp" and type(out.tensor).__name__ == "DRamTensorHandle":
            eng = "pool"
        sem = self._alias(sem)
        waits = self._deps(eng, reads, writes)
        self.dma_cnt[sem] = self.dma_cnt.get(sem, 0) + 16
        ev = ("d", sem, self.dma_cnt[sem])
        for t in reads:
            t.r[sem] = ev
        for t in writes:
            t.w = ev
            t.r = {}
        fn = lambda e: e.dma_start(out=out, in_=in_, **kw)
        self.ins[eng].append([fn, waits, None, sem])

    def dma_fn(self, eng, fn, sem, reads=(), writes=()):
        sem = self._alias(sem)
        waits = self._deps(eng, reads, writes)
        self.dma_cnt[sem] = self.dma_cnt.get(sem, 0) + 16
        ev = ("d", sem, self.dma_cnt[sem])
        for t in reads:
            t.r[sem] = ev
        for t in writes:
            t.w = ev
            t.r = {}
        self.ins[eng].append([fn, waits, None, sem])

    def coll(self, kind, groups, in_ap, out_ap, sem, reads=(), writes=()):
        eng = "pool"
        sem = "coll_" + sem
        self.__dict__.setdefault("coll_sems", set()).add(sem)
        waits = self._deps(eng, reads, writes)
        self.dma_cnt[sem] = self.dma_cnt.get(sem, 0) + 1
        ev = ("d", sem, self.dma_cnt[sem])
        for t in reads:
            t.r[sem] = ev
        for t in writes:
            t.w = ev
            t.r = {}
        fn = lambda e: e.collective_compute(kind, ALU.bypass, replica_groups=groups, ins=[in_ap.opt()], outs=[out_ap.opt()])
        self.ins[eng].append([fn, waits, None, (sem, 1)])

    def barrier(self):
        last = {}
        for e in self.ENG:
            for i in range(len(self.ins[e]) - 1, -1, -1):
                if self.ins[e][i][0] is not None and self.ins[e][i][3] is None:
                    last[e] = i
                    break
        for e in self.ENG:
            waits = []
            seen = self.seen[e]
            for src, idx in last.items():
                if src == e or seen.get(src, -1) >= idx:
                    continue
                seen[src] = idx
                waits.append(("c", src, idx))
            for sem, val in self.dma_cnt.items():
                if sem in self.__dict__.get("coll_sems", ()):
                    continue
                if seen.get(sem, -1) >= val:
                    continue
                seen[sem] = val
                waits.append(("d", sem, val))
            self.ins[e].append([None, waits, False, None])
        self.sem_alias = {}

    def build(self):
        nc = self.nc
        for e in self.ENG:
            for rec in self.ins[e]:
                for ev in rec[1]:
                    if ev[0] == "c":
                        self.ins[ev[1]][ev[2]][2] = True
        semval = {}
        for e in self.ENG:
            c = 0
            for i, rec in enumerate(self.ins[e]):
                if rec[2]:
                    c += 1
                    semval[(e, i)] = c
        names = [e for e in self.ENG if e != "sp"] + sorted(self.dma_cnt)
        sems = {n: self.stack.enter_context(nc.semaphore("s_" + n)) for n in names}
        final_dma = dict(self.dma_cnt)

        def replay(ename, eobj, last=False):
            for i, (fn, waits, sig, dsem) in enumerate(self.ins[ename]):
                for ev in waits:
                    if ev[0] == "c":
                        eobj.wait_ge(sems[ev[1]], semval[(ev[1], ev[2])])
                    else:
                        eobj.wait_ge(sems[ev[1]], ev[2])
                if fn is None:
                    continue
                inst = fn(eobj)
                if isinstance(dsem, tuple):
                    inst.then_inc(sems[dsem[0]], dsem[1])
                elif dsem is not None:
                    inst.then_inc(sems[dsem], 16)
                elif sig:
                    inst.then_inc(sems[ename], 1)
            if last:
                for n, v in final_dma.items():
                    eobj.wait_ge(sems[n], v)

        with nc.Block() as block:
            @block.tensor
            def _(e):
                replay("pe", e)

            @block.scalar
            def _(e):
                replay("act", e)

            @block.vector
            def _(e):
                replay("dve", e)

            @block.gpsimd
            def _(e):
                replay("pool", e)

            @block.sync
            def _(e):
                replay("sp", e, last=True)
        self.stack.close()


def w_join(P, toks, t_out):
    wj = P.sb([128, 1], F32)
    P.op("dve", lambda e: e.memset(wj[:], 0.0), reads=toks, writes=[t_out])


def _bf(a):
    return np.ascontiguousarray(a)


class NormCtx:
    def __init__(self, P, hTv, ngs, t_ng, ones_bf, t_ones, epsb, t_eps, ps_ss, t_ps_ss, blk=512):
        self.P = P
        self.blk = blk
        self.hTv = hTv
        self.ngs, self.t_ng = ngs, t_ng
        self.ones_bf, self.t_ones = ones_bf, t_ones
        self.epsb, self.t_eps = epsb, t_eps
        self.ps_ss, self.t_ps_ss = ps_ss, t_ps_ss
        self.xt = [P.sb([128, 8, blk], F32) for _ in range(2)]
        self.t_xt = [Tok(), Tok()]
        self.xsq = P.sb([128, 8, blk], BF16)
        self.t_xsq = Tok()
        self.lnv = P.sb([128, blk], F32)
        self.rstd = P.sb([128, blk], F32)
        self.t_lnv, self.t_rstd = Tok(), Tok()
        self.uT = [P.sb([128, 8, blk], BF16) for _ in range(2)]
        self.t_uT = [Tok(), Tok()]

    def load(self, j):
        k = j % 2
        if self.hTv is None:
            self.loader(j, self.xt[k], f"xt{k}", self.t_xt[k])
        else:
            self.P.dma("sp", self.xt[k][:], self.hTv[:, :, j * self.blk:(j + 1) * self.blk], f"xt{k}", writes=[self.t_xt[k]])

    def norm(self, j):
        P = self.P
        k = j % 2
        xt, xsq, uT = self.xt[k], self.xsq, self.uT[k]
        P.op("act", lambda e: e.activation(xsq[:], xt[:], AF.Square), reads=[self.t_xt[k]], writes=[self.t_xsq])
        for c in range(8):
            P.op("pe", (lambda e, c=c: e.matmul(self.ps_ss[:, 0:self.blk], self.ones_bf[:], xsq[:, c, :], start=(c == 0), stop=(c == 7))),
                 reads=[self.t_ones, self.t_xsq], writes=[self.t_ps_ss])
        P.op("act", lambda e: e.activation(self.lnv[:], self.ps_ss[:, 0:self.blk], AF.Ln, bias=self.epsb[:, 0:1], scale=1.0 / D),
             reads=[self.t_ps_ss, self.t_eps], writes=[self.t_lnv])
        P.op("act", lambda e: e.activation(self.rstd[:], self.lnv[:], AF.Exp, scale=-0.5), reads=[self.t_lnv], writes=[self.t_rstd])
        for c in range(8):
            P.op("dve", (lambda e, c=c: e.scalar_tensor_tensor(uT[:, c, :], xt[:, c, :], self.ngs[:, c:c + 1], self.rstd[:],
                                                              ALU.mult, ALU.mult)),
                 reads=[self.t_xt[k], self.t_ng, self.t_rstd], writes=[self.t_uT[k]])
        return uT, self.t_uT[k]


NWC = 1792


def emit_mixer1(nc, P, PSB, t_PSB, PXg, t_PXg, io, nblk=16, do_v=True, do_g=True, sub=99):
    hgl, t_hg, w, ng, qkg, cs, cwb, ident = (io[k] for k in ("hg", "t_hg", "w", "ng", "qkg", "cs", "cwb", "ident"))
    mp, mg = io["mp"], io["mg"]
    dint = lambda n, s_, d: nc.dram_tensor(n, s_, d, kind="Internal").ap()
    qT_s = dint("qT_s", [2, 128, S], BF16)
    z_s = dint("z_s", [S, 256], F32)
    xc_s = dint("xc_s", [256, S], F32)
    gz_s = dint("gz_s", [256, S], F32)
    t_mpp = [[Tok() for _ in range(6)] for _ in range(8)]

    wb = P.sb([128, 8, NWC], BF16)
    t_wb = Tok()
    wv = w.rearrange("(c p) n -> p c n", p=128)
    _tw = []
    for c in range(8):
        _tw.append(Tok())
        P.dma("pool", wb[:, c, :], wv[:, c, :], "wb", writes=[_tw[-1]])
    w_join(P, _tw, t_wb)
    ngs = P.sb([128, 8], F32); t_ng = Tok()
    P.dma("sp", ngs[:], ng[:, :], "c_ng", writes=[t_ng])
    qg = P.sb([128, 2, 128], F32); t_qg = Tok()
    P.dma("sp", qg[:, 0, :], qkg[0:1, :].to_broadcast([128, 128]), "c_qg", writes=[t_qg])
    P.dma("sp", qg[:, 1, :], qkg[1:2, :].to_broadcast([128, 128]), "c_qg", writes=[t_qg])
    P.op("dve", lambda e: e.tensor_scalar(qg[:, 0, :], qg[:, 0, :], float(128 ** -0.5), None, ALU.mult), reads=[t_qg], writes=[t_qg])
    cw = P.sb([128, 8], F32); t_cw = Tok()
    P.dma("sp", cw[:], cwb[:, :], "c_cw", writes=[t_cw])
    idb = P.sb([128, 128], BF16); t_id = Tok()
    P.dma("sp", idb[:], ident[:, :], "c_id", writes=[t_id])
    ones_bf = P.sb([128, 128], BF16); t_ones = Tok()
    P.op("pool", lambda e: e.memset(ones_bf[:], 1.0), writes=[t_ones])
    epsb = P.sb([128, 2], F32); t_eps = Tok()
    P.op("pool", lambda e: e.memset(epsb[:], EPS), writes=[t_eps])
    kT = P.sb([128, S], BF16)
    t_kT = [Tok() for _ in range(64)]
    vx = P.sb([128, 64, 129], BF16)
    t_vx = [Tok() for _ in range(64)]
    t_vones = Tok()
    P.op("pool", lambda e: e.memset(vx[:, :, 128:129], 1.0), writes=[t_vones])
    pT = [[PSB[1], PSB[2]], [PSB[3], PSB[4]]]
    t_pT = [[t_PSB[1], t_PSB[2]], [t_PSB[3], t_PSB[4]]]
    pB = [PSB[0], PSB[5], PSB[6]]
    t_pB = [t_PSB[0], t_PSB[5], t_PSB[6]]
    pX = PXg
    t_pX = [t_PXg, t_PXg]

    nctx = NormCtx(P, None, ngs, t_ng, ones_bf, t_ones, epsb, t_eps, pB[0], t_pB[0])

    def _loader(j, xt, sem, tok):
        i, r0 = j // 2, (j % 2) * 2
        hv = hgl[i].rearrange("(r c p) t -> p r c t", r=4, p=128)
        for rr in range(2):
            P.dma("sp", xt[:, :, rr * 256:(rr + 1) * 256], hv[:, r0 + rr, :, :], sem, reads=[t_hg[i]], writes=[tok])
    nctx.loader = _loader
    csb = [P.sb([128, 4, 128], F32) for _ in range(2)]; t_cs = [Tok(), Tok()]
    st = [P.sb([128, 12], F32) for _ in range(2)]; t_st = [Tok(), Tok()]
    junk = P.sb([128, 128], F32); t_junk = Tok()
    xn = [P.sb([128, 3, 128], F32) for _ in range(2)]; t_xn = [Tok(), Tok()]
    tmp = [P.sb([128, 3, 64], F32) for _ in range(4)]; t_tmp = [Tok() for _ in range(4)]
    rot = [P.sb([128, 3, 128], BF16) for _ in range(2)]; t_rot = [Tok(), Tok()]
    qTst = [P.sb([128, 2, 512], BF16) for _ in range(2)]; t_qTst = [Tok(), Tok()]
    zst = [P.sb([128, 4, 256], F32) for _ in range(2)]; t_zst = [Tok(), Tok()]
    ccs = P.sb([128, 512], F32); t_ccs = Tok()
    sis = P.sb([128, 512], F32); t_sis = Tok()
    xcst = [P.sb([128, 2, 512], F32) for _ in range(2)]; t_xcst = [Tok(), Tok()]
    gzst = [P.sb([128, 2, 512], F32) for _ in range(2)]; t_gzst = [Tok(), Tok()]
    t_qs = [Tok() for _ in range(16)]
    t_zs = [Tok() for _ in range(16)]
    t_xcs = [Tok() for _ in range(16)]
    t_gzs = [Tok() for _ in range(16)]
    csv = cs.rearrange("(n p) f -> p n f", p=128)
    z_sv = z_s.rearrange("(n p) f -> p n f", p=128)
    qT_sv = qT_s.rearrange("h d t -> d h t")
    xc_sv = xc_s.rearrange("(c p) t -> p c t", p=128)
    gz_sv = gz_s.rearrange("(c p) t -> p c t", p=128)

    def r4(ap):
        return ap.rearrange("p h (a r i) -> p h a r i", a=2, r=2)

    jorder = [2 * p + q for p in io.get("order", list(range(8))) for q in range(2)][:nblk] if nblk == 16 else list(range(nblk))
    nctx.load(jorder[0])
    pend_tr = None
    for jn, j in enumerate(jorder):
        kb = j % 2
        jnext = jorder[jn + 1] if jn + 1 < len(jorder) else None
        if jnext is not None:
            nctx.load(jnext)
        P.dma("sp", csb[kb][:], csv[:, 4 * j:4 * j + 4, :], f"cs{kb}", writes=[t_cs[kb]])
        if jn == 0:
            pend = nctx.norm(j)
        uT, t_uT = pend
        def fm(m, pi):
            for c in range(8):
                P.op("pe", (lambda e, c=c, m=m, pi=pi, uT=uT: e.matmul(pB[pi][:], wb[:, c, 768 + m * 128:768 + (m + 1) * 128], uT[:, c, :],
                                                                 start=(c == 0), stop=(c == 7))),
                     reads=[t_uT, t_wb], writes=[t_pB[pi]])

        def fm_part(part):
            cc = part // 2
            if part % 2 == 0:
                fm(2 + cc, 1)
                P.op("act", lambda e: e.activation(ccs[:], pB[1][:], AF.Copy), reads=[t_pB[1]], writes=[t_ccs])
                fm(4 + cc, 2)
                P.op("dve", (lambda e, cc=cc, kb=kb: e.tensor_tensor(xcst[kb][:, cc, :], pB[2][:], ccs[:], ALU.mult)),
                     reads=[t_pB[2], t_ccs], writes=[t_xcst[kb]])
            else:
                fm(6 + cc, 1)
                P.op("act", lambda e: e.activation(sis[:], pB[1][:], AF.Silu), reads=[t_pB[1]], writes=[t_sis])
                fm(0 + cc, 2)
                P.op("dve", (lambda e, cc=cc, kb=kb: e.tensor_tensor(gzst[kb][:, cc, :], pB[2][:], sis[:], ALU.mult)),
                     reads=[t_pB[2], t_sis], writes=[t_gzst[kb]])
        for tc in range(4):
            ch = 4 * j + tc
            k = ch % 2
            ps = pT[k]
            if tc == 1 and jnext is not None:
                pend = nctx.norm(jnext)
            for (half, c0, c1) in ((0, 0, 512), (1, 512, 768)):
                for c in range(8):
                    P.op("pe", (lambda e, c=c, half=half, c0=c0, c1=c1, ps=ps, tc=tc, uT=uT:
                                e.matmul(ps[half][:, 0:c1 - c0], uT[:, c, tc * 128:(tc + 1) * 128], wb[:, c, c0:c1],
                                         start=(c == 0), stop=(c == 7))),
                         reads=[t_uT, t_wb], writes=[t_pT[k][half]])
            fm_part(tc)
            if pend_tr is not None:
                pend_tr()
            for h in range(3):
                P.op("act", (lambda e, h=h, ps=ps, k=k: e.activation(junk[:], ps[0][:, h * 128:(h + 1) * 128], AF.Square,
                                                                     accum_out=st[k][:, h:h + 1])),
                     reads=[t_pT[k][0]], writes=[t_junk, t_st[k]])
            P.op("act", (lambda e, k=k: e.activation(st[k][:, 4:7], st[k][:, 0:3], AF.Ln, bias=epsb[:, 0:1], scale=1.0 / 128)),
                 reads=[t_st[k], t_eps], writes=[t_st[k]])
            P.op("act", (lambda e, k=k: e.activation(st[k][:, 8:11], st[k][:, 4:7], AF.Exp, scale=-0.5)),
                 reads=[t_st[k]], writes=[t_st[k]])
            for h in range(3):
                P.op("dve", (lambda e, h=h, ps=ps, k=k: e.scalar_tensor_tensor(
                    xn[k][:, h, :], ps[0][:, h * 128:(h + 1) * 128], st[k][:, 8 + h:9 + h], qg[:, 1 if h == 2 else 0, :],
                    ALU.mult, ALU.mult)), reads=[t_pT[k][0], t_st[k], t_qg], writes=[t_xn[k]])
            P.op("act", (lambda e, ps=ps, ch=ch: e.activation(vx[:, ch, 0:128], ps[0][:, 384:512], AF.Copy)),
                 reads=[t_pT[k][0], t_vones], writes=[t_vx[ch]])
            P.op("act", (lambda e, ps=ps, tc=tc, kb=kb: e.activation(zst[kb][:, tc, :], ps[1][:, 0:256], AF.Silu)),
                 reads=[t_pT[k][1]], writes=[t_zst[kb]])
            x4 = r4(xn[k][:])
            o4 = r4(rot[k][:])
            x1, x2 = x4[:, :, :, 0, :], x4[:, :, :, 1, :]
            cosv = csb[kb][:, tc, 0:64].rearrange("p (a i) -> p a i", a=2).unsqueeze(1).to_broadcast([128, 3, 2, 32])
            sinv = csb[kb][:, tc, 64:128].rearrange("p (a i) -> p a i", a=2).unsqueeze(1).to_broadcast([128, 3, 2, 32])
            tv = [t[:].rearrange("p h (a i) -> p h a i", a=2) for t in tmp]
            P.op("dve", (lambda e, x1=x1, cosv=cosv, tv=tv: e.tensor_tensor(tv[0], x1, cosv, ALU.mult)),
                 reads=[t_xn[k], t_cs[kb]], writes=[t_tmp[0]])
            P.op("dve", (lambda e, x2=x2, sinv=sinv, tv=tv: e.tensor_tensor(tv[1], x2, sinv, ALU.mult)),
                 reads=[t_xn[k], t_cs[kb]], writes=[t_tmp[1]])
            P.op("dve", (lambda e, o4=o4, tv=tv: e.tensor_tensor(o4[:, :, :, 0, :], tv[0], tv[1], ALU.subtract)),
                 reads=[t_tmp[0], t_tmp[1]], writes=[t_rot[k]])
            P.op("dve", (lambda e, x1=x1, sinv=sinv, tv=tv: e.tensor_tensor(tv[2], x1, sinv, ALU.mult)),
                 reads=[t_xn[k], t_cs[kb]], writes=[t_tmp[2]])
            P.op("dve", (lambda e, x2=x2, cosv=cosv, tv=tv: e.tensor_tensor(tv[3], x2, cosv, ALU.mult)),
                 reads=[t_xn[k], t_cs[kb]], writes=[t_tmp[3]])
            P.op("dve", (lambda e, o4=o4, tv=tv: e.tensor_tensor(o4[:, :, :, 1, :], tv[2], tv[3], ALU.add)),
                 reads=[t_tmp[2], t_tmp[3]], writes=[t_rot[k]])
            def tr(k=k, ch=ch, kb=kb, tc=tc, j=j):
                for h in range(3):
                    P.op("pe", (lambda e, h=h, k=k: e.transpose(pX[:, k * 512 + h * 128:k * 512 + (h + 1) * 128], rot[k][:, h, :], idb[:])),
                         reads=[t_rot[k], t_id], writes=[t_pX[k]])
                P.op("act", (lambda e, k=k, ch=ch: e.activation(kT[:, ch * 128:(ch + 1) * 128], pX[:, k * 512 + 256:k * 512 + 384], AF.Copy)),
                     reads=[t_pX[k]], writes=[t_kT[ch]])
                P.op("dve", (lambda e, k=k, kb=kb, tc=tc: e.tensor_copy(
                    qTst[kb][:, :, tc * 128:(tc + 1) * 128], pX[:, k * 512:k * 512 + 256].rearrange("p (h t) -> p h t", h=2))),
                     reads=[t_pX[k]], writes=[t_qTst[kb]])
                if tc == 3:
                    P.dma("sp", qT_sv[:, :, j * 512:(j + 1) * 512], qTst[kb][:], f"qo{kb}", reads=[t_qTst[kb]], writes=[t_qs[j]])
            pend_tr = tr
        P.dma("sp", z_sv[:, 4 * j:4 * j + 4, :], zst[kb][:], f"zo{kb}", reads=[t_zst[kb]], writes=[t_zs[j]])
        P.dma("sp", xc_sv[:, :, j * 512:(j + 1) * 512], xcst[kb][:], f"xo{kb}", reads=[t_xcst[kb]], writes=[t_xcs[j]])
        P.dma("sp", gz_sv[:, :, j * 512:(j + 1) * 512], gzst[kb][:], f"go{kb}", reads=[t_gzst[kb]], writes=[t_gzs[j]])
    pend_tr()

    xcv = [P.sb([128, 2, 514], F32) for _ in range(2)]; t_xcv = [Tok(), Tok()]
    gzv = [P.sb([128, 2, 512], F32) for _ in range(2)]; t_gzv = [Tok(), Tok()]
    cv1 = P.sb([128, 512], F32); t_cv1 = Tok()
    cv2 = P.sb([128, 512], F32); t_cv2 = Tok()
    cvo = [P.sb([128, 2, 512], BF16) for _ in range(2)]; t_cvo = [Tok(), Tok()]
    for j in range(16 if do_v else 0):
        kb = j % 2
        lo = max(j * 512 - 1, 0)
        hi = min(j * 512 + 513, S)
        o0 = lo - (j * 512 - 1)
        rd = [t_xcs[j]] + ([t_xcs[j - 1]] if j > 0 else []) + ([t_xcs[j + 1]] if j < 15 else [])
        if j == 0:
            P.op("pool", lambda e: e.memset(xcv[0][:, :, 0:1], 0.0), writes=[t_xcv[0]])
        if j == 15:
            P.op("pool", lambda e: e.memset(xcv[1][:, :, 513:514], 0.0), writes=[t_xcv[1]])
        P.dma("sp", xcv[kb][:, :, o0:o0 + hi - lo], xc_sv[:, :, lo:hi], f"xv{kb}", reads=rd, writes=[t_xcv[kb]])
        P.dma("sp", gzv[kb][:], gz_sv[:, :, j * 512:(j + 1) * 512], f"gv{kb}", reads=[t_gzs[j]], writes=[t_gzv[kb]])
        for cc in range(2):
            w0, w1, w2, bb = (cw[:, cc * 4 + i:cc * 4 + i + 1] for i in range(4))
            P.op("dve", (lambda e, kb=kb, cc=cc, w0=w0, bb=bb: e.tensor_scalar(cv1[:], xcv[kb][:, cc, 0:512], w0, bb, ALU.mult, ALU.add)),
                 reads=[t_xcv[kb], t_cw], writes=[t_cv1])
            P.op("dve", (lambda e, kb=kb, cc=cc, w1=w1: e.scalar_tensor_tensor(cv2[:], xcv[kb][:, cc, 1:513], w1, cv1[:], ALU.mult, ALU.add)),
                 reads=[t_xcv[kb], t_cw, t_cv1], writes=[t_cv2])
            P.op("dve", (lambda e, kb=kb, cc=cc, w2=w2: e.scalar_tensor_tensor(cv1[:], xcv[kb][:, cc, 2:514], w2, cv2[:], ALU.mult, ALU.add)),
                 reads=[t_xcv[kb], t_cw, t_cv2], writes=[t_cv1])
            P.op("dve", (lambda e, kb=kb, cc=cc: e.tensor_tensor(cvo[kb][:, cc, :], cv1[:], gzv[kb][:, cc, :], ALU.mult)),
                 reads=[t_cv1, t_gzv[kb]], writes=[t_cvo[kb]])
        P.dma("sp", mp[j // 2].rearrange("(a p) t -> p a t", p=128)[:, 2:4, (j % 2) * 512:(j % 2 + 1) * 512], cvo[kb][:], f"co{kb}",
              reads=[t_cvo[kb]], writes=[t_mpp[j // 2][4 + j % 2]])

    qTb = [P.sb([128, 512], BF16) for _ in range(2)]; t_qTb = [Tok(), Tok()]
    szb = [P.sb([128, 4, 128], F32) for _ in range(2)]; t_szb = [Tok(), Tok()]
    pt = [P.sb([128, 512], BF16) for _ in range(3)]; t_pt = [Tok() for _ in range(3)]
    rden = P.sb([128, 4], F32); t_rden = Tok()
    aout = [P.sb([128, 4, 128], BF16) for _ in range(2)]; t_aout = [Tok(), Tok()]
    pS = [pT[0][0][:], pT[0][1][:]]
    t_pS = t_pT[0]
    pO = [pT[1][0][:], pT[1][1][:], pB[1][:], pB[2][:]]
    t_pO = [t_pT[1][0], t_pT[1][1], t_pB[1], t_pB[2]]
    aoT = [P.sb([128, 4, 128], BF16) for _ in range(2)]; t_aoT = [Tok(), Tok()]
    it = 0
    for hd in range(2 if do_g else 0):
        for qb in range(min(16, sub)):
            kq = it % 2
            it += 1
            P.dma("sp", qTb[kq][:], qT_s[hd][:, qb * 512:(qb + 1) * 512], f"ql{kq}", reads=[t_qs[qb]], writes=[t_qTb[kq]])
            P.dma("sp", szb[kq][:], z_sv[:, 4 * qb:4 * qb + 4, hd * 128:(hd + 1) * 128], f"zl{kq}", reads=[t_zs[qb]], writes=[t_szb[kq]])

            def smm(kc):
                P.op("pe", (lambda e, kc=kc, kq=kq: e.matmul(pS[kc % 2], kT[:, kc * 128:(kc + 1) * 128], qTb[kq][:], start=True, stop=True)),
                     reads=[t_kT[kc], t_qTb[kq]], writes=[t_pS[kc % 2]])
            smm(0)
            for kc in range(64):
                if kc + 1 < 64:
                    smm(kc + 1)
                pk = kc % 3
                P.op("act", (lambda e, kc=kc, pk=pk: e.activation(pt[pk][:], pS[kc % 2], AF.Exp)),
                     reads=[t_pS[kc % 2]], writes=[t_pt[pk]])
                for qs in range(4):
                    P.op("pe", (lambda e, kc=kc, pk=pk, qs=qs: e.matmul(pO[qs][:, 0:129], pt[pk][:, qs * 128:(qs + 1) * 128], vx[:, kc, :],
                                                                       start=(kc == 0), stop=(kc == 63))),
                         reads=[t_pt[pk], t_vx[kc], t_vones], writes=[t_pO[qs]])
            for qs in range(4):
                P.op("dve", (lambda e, qs=qs: e.reciprocal(rden[:, qs:qs + 1], pO[qs][:, 128:129])), reads=[t_pO[qs]], writes=[t_rden])
                P.op("dve", (lambda e, qs=qs, kq=kq: e.scalar_tensor_tensor(aout[kq][:, qs, :], pO[qs][:, 0:128], rden[:, qs:qs + 1],
                                                                          szb[kq][:, qs, :], ALU.mult, ALU.mult)),
                     reads=[t_pO[qs], t_rden, t_szb[kq]], writes=[t_aout[kq]])
            for qs in range(4):
                P.op("pe", (lambda e, qs=qs, kq=kq: e.transpose(pX[:, qs * 128:(qs + 1) * 128], aout[kq][:, qs, :], idb[:])),
                     reads=[t_aout[kq], t_id], writes=[t_pX[0]])
            P.op("act", (lambda e, kq=kq: e.activation(aoT[kq][:], pX[:, 0:512].rearrange("p (a t) -> p a t", a=4), AF.Copy)),
                 reads=[t_pX[0]], writes=[t_aoT[kq]])
            P.dma("sp", mp[qb // 2][hd * 128:(hd + 1) * 128, (qb % 2) * 512:(qb % 2 + 1) * 512], aoT[kq][:].rearrange("p a t -> p (a t)"), f"ao{kq}",
                  reads=[t_aoT[kq]], writes=[t_mpp[qb // 2][hd * 2 + qb % 2]])
            if hd == 1 and qb % 2 == 1:
                P.coll("AllGather", GROUPS, mp[qb // 2], mg[qb // 2], f"cc{(qb // 2) % 4}", reads=t_mpp[qb // 2], writes=[io["t_mg"][qb // 2]])


def rope_tables():
    t = np.arange(S)
    pos = np.stack([t // 64, t % 64], axis=-1).astype(np.float32)
    inv = (np.float32(10000.0) ** (-np.arange(32, dtype=np.float32) / np.float32(32))).astype(np.float32)
    ang = (pos[:, :, None] * inv).astype(np.float32)
    return np.concatenate([np.cos(ang).reshape(S, 64), np.sin(ang).reshape(S, 64)], axis=1).astype(np.float32)


NWA = 2308
NA_CFG_TILES = (0, 1, 2, 62, 63)


def na_tile_cfg(i):
    if i < 2:
        return i, 0, 4
    if i >= 62:
        return 3 + (i - 62), 120, 4
    return 2, 2 * i - 4, 5


def emit_mixer0(nc, P, PSB, t_PSB, PXg, t_PXg, io, do_m=True, do_n=True):
    BLK = 256
    NB = S // BLK
    hT, w, ng, gateb, mlg, tri, nab, nam, ident = (io[k] for k in ("hT", "w", "ng", "gateb", "mlg", "tri", "nab", "nam", "ident"))
    mp, mg = io["mp"], io["mg"]
    dint = lambda n, s, d: nc.dram_tensor(n, s, d, kind="Internal").ap()
    fm_s = dint("fm_s", [64, 128, 768], BF16)
    tm_s = dint("tm_s", [64, 128, 769], BF16)
    g_s = dint("g_s", [64, 128, 768], F32)
    na_s = dint("na_s", [4, 128, S], BF16)
    vna_s = dint("vna_s", [64, 128, 260], BF16)
    h_s = dint("h_s", [2, 64, 128, 256], F32)
    t_mpp = [[Tok() for _ in range(16)] for _ in range(8)]

    wb = P.sb([128, 8, NWA], BF16); t_wb = Tok()
    wv = w.rearrange("(c p) n -> p c n", p=128)
    _tw = []
    for c in range(8):
        for (a, b_) in ((0, 1024), (1024, 2048), (2048, NWA)):
            _tw.append(Tok())
            P.dma("pool", wb[:, c, a:b_], wv[:, c, a:b_], "wb", writes=[_tw[-1]])
    w_join(P, _tw, t_wb)
    ngs = P.sb([128, 8], F32); t_ng = Tok()
    P.dma("sp", ngs[:], ng[:, :], "c_ng", writes=[t_ng])
    gbb = P.sb([128, 4], F32); t_gbb = Tok()
    P.dma("sp", gbb[:], gateb[0:1, :].to_broadcast([128, 4]), "c_gb", writes=[t_gbb])
    mlgb = P.sb([128, 256], F32); t_mlg = Tok()
    P.dma("sp", mlgb[:], mlg[0:1, :].to_broadcast([128, 256]), "c_mlg", writes=[t_mlg])
    trs = P.sb([128, 2, 128], F32); t_tri = Tok()
    P.dma("sp", trs[:], tri.rearrange("a s t -> s a t"), "c_tri", writes=[t_tri])
    idb = P.sb([128, 128], BF16); t_id = Tok()
    P.dma("sp", idb[:], ident[:, :], "c_id", writes=[t_id])
    ones_bf = P.sb([128, 128], BF16); t_ones = Tok()
    P.op("pool", lambda e: e.memset(ones_bf[:], 1.0), writes=[t_ones])
    ones32 = P.sb([128, 128], F32); t_ones32 = Tok()
    P.op("pool", lambda e: e.memset(ones32[:], 1.0), writes=[t_ones32])
    epsb = P.sb([128, 2], F32); t_eps = Tok()
    P.op("pool", lambda e: e.memset(epsb[:, 0:1], EPS), writes=[t_eps])
    P.op("pool", lambda e: e.memset(epsb[:, 1:2], 1.0), writes=[t_eps])
    EA = P.sb([128, 64, 2], F32); t_EA = [Tok() for _ in range(64)]
    EG = P.sb([128, 64, 2], F32); t_EG = [Tok() for _ in range(64)]
    pb, t_pb, pX, t_pX = PSB, t_PSB, PXg, t_PXg

    nctx = NormCtx(P, hT.rearrange("(c p) t -> p c t", p=128), ngs, t_ng, ones_bf, t_ones, epsb, t_eps, pb[0], t_pb[0], blk=BLK)
    g4 = [P.sb([128, 4], F32) for _ in range(2)]; t_g4 = [Tok(), Tok()]
    sm = [P.sb([128, 16], F32) for _ in range(2)]; t_sm = [Tok(), Tok()]
    X5 = [P.sb([128, 5, 256], BF16) for _ in range(2)]; t_X5 = [Tok(), Tok()]
    FMst = [P.sb([128, 768], BF16) for _ in range(2)]; t_FMst = [Tok(), Tok()]
    vml = [P.sb([128, 257], BF16) for _ in range(2)]; t_vml = [Tok(), Tok()]
    GS = [P.sb([128, 3, 256], F32) for _ in range(2)]; t_GS = [Tok(), Tok()]
    Y4 = [P.sb([128, 512], BF16) for _ in range(2)]; t_Y4 = [Tok(), Tok()]
    NAst = [P.sb([128, 4, 128], BF16) for _ in range(2)]; t_NAst = [Tok(), Tok()]
    vst = [P.sb([128, 4, 65], BF16) for _ in range(2)]; t_vst = [Tok(), Tok()]
    t_fm = [Tok() for _ in range(64)]
    t_tm = [Tok() for _ in range(64)]
    t_gs = [Tok() for _ in range(64)]
    t_nas = [Tok() for _ in range(64)]
    t_vna = [Tok() for _ in range(64)]
    for k in range(2):
        P.op("pool", (lambda e, k=k: e.memset(vml[k][:, 256:257], 1.0)), writes=[t_vml[k]])
        P.op("pool", (lambda e, k=k: e.memset(vst[k][:, :, 64:65], 1.0)), writes=[t_vst[k]])
    na_sv = na_s.rearrange("a p t -> p a t")

    def grp(uT, t_uT, tc, c0, c1, bank):
        for c in range(8):
            P.op("pe", (lambda e, c=c: e.matmul(pb[bank][:, 0:c1 - c0], uT[:, c, tc * 128:(tc + 1) * 128], wb[:, c, c0:c1],
                                                start=(c == 0), stop=(c == 7))),
                 reads=[t_uT, t_wb], writes=[t_pb[bank]])

    nctx.load(0)
    pend_nat = None
    for j in range(NB):
        if j + 1 < NB:
            nctx.load(j + 1)
        if j == 0:
            pend = nctx.norm(0)
        uT, t_uT = pend
        for tc in range(BLK // 128):
            ch = j * (BLK // 128) + tc
            k = ch % 2
            if tc == 1 and j + 1 < NB:
                pend = nctx.norm(j + 1)
            if pend_nat is not None:
                pend_nat()
            grp(uT, t_uT, tc, 2048, 2308, 5)
            P.op("dve", (lambda e, k=k: e.tensor_tensor(g4[k][:], pb[5][:, 256:260], gbb[:], ALU.add)),
                 reads=[t_pb[5], t_gbb], writes=[t_g4[k]])
            P.op("act", (lambda e, k=k: e.activation(vst[k][:, :, 0:64], pb[5][:, 0:256].rearrange("p (h d) -> p h d", h=4), AF.Copy)),
                 reads=[t_pb[5]], writes=[t_vst[k]])
            P.dma("sp", vna_s[ch], vst[k][:].rearrange("p h d -> p (h d)"), f"vn{k}", reads=[t_vst[k]], writes=[t_vna[ch]])
            P.op("act", (lambda e, k=k: e.activation(sm[k][:, 0:2], g4[k][:, 1:4:2], AF.Exp, scale=-1.0)),
                 reads=[t_g4[k]], writes=[t_sm[k]])
            P.op("act", (lambda e, k=k: e.activation(sm[k][:, 2:4], sm[k][:, 0:2], AF.Ln, bias=epsb[:, 1:2])),
                 reads=[t_sm[k], t_eps], writes=[t_sm[k]])
            grp(uT, t_uT, tc, 0, 512, 1)
            grp(uT, t_uT, tc, 512, 1024, 2)
            P.op("act", (lambda e, k=k: e.activation(vml[k][:, 0:256], pb[2][:, 0:256], AF.Copy)), reads=[t_pb[2]], writes=[t_vml[k]])
            P.dma("sp", tm_s[ch][:, 512:769], vml[k][:], f"vo{k}", reads=[t_vml[k]], writes=[t_tm[ch]])
            P.op("pe", (lambda e, k=k: e.matmul(pb[6][:, 0:1], trs[:, 0, :], sm[k][:, 2:3], start=True, stop=True)),
                 reads=[t_tri, t_sm[k]], writes=[t_pb[6]])
            P.op("pe", (lambda e, k=k: e.matmul(pb[6][:, 1:2], trs[:, 1, :], sm[k][:, 3:4], start=True, stop=True)),
                 reads=[t_tri, t_sm[k]], writes=[t_pb[6]])
            P.op("pe", (lambda e, k=k: e.matmul(pb[6][:, 2:4], ones32[:], sm[k][:, 2:4], start=True, stop=True)),
                 reads=[t_ones32, t_sm[k]], writes=[t_pb[6]])
            P.op("act", (lambda e, k=k: e.activation(sm[k][:, 4:6], pb[6][:, 0:2], AF.Exp, scale=-1.0)),
                 reads=[t_pb[6]], writes=[t_sm[k]])
            P.op("act", (lambda e, ch=ch: e.activation(EG[:, ch, :], pb[6][:, 2:4], AF.Exp, scale=-1.0)),
                 reads=[t_pb[6]], writes=[t_EG[ch]])
            P.op("dve", (lambda e, k=k: e.tensor_tensor(sm[k][:, 6:8], pb[6][:, 0:2], g4[k][:, 0:4:2], ALU.add)),
                 reads=[t_pb[6], t_g4[k]], writes=[t_sm[k]])
            P.op("act", (lambda e, k=k: e.activation(sm[k][:, 8:10], sm[k][:, 6:8], AF.Exp)), reads=[t_sm[k]], writes=[t_sm[k]])
            P.op("dve", (lambda e, k=k, ch=ch: e.tensor_scalar(EA[:, ch, :], sm[k][:, 8:10], 1.0 / 16.0, None, ALU.mult)),
                 reads=[t_sm[k]], writes=[t_EA[ch]])
            grp(uT, t_uT, tc, 1024, 1536, 3)
            P.op("dve", (lambda e, k=k: e.tensor_scalar(X5[k][:, 0, :], pb[1][:, 0:256], sm[k][:, 4:5], None, ALU.mult)),
                 reads=[t_pb[1], t_sm[k]], writes=[t_X5[k]])
            P.op("dve", (lambda e, k=k: e.tensor_scalar(X5[k][:, 1, :], pb[1][:, 0:256], sm[k][:, 5:6], None, ALU.mult)),
                 reads=[t_pb[1], t_sm[k]], writes=[t_X5[k]])
            P.op("act", (lambda e, k=k: e.activation(X5[k][:, 2, :], pb[1][:, 256:512], AF.Copy)), reads=[t_pb[1]], writes=[t_X5[k]])
            P.op("dve", (lambda e, k=k, ch=ch: e.tensor_scalar(X5[k][:, 3, :], pb[1][:, 256:512], EA[:, ch, 0:1], None, ALU.mult)),
                 reads=[t_pb[1], t_EA[ch]], writes=[t_X5[k]])
            P.op("dve", (lambda e, k=k, ch=ch: e.tensor_scalar(X5[k][:, 4, :], pb[1][:, 256:512], EA[:, ch, 1:2], None, ALU.mult)),
                 reads=[t_pb[1], t_EA[ch]], writes=[t_X5[k]])
            P.dma("sp", tm_s[ch][:, 0:512], X5[k][:, 3:5, :].rearrange("p a d -> p (a d)"), f"to{k}", reads=[t_X5[k]], writes=[t_tm[ch]])
            grp(uT, t_uT, tc, 1536, 2048, 4)
            P.op("act", (lambda e, k=k: e.activation(GS[k][:, 0, :], pb[2][:, 256:512], AF.Tanh, scale=0.5)), reads=[t_pb[2]], writes=[t_GS[k]])
            P.op("dve", (lambda e, k=k: e.tensor_scalar(GS[k][:, 0, :], GS[k][:, 0, :], 0.5, 0.5, ALU.mult, ALU.add)), reads=[], writes=[t_GS[k]])
            P.op("act", (lambda e, k=k: e.activation(GS[k][:, 1:3, :], pb[3][:].rearrange("p (a d) -> p a d", a=2), AF.Silu)),
                 reads=[t_pb[3]], writes=[t_GS[k]])
            P.dma("sp", g_s[ch], GS[k][:].rearrange("p a d -> p (a d)"), f"go{k}", reads=[t_GS[k]], writes=[t_gs[ch]])
            P.op("act", (lambda e, k=k: e.activation(Y4[k][:], pb[4][:], AF.Copy)), reads=[t_pb[4]], writes=[t_Y4[k]])
            for s_ in range(3):
                for hf in range(2):
                    P.op("pe", (lambda e, k=k, s_=s_, hf=hf: e.transpose(pX[:, (s_ * 2 + hf) * 128:(s_ * 2 + hf + 1) * 128],
                                                                        X5[k][:, s_, hf * 128:(hf + 1) * 128], idb[:])),
                         reads=[t_X5[k], t_id], writes=[t_pX])
            P.op("act", (lambda e, k=k: e.activation(FMst[k][:], pX[:, 0:768], AF.Copy)), reads=[t_pX], writes=[t_FMst[k]])
            P.dma("sp", fm_s[ch], FMst[k][:], f"fo{k}", reads=[t_FMst[k]], writes=[t_fm[ch]])

            def nat(k=k, ch=ch):
                for a in range(4):
                    P.op("pe", (lambda e, k=k, a=a: e.transpose(pX[:, a * 128:(a + 1) * 128], Y4[k][:, a * 128:(a + 1) * 128], idb[:])),
                         reads=[t_Y4[k], t_id], writes=[t_pX])
                P.op("dve", (lambda e, k=k: e.tensor_copy(NAst[k][:], pX[:, 0:512].rearrange("p (a t) -> p a t", a=4))),
                     reads=[t_pX], writes=[t_NAst[k]])
                P.dma("sp", na_sv[:, :, ch * 128:(ch + 1) * 128], NAst[k][:], f"no{k}", reads=[t_NAst[k]], writes=[t_nas[ch]])
            pend_nat = nat
    pend_nat()

    if True:
        fmB = [[P.sb([128, 768], BF16) for _ in range(2)] for _ in range(2)]
        tmB = [[P.sb([128, 769], BF16) for _ in range(2)] for _ in range(2)]
        t_fmB = [[Tok(), Tok()], [Tok(), Tok()]]
        t_tmB = [[Tok(), Tok()], [Tok(), Tok()]]
        wT = [[P.sb([128, 128], BF16) for _ in range(2)] for _ in range(2)]; t_wT = [[Tok(), Tok()], [Tok(), Tok()]]
        Cf = [P.sb([128, 2, 257], F32) for _ in range(2)]; t_Cf = [Tok(), Tok()]
        Cb = [P.sb([128, 2, 257], BF16) for _ in range(2)]; t_Cb = [Tok(), Tok()]
        ctmp = [[P.sb([128, 2, 257], F32) for _ in range(2)] for _ in range(2)]; t_ctmp = [[Tok(), Tok()], [Tok(), Tok()]]
        dn = [P.sb([128, 4], F32) for _ in range(2)]; t_dn = [Tok(), Tok()]
        hst = [[P.sb([128, 256], F32) for _ in range(2)] for _ in range(2)]
        t_hst = [[Tok(), Tok()], [Tok(), Tok()]]
        t_hs = [[Tok() for _ in range(64)] for _ in range(2)]
        for d_ in range(2):
            P.op("pool", (lambda e, d_=d_: e.memset(Cf[d_][:], 0.0)), writes=[t_Cf[d_]])
        def m_front(c):
            kk = c % 2
            chs = (c, 63 - c)
            for d_ in range(2):
                ch = chs[d_]
                P.dma("sp", fmB[d_][kk][:], fm_s[ch], f"fl{d_}{kk}", reads=[t_fm[ch]], writes=[t_fmB[d_][kk]])
                P.dma("sp", tmB[d_][kk][:], tm_s[ch], f"tl{d_}{kk}", reads=[t_tm[ch]], writes=[t_tmB[d_][kk]])
            for d_ in range(2):
                ch = chs[d_]
                fmv = fmB[d_][kk][:].rearrange("p (a h t) -> p a h t", a=3, h=2)
                for hf in range(2):
                    P.op("pe", (lambda e, d_=d_, hf=hf, fmv=fmv: e.matmul(pb[0][:, d_ * 128:(d_ + 1) * 128], fmv[:, 2, hf, :], fmv[:, d_, hf, :],
                                                                         start=(hf == 0), stop=(hf == 1))),
                         reads=[t_fmB[d_][kk]], writes=[t_pb[0]])
                P.op("dve", (lambda e, d_=d_, ch=ch, kk=kk: e.scalar_tensor_tensor(wT[d_][kk][:], pb[0][:, d_ * 128:(d_ + 1) * 128], EA[:, ch, d_:d_ + 1], trs[:, d_, :],
                                                                           ALU.mult, ALU.mult)),
                     reads=[t_pb[0], t_EA[ch], t_tri], writes=[t_wT[d_][kk]])
            for d_ in range(2):
                ch = chs[d_]
                tmv = tmB[d_][kk]
                for hf in range(2):
                    P.op("pe", (lambda e, d_=d_, hf=hf, tmv=tmv: e.matmul(pb[3 + hf][:, 0:257], tmv[:, d_ * 256 + hf * 128:d_ * 256 + (hf + 1) * 128],
                                                                         tmv[:, 512:769], start=True, stop=True)),
                         reads=[t_tmB[d_][kk]], writes=[t_pb[3 + hf]])
                    P.op("act", (lambda e, d_=d_, hf=hf, ch=ch, kk=kk: e.activation(ctmp[d_][kk][:, hf, :], pb[3 + hf][:, 0:257], AF.Copy, scale=EG[:, ch, d_:d_ + 1])),
                         reads=[t_pb[3 + hf], t_EG[ch]], writes=[t_ctmp[d_][kk]])

        def m_back(c):
            kk = c % 2
            chs = (c, 63 - c)
            for d_ in range(2):
                ch = chs[d_]
                fmv = fmB[d_][kk][:].rearrange("p (a h t) -> p a h t", a=3, h=2)
                tmv = tmB[d_][kk]
                if c > 0:
                    for hf in range(2):
                        P.op("pe", (lambda e, d_=d_, hf=hf, fmv=fmv: e.matmul(pb[1 + d_][:, 0:257], fmv[:, d_, hf, :], Cb[d_][:, hf, :],
                                                                             start=(hf == 0), stop=False)),
                             reads=[t_fmB[d_][kk], t_Cb[d_]], writes=[t_pb[1 + d_]])
                P.op("pe", (lambda e, d_=d_, tmv=tmv, c=c, kk=kk: e.matmul(pb[1 + d_][:, 0:257], wT[d_][kk][:], tmv[:, 512:769], start=(c == 0), stop=True)),
                     reads=[t_wT[d_][kk], t_tmB[d_][kk]], writes=[t_pb[1 + d_]])
                P.op("act", (lambda e, d_=d_: e.activation(dn[d_][:, 0:1], pb[1 + d_][:, 256:257], AF.Abs)), reads=[t_pb[1 + d_]], writes=[t_dn[d_]])
                P.op("dve", (lambda e, d_=d_: e.tensor_scalar(dn[d_][:, 1:2], dn[d_][:, 0:1], 1.0, None, ALU.max)), reads=[t_dn[d_]], writes=[t_dn[d_]])
                P.op("dve", (lambda e, d_=d_: e.reciprocal(dn[d_][:, 2:3], dn[d_][:, 1:2])), reads=[t_dn[d_]], writes=[t_dn[d_]])
                P.op("dve", (lambda e, d_=d_, kk=kk: e.tensor_scalar(hst[d_][kk][:], pb[1 + d_][:, 0:256], dn[d_][:, 2:3], None, ALU.mult)),
                     reads=[t_pb[1 + d_], t_dn[d_]], writes=[t_hst[d_][kk]])
                P.dma("sp", h_s[d_][ch], hst[d_][kk][:], f"ho{d_}{kk}", reads=[t_hst[d_][kk]], writes=[t_hs[d_][ch]])
                P.op("dve", (lambda e, d_=d_, ch=ch, kk=kk: e.scalar_tensor_tensor(Cf[d_][:], Cf[d_][:], EG[:, ch, d_:d_ + 1], ctmp[d_][kk][:], ALU.mult, ALU.add)),
                     reads=[t_EG[ch], t_ctmp[d_][kk]], writes=[t_Cf[d_]])
                P.op("dve", (lambda e, d_=d_: e.tensor_copy(Cb[d_][:], Cf[d_][:])), reads=[t_Cf[d_]], writes=[t_Cb[d_]])

        hfb = P.sb([128, 2, 4, 256], F32); t_hfb = Tok()
        gfb = P.sb([128, 4, 512], F32); t_gfb = Tok()
        hsum = P.sb([128, 4, 256], F32); t_hsum = Tok()
        junk = P.sb([128, 256], F32); t_junk = Tok()
        fst = P.sb([128, 12], F32); t_fst = Tok()
        mo = P.sb([128, 4, 256], BF16); t_mo = Tok()
        moT = P.sb([128, 2, 4, 128], BF16); t_moT = Tok()

        def m_final4(g):
            c0 = 4 * g
            for d_ in range(2):
                P.dma("sp", hfb[:, d_, :, :], h_s[d_][c0:c0 + 4].rearrange("j p f -> p j f"), f"hl{d_}",
                      reads=[t_hs[d_][c0 + j] for j in range(4)], writes=[t_hfb])
            P.dma("sp", gfb[:], g_s[c0:c0 + 4].rearrange("j p f -> p j f")[:, :, 0:512], "gl0", reads=[t_gs[c0 + j] for j in range(4)], writes=[t_gfb])
            P.op("dve", (lambda e: e.tensor_tensor(hsum[:], hfb[:, 0, :, :], hfb[:, 1, :, :], ALU.add)), reads=[t_hfb], writes=[t_hsum])
            for j in range(4):
                P.op("act", (lambda e, j=j: e.activation(junk[:], hsum[:, j, :], AF.Square, accum_out=fst[:, j:j + 1])),
                     reads=[t_hsum], writes=[t_junk, t_fst])
            P.op("act", (lambda e: e.activation(fst[:, 4:8], fst[:, 0:4], AF.Ln, bias=epsb[:, 0:1], scale=1.0 / 256)), reads=[t_fst, t_eps], writes=[t_fst])
            P.op("act", (lambda e: e.activation(fst[:, 8:12], fst[:, 4:8], AF.Exp, scale=-0.5)), reads=[t_fst], writes=[t_fst])
            for j in range(4):
                P.op("dve", (lambda e, j=j: e.scalar_tensor_tensor(hsum[:, j, :], hsum[:, j, :], fst[:, 8 + j:9 + j], mlgb[:], ALU.mult, ALU.mult)),
                     reads=[t_fst, t_mlg], writes=[t_hsum])
            P.op("dve", (lambda e: e.tensor_tensor(hsum[:], hsum[:], gfb[:, :, 0:256], ALU.mult)), reads=[t_gfb], writes=[t_hsum])
            P.op("dve", (lambda e: e.tensor_tensor(mo[:], hsum[:], gfb[:, :, 256:512], ALU.mult)), reads=[t_gfb, t_hsum], writes=[t_mo])
            for hf in range(2):
                for j in range(4):
                    P.op("pe", (lambda e, hf=hf, j=j: e.transpose(pX[:, (hf * 4 + j) * 128:(hf * 4 + j + 1) * 128], mo[:, j, hf * 128:(hf + 1) * 128], idb[:])),
                         reads=[t_mo, t_id], writes=[t_pX])
            P.op("act", (lambda e: e.activation(moT[:].rearrange("p a j t -> p (a j t)"), pX[:, 0:1024], AF.Copy)), reads=[t_pX], writes=[t_moT])
            p_, q_ = c0 // 8, (c0 % 8) * 128
            P.dma("sp", mp[p_].rearrange("(a p) t -> p a t", p=128)[:, 0:2, q_:q_ + 512], moT[:].rearrange("p a j t -> p a (j t)"), "mo0",
                  reads=[t_moT], writes=[t_mpp[p_][(c0 % 8) + j] for j in range(4)])

    if True:
        Et = P.sb([128, 5, 20, 128], BF16); t_E = [Tok() for _ in range(5)]
        btmp = P.sb([128, 20, 128], F32); t_btmp = Tok()
        mtmp = P.sb([128, 5, 128], F32); t_mtmp = Tok()
        for cf in range(5):
            P.dma("sp", btmp[:], nab[cf], "bl", writes=[t_btmp])
            P.dma("sp", mtmp[:], nam[cf], "ml", writes=[t_mtmp])
            P.op("act", (lambda e: e.activation(btmp[:], btmp[:], AF.Exp)), writes=[t_btmp])
            for h in range(4):
                P.op("dve", (lambda e, cf=cf, h=h: e.tensor_tensor(Et[:, cf, h * 5:(h + 1) * 5, :], btmp[:, h * 5:(h + 1) * 5, :], mtmp[:], ALU.mult)),
                     reads=[t_btmp, t_mtmp], writes=[t_E[cf]])
        qn = [P.sb([128, 2, 128], BF16) for _ in range(2)]; t_qn = [Tok(), Tok()]
        kn_ = [P.sb([128, 2, 640], BF16) for _ in range(2)]; t_kn = [Tok(), Tok()]
        vn = [P.sb([128, 5, 260], BF16) for _ in range(2)]; t_vn = [Tok(), Tok()]
        gzn = [P.sb([128, 256], F32) for _ in range(2)]; t_gzn = [Tok(), Tok()]
        pp = [P.sb([128, 5, 128], BF16) for _ in range(3)]; t_pp = [Tok() for _ in range(3)]
        rdn = P.sb([128, 4], F32); t_rdn = Tok()
        no = [P.sb([128, 256], BF16) for _ in range(2)]; t_no = [Tok(), Tok()]
        noT = [P.sb([128, 2, 128], BF16) for _ in range(2)]; t_noT = [Tok(), Tok()]
        vna_v = vna_s.rearrange("n p f -> p n f")
        cnt = [0]

        def n_tile(i):
            k = i % 2
            cf, r0, nb = na_tile_cfg(i)
            c0 = r0 // 2
            nk = 512 if nb == 4 else 576
            P.dma("sp", qn[k][:], na_sv[:, 0:2, i * 128:(i + 1) * 128], f"nq{k}", reads=[t_nas[i]], writes=[t_qn[k]])
            P.dma("sp", kn_[k][:, :, 0:nk], na_sv[:, 2:4, r0 * 64:r0 * 64 + nk], f"nk{k}",
                  reads=[t_nas[cc] for cc in range(c0, c0 + nb)], writes=[t_kn[k]])
            P.dma("sp", vn[k][:, 0:nb, :], vna_v[:, c0:c0 + nb, :], f"nv{k}", reads=[t_vna[cc] for cc in range(c0, c0 + nb)], writes=[t_vn[k]])
            P.dma("sp", gzn[k][:], g_s[i][:, 512:768], f"ng{k}", reads=[t_gs[i]], writes=[t_gzn[k]])
            ob = 6
            sbanks = (5, 7)

            def s_mm(h):
                pr, off = h // 2, (h % 2) * 64
                sb_ = sbanks[h % 2]
                for bl in range(nb):
                    kn = 128 if bl < 4 else 64
                    if bl < 4:
                        dst, tk = pb[sb_][0:kn, bl * 128:(bl + 1) * 128], t_pb[sb_]
                    else:
                        dst, tk = pb[0][0:kn, 256 + 128 * (h % 2):384 + 128 * (h % 2)], t_pb[0]
                    P.op("pe", (lambda e, k=k, pr=pr, off=off, bl=bl, kn=kn, dst=dst: e.matmul(
                        dst, kn_[k][off:off + 64, pr, bl * 128:bl * 128 + kn], qn[k][off:off + 64, pr, :],
                        start=True, stop=True)), reads=[t_kn[k], t_qn[k]], writes=[tk])
            s_mm(0)
            for h in range(4):
                sb_ = sbanks[h % 2]
                pk = cnt[0] % 3
                cnt[0] += 1
                if h + 1 < 4:
                    s_mm(h + 1)
                P.op("act", (lambda e, pk=pk, sb_=sb_: e.activation(pp[pk][:, 0:4, :], pb[sb_][:].rearrange("p (b q) -> p b q", b=4), AF.Exp, scale=0.125)),
                     reads=[t_pb[sb_]], writes=[t_pp[pk]])
                if nb == 5:
                    P.op("act", (lambda e, pk=pk, h=h: e.activation(pp[pk][0:64, 4, :], pb[0][0:64, 256 + 128 * (h % 2):384 + 128 * (h % 2)], AF.Exp, scale=0.125)),
                         reads=[t_pb[0]], writes=[t_pp[pk]])
                P.op("dve", (lambda e, pk=pk, cf=cf, h=h: e.tensor_tensor(pp[pk][:, 0:4, :], pp[pk][:, 0:4, :], Et[:, cf, h * 5:h * 5 + 4, :], ALU.mult)),
                     reads=[t_E[cf]], writes=[t_pp[pk]])
                if nb == 5:
                    P.op("dve", (lambda e, pk=pk, cf=cf, h=h: e.tensor_tensor(pp[pk][0:64, 4, :], pp[pk][0:64, 4, :], Et[0:64, cf, h * 5 + 4, :], ALU.mult)),
                         reads=[t_E[cf]], writes=[t_pp[pk]])
                for bl in range(nb):
                    kn = 128 if bl < 4 else 64
                    P.op("pe", (lambda e, k=k, pk=pk, h=h, bl=bl, kn=kn, ob=ob: e.matmul(
                        pb[ob][:, h * 65:(h + 1) * 65], pp[pk][0:kn, bl, :], vn[k][0:kn, bl, h * 65:(h + 1) * 65],
                        start=(bl == 0), stop=(bl == nb - 1))), reads=[t_pp[pk], t_vn[k]], writes=[t_pb[ob]])
            ov = pb[ob][:, 0:260].rearrange("p (h d) -> p h d", h=4)
            P.op("dve", (lambda e, ov=ov: e.reciprocal(rdn[:], ov[:, :, 64])), reads=[t_pb[ob]], writes=[t_rdn])
            for h in range(4):
                P.op("dve", (lambda e, k=k, h=h, ov=ov: e.scalar_tensor_tensor(no[k][:, h * 64:(h + 1) * 64], ov[:, h, 0:64], rdn[:, h:h + 1],
                                                                             gzn[k][:, h * 64:(h + 1) * 64], ALU.mult, ALU.mult)),
                     reads=[t_pb[ob], t_rdn, t_gzn[k]], writes=[t_no[k]])
            for hf in range(2):
                P.op("pe", (lambda e, k=k, hf=hf: e.transpose(pX[:, hf * 128:(hf + 1) * 128], no[k][:, hf * 128:(hf + 1) * 128], idb[:])),
                     reads=[t_no[k], t_id], writes=[t_pX])
            P.op("act", (lambda e, k=k: e.activation(noT[k][:], pX[:, 0:256].rearrange("p (a t) -> p a t", a=2), AF.Copy)),
                 reads=[t_pX], writes=[t_noT[k]])
            P.dma("sp", mp[i // 8].rearrange("(a p) t -> p a t", p=128)[:, 2:4, (i % 8) * 128:(i % 8 + 1) * 128], noT[k][:], f"nn{k}",
                  reads=[t_noT[k]], writes=[t_mpp[i // 8][8 + i % 8]])


    def ag(p):
        P.coll("AllGather", GROUPS, mp[p], mg[p], f"cc{p % 4}", reads=t_mpp[p], writes=[io["t_mg"][p]])
    m_front(0)
    for c in range(64):
        if c + 1 < 64:
            m_front(c + 1)
        m_back(c)
        n_tile(c)
        if c >= 35 and c % 4 == 3:
            m_final4((c - 3) // 4)
            m_final4((63 - c) // 4)
            if c % 8 == 7:
                q = (c - 39) // 8
                ag(3 - q)
                ag(4 + q)


def na_bias_tables(rpb4):
    kp = np.arange(128)
    q = np.arange(128)
    bias = np.zeros((5, 128, 20, 128), np.float32)
    mask = np.zeros((5, 128, 5, 128), np.float32)
    for ci, i in enumerate(NA_CFG_TILES):
        _, r0k, nb = na_tile_cfg(i)
        qr = 2 * i + q // 64
        qc = q % 64
        rr0 = np.clip(qr - 4, 0, 120)
        cc0 = np.clip(qc - 8, 0, 48)
        for bl in range(nb):
            npart = 128 if bl < 4 else 64
            kr = r0k + bl * 2 + kp // 64
            kc = kp % 64
            valid = ((kr[:, None] >= rr0[None, :]) & (kr[:, None] < rr0[None, :] + 8) &
                     (kc[:, None] >= cc0[None, :]) & (kc[:, None] < cc0[None, :] + 16) & (kp[:, None] < npart))
            dr = np.clip(kr[:, None] - qr[None, :] + 7, 0, 14)
            dc = np.clip(kc[:, None] - qc[None, :] + 15, 0, 30)
            mask[ci, :, bl, :] = valid
            for h in range(4):
                bias[ci, :, h * 5 + bl, :] = rpb4[h][dr, dc]
    return bias, mask


def emit_outproj(nc, P, PSB, t_PSB, io, final):
    mg, t_mg, wout = io["mg"], io["t_mg"], io["wout"]
    m0 = P.mark()
    wb = P.sb([128, 16, D], BF16); t_wb = Tok()
    _tw = []
    for c in range(16):
        r, q = c // 4, c % 4
        row0 = (r * 256 + q * 128) if q < 2 else (1024 + r * 256 + (q - 2) * 128)
        _tw.append(Tok())
        P.dma("pool", wb[:, c, :], wout[row0:row0 + 128, :], "wb", writes=[_tw[-1]])
    w_join(P, _tw, t_wb)
    NBUF = 4
    mt = [P.sb([128, 16, 256], BF16) for _ in range(NBUF)]; t_mt = [Tok() for _ in range(NBUF)]
    xr = [P.sb([128, 8, 256], F32) for _ in range(NBUF)]; t_xr = [Tok() for _ in range(NBUF)]
    hT = [P.sb([128, 8, 256], F32) for _ in range(2)]; t_hT = [Tok(), Tok()]
    if final:
        ones_bf = P.sb([128, 128], BF16); t_ones = Tok()
        P.op("pool", lambda e: e.memset(ones_bf[:], 1.0), writes=[t_ones])
        epsb = P.sb([128, 1], F32); t_eps = Tok()
        P.op("pool", lambda e: e.memset(epsb[:], EPS), writes=[t_eps])
        gfs = P.sb([128, 8], F32); t_gf = Tok()
        P.dma("sp", gfs[:], io["gfin"][:, :], "c_ng", writes=[t_gf])
        xsq = P.sb([128, 8, 256], BF16); t_xsq = Tok()
        lnv = P.sb([128, 256], F32); t_lnv = Tok()
        rstd = P.sb([128, 256], F32); t_rstd = Tok()
        ob = [P.sb([128, 8, 256], F32) for _ in range(2)]; t_ob = [Tok(), Tok()]
    order = io.get("order", list(range(8)))
    for n_, i in enumerate(order):
        k = n_ % 2
        kl = n_ % NBUF

        def ld(e, i=i, k=kl):
            if "rank" not in P.__dict__:
                P.rank = e.partition_id() % 4
            r = P.rank
            return e.dma_start(out=mt[k][:], in_=mg[i].rearrange("(c p) t -> p c t", p=128)[:, :, bass.ts(r, 256)])
        P.dma_fn("sp", ld, f"ml{kl}", reads=[t_mg[i]], writes=[t_mt[kl]])
        if "xres" in io:
            P.dma("sp", xr[kl][:], io["xres"][i].rearrange("(c p) t -> p c t", p=128), f"xr{kl}", writes=[t_xr[kl]])
        else:
            P.dma("sp", xr[kl][:], io["hp_in"][i].rearrange("(c p) t -> p c t", p=128), f"xr{kl}", reads=[io["t_hp_in"][i]], writes=[t_xr[kl]])
        for n in range(8):
            bank = 1 + n % 4
            for c in range(16):
                P.op("pe", (lambda e, c=c, n=n, kl=kl, bank=bank: e.matmul(PSB[bank][:, 0:256], wb[:, c, n * 128:(n + 1) * 128], mt[kl][:, c, :],
                                                                      start=(c == 0), stop=(c == 15))),
                     reads=[t_wb, t_mt[kl]], writes=[t_PSB[bank]])
            P.op("dve", (lambda e, n=n, k=k, kl=kl, bank=bank: e.tensor_tensor(hT[k][:, n, :], PSB[bank][:, 0:256], xr[kl][:, n, :], ALU.add)),
                 reads=[t_PSB[bank], t_xr[kl]], writes=[t_hT[k]])
        if not final:
            hp, t_hp, hgl, t_hg = io["hp"], io["t_hp"], io["hg"], io["t_hg"]
            P.dma("sp", hp[i].rearrange("(c p) t -> p c t", p=128), hT[k][:], f"ho{k}", reads=[t_hT[k]], writes=[t_hp[i]])
            P.coll("AllGather", GROUPS, hp[i], hgl[i], f"cc{4 + i % 4}", reads=[t_hp[i]], writes=[t_hg[i]])
        else:
            P.op("act", (lambda e, k=k: e.activation(xsq[:], hT[k][:], AF.Square)), reads=[t_hT[k]], writes=[t_xsq])
            for c in range(8):
                P.op("pe", (lambda e, c=c: e.matmul(PSB[0][:, 0:256], ones_bf[:], xsq[:, c, :], start=(c == 0), stop=(c == 7))),
                     reads=[t_ones, t_xsq], writes=[t_PSB[0]])
            P.op("act", (lambda e: e.activation(lnv[:], PSB[0][:, 0:256], AF.Ln, bias=epsb[:, 0:1], scale=1.0 / D)),
                 reads=[t_PSB[0], t_eps], writes=[t_lnv])
            P.op("act", (lambda e: e.activation(rstd[:], lnv[:], AF.Exp, scale=-0.5)), reads=[t_lnv], writes=[t_rstd])
            for c in range(8):
                P.op("dve", (lambda e, c=c, k=k: e.scalar_tensor_tensor(ob[k][:, c, :], hT[k][:, c, :], gfs[:, c:c + 1], rstd[:], ALU.mult, ALU.mult)),
                     reads=[t_hT[k], t_gf, t_rstd], writes=[t_ob[k]])
            P.dma("sp", io["outT"][i].rearrange("(c p) t -> p c t", p=128), ob[k][:], f"oo{k}", reads=[t_ob[k]])
    P.barrier()
    P.release(m0)


def build_fused(nstages=4, a_kw=None, c_kw=None):
    nc = bass.Bass("TRN2", target_bir_lowering=False)
    din = lambda n, s_, d=F32: nc.dram_tensor(n, s_, d, kind="ExternalInput").ap()
    dint = lambda n, s_, d: nc.dram_tensor(n, s_, d, kind="Internal").ap()
    a_hT = din("a_hT", [D, S]); a_w = din("a_w", [D, NWA]); a_ng = din("a_ng", [128, 8])
    gateb = din("gateb", [1, 4]); mlg = din("mlg", [1, 256]); tri = din("tri", [2, 128, 128])
    nab = din("nab", [5, 128, 20, 128]); nam = din("nam", [5, 128, 5, 128]); ident = din("ident", [128, 128], BF16)
    b_wout = din("b_wout", [2048, D]); b_xres = din("b_xres", [8, D, 256])
    c_w = din("c_w", [D, NWC]); c_ng = din("c_ng", [128, 8]); qkg = din("qkg", [2, 128]); cs = din("cs", [S, 128]); cwb = din("cwb", [128, 8])
    d_wout = din("d_wout", [2048, D]); gfin = din("gfin", [128, 8])
    outT = nc.dram_tensor("outT", [8, D, 256], F32, kind="ExternalOutput").ap()
    mp0 = [dint(f"mp0_{i}", [512, 1024], BF16) for i in range(8)]
    mg0 = [dint(f"mg0_{i}", [2048, 1024], BF16) for i in range(8)]
    mp1 = [dint(f"mp1_{i}", [512, 1024], BF16) for i in range(8)]
    mg1 = [dint(f"mg1_{i}", [2048, 1024], BF16) for i in range(8)]
    hp = [dint(f"hp_{i}", [D, 256], F32) for i in range(8)]
    hgl = [dint(f"hg_{i}", [4 * D, 256], F32) for i in range(8)]
    t_mg0 = [Tok() for _ in range(8)]; t_mg1 = [Tok() for _ in range(8)]
    t_hp = [Tok() for _ in range(8)]; t_hg = [Tok() for _ in range(8)]

    P = Prog(nc)
    PSB = [P.ps([128, 512]) for _ in range(8)]
    t_PSB = [XTok() for _ in range(8)]
    PX = PSB[7][:].bitcast(BF16); t_PX = t_PSB[7]

    m = P.mark()
    emit_mixer0(nc, P, PSB, t_PSB, PX, t_PX, dict(hT=a_hT, w=a_w, ng=a_ng, gateb=gateb, mlg=mlg, tri=tri, nab=nab, nam=nam, ident=ident,
                                                 mp=mp0, mg=mg0, t_mg=t_mg0), **(a_kw or {}))
    P.barrier(); P.release(m)
    if nstages >= 2:
        emit_outproj(nc, P, PSB, t_PSB, dict(mg=mg0, t_mg=t_mg0, wout=b_wout, xres=b_xres, hp=hp, t_hp=t_hp, hg=hgl, t_hg=t_hg,
                                             order=[3, 4, 2, 5, 1, 6, 0, 7]), final=False)
    m = P.mark()
    if nstages >= 3:
        emit_mixer1(nc, P, PSB, t_PSB, PX, t_PX, dict(hg=hgl, t_hg=t_hg, w=c_w, ng=c_ng, qkg=qkg, cs=cs, cwb=cwb, ident=ident,
                                                     mp=mp1, mg=mg1, t_mg=t_mg1, order=[3, 4, 2, 5, 1, 6, 0, 7]), **(c_kw or {}))
    P.barrier(); P.release(m)
    if nstages >= 4:
        emit_outproj(nc, P, PSB, t_PSB, dict(mg=mg1, t_mg=t_mg1, wout=d_wout, hp_in=hp, t_hp_in=t_hp, gfin=gfin, outT=outT), final=True)
    P.build()
    return nc


def kernel(x, norm_g, final_g, ev_w_in, ev_gate_b, ev_w_out, ev_ml_norm_g, ev_na_rpb,
           od_w_in, od_w_out, od_q_norm_g, od_k_norm_g, od_conv_w, od_conv_b):
    f = lambda a: np.asarray(a, dtype=np.float32)
    x, norm_g, final_g = f(x), f(norm_g), f(final_g)
    ev_w_in, ev_gate_b, ev_w_out, ev_ml_norm_g, ev_na_rpb = f(ev_w_in)[0], f(ev_gate_b)[0], f(ev_w_out)[0], f(ev_ml_norm_g)[0], f(ev_na_rpb)[0]
    od_w_in, od_w_out, qg, kg, conv_w, conv_b = f(od_w_in)[0], f(od_w_out)[0], f(od_q_norm_g)[0], f(od_k_norm_g)[0], f(od_conv_w)[0], f(od_conv_b)[0]
    nc = build_fused()
    ident = np.eye(128, dtype=np.float32).astype(ml_dtypes.bfloat16)
    s_ = np.arange(128)
    tri = np.stack([(s_[:, None] <= s_[None, :]), (s_[:, None] >= s_[None, :])]).astype(np.float32)
    cs = rope_tables()
    xT = [np.ascontiguousarray(x[b].T) for b in range(2)]
    r256 = np.arange(256)
    in_maps = []
    for core in range(NCORES):
        b, hg = core // 4, core % 4
        cols0 = np.concatenate([
            0 + hg * 256 + r256, 1024 + hg * 256 + r256, 2048 + hg * 256 + r256, 3072 + hg * 256 + r256,
            4096 + hg * 256 + r256, 5136 + 3072 + hg * 256 + r256, 5136 + hg * 256 + r256, 5136 + 1024 + hg * 256 + r256,
            5136 + 2048 + hg * 256 + r256, 5120 + np.array([hg, 4 + hg, 8 + hg, 12 + hg])])
        gb = ev_gate_b[np.array([hg, 4 + hg, 8 + hg, 12 + hg])].reshape(1, 4)
        bias, mask = na_bias_tables(ev_na_rpb[4 * hg:4 * hg + 4])
        kvh = hg // 2
        cols1 = np.concatenate([
            hg * 256 + r256, 1024 + kvh * 128 + np.arange(128), 1280 + kvh * 128 + np.arange(128),
            1536 + hg * 256 + r256, 2560 + hg * 256 + r256, 3584 + hg * 256 + r256, 4608 + hg * 256 + r256, 5632 + hg * 256 + r256])
        cwb = np.zeros((128, 8), np.float32)
        for cc in range(2):
            ch = hg * 256 + cc * 128 + np.arange(128)
            for i in range(3):
                cwb[:, cc * 4 + i] = conv_w[i, ch]
            cwb[:, cc * 4 + 3] = conv_b[ch]
        xres = np.stack([xT[b][:, (4 * i + hg) * 256:(4 * i + hg + 1) * 256] for i in range(8)])
        in_maps.append({
            "a_hT": xT[b], "a_w": np.ascontiguousarray(ev_w_in[:, cols0]), "a_ng": np.ascontiguousarray(norm_g[0].reshape(8, 128).T),
            "gateb": np.ascontiguousarray(gb), "mlg": np.ascontiguousarray(ev_ml_norm_g[hg * 256:(hg + 1) * 256].reshape(1, 256)),
            "tri": tri, "nab": bias, "nam": mask, "ident": ident,
            "b_wout": np.ascontiguousarray(ev_w_out), "b_xres": np.ascontiguousarray(xres),
            "c_w": np.ascontiguousarray(od_w_in[:, cols1]), "c_ng": np.ascontiguousarray(norm_g[1].reshape(8, 128).T),
            "qkg": np.stack([qg, kg]), "cs": cs, "cwb": cwb,
            "d_wout": np.ascontiguousarray(od_w_out), "gfin": np.ascontiguousarray(final_g.reshape(8, 128).T)})
    res = run_bass_kernel_spmd(nc, in_maps, core_ids=list(range(NCORES)))
    out = np.empty((2, S, D), np.float32)
    for core in range(NCORES):
        b, tq = core // 4, core % 4
        o = res.results[core]["outT"]
        for i in range(8):
            out[b, (4 * i + tq) * 256:(4 * i + tq + 1) * 256, :] = o[i].T
    return out
```

```python
import numpy as np
import ml_dtypes
from contextlib import ExitStack
import concourse.bass as bass
import concourse.mybir as mybir
from concourse.bass_utils import run_bass_kernel_spmd

F32 = mybir.dt.float32
BF16 = mybir.dt.bfloat16
AF = mybir.ActivationFunctionType
ALU = mybir.AluOpType
AX = mybir.AxisListType

S = 8192
D = 1024
NCORES = 8
GROUPS = [[0, 1, 2, 3], [4, 5, 6, 7]]
EPS = 1e-6


class Tok:
    __slots__ = ("w", "r", "x")

    def __init__(self, x=False):
        self.w = None
        self.r = {}
        self.x = x


def XTok():
    return Tok(True)


class Prog:
    ENG = ("pe", "act", "dve", "pool", "sp")

    def __init__(self, nc):
        self.nc = nc
        self.ins = {e: [] for e in self.ENG}
        self.seen = {e: {} for e in self.ENG}
        self.dma_cnt = {}
        self.stack = ExitStack()
        self._n = 0

    SB_BASE = 16512
    SB_TOP = 229376

    def sb(self, shape, dt, name=None):
        self._n += 1
        nb = int(np.prod(shape[1:])) * (4 if dt == F32 else 2)
        nb = (nb + 31) // 32 * 32
        off = getattr(self, "sbp", self.SB_BASE)
        assert off + nb <= self.SB_TOP, f"SBUF overflow: {off + nb}"
        self.sbp = off + nb
        return self.nc.alloc_sbuf_tensor_at(name or f"sb{self._n}", list(shape), dt, offset=off)

    def mark(self):
        return getattr(self, "sbp", self.SB_BASE)

    def release(self, m):
        self.sbp = m

    def ps(self, shape, dt=F32, name=None):
        self._n += 1
        return self.stack.enter_context(self.nc.psum_tensor(name or f"ps{self._n}", list(shape), dt))

    def _deps(self, eng, reads, writes):
        deps = {}
        def add(ev):
            key = ev[1]
            if key not in deps or deps[key][2] < ev[2]:
                deps[key] = ev
        for t in reads:
            if t.w is not None:
                add(t.w)
        for t in writes:
            if t.w is not None:
                add(t.w)
            for ev in t.r.values():
                add(ev)
        waits = []
        seen = self.seen[eng]
        for key, ev in deps.items():
            if ev[0] == "c" and key == eng and eng == "pe":
                continue
            if seen.get(key, -1) >= ev[2]:
                continue
            seen[key] = ev[2]
            waits.append(ev)
        return waits

    def op(self, eng, fn, reads=(), writes=()):
        xs = [t for t in reads if t.x]
        if xs:
            reads = [t for t in reads if not t.x]
            writes = list(writes) + xs
        waits = self._deps(eng, reads, writes)
        idx = len(self.ins[eng])
        ev = ("c", eng, idx)
        for t in reads:
            t.r[eng] = ev
        for t in writes:
            t.w = ev
            t.r = {}
        self.ins[eng].append([fn, waits, False, None])

    def _alias(self, sem):
        al = self.__dict__.setdefault("sem_alias", {})
        if sem not in al:
            al[sem] = f"d{len(al)}"
        return al[sem]

    def dma(self, eng, out, in_, sem, reads=(), writes=(), **kw):
        if eng == "sp" and type(out.tensor).__name__ == "DRamTensorHandle":
            eng = "pool"
        sem = self._alias(sem)
        waits = self._deps(eng, reads, writes)
        self.dma_cnt[sem] = self.dma_cnt.get(sem, 0) + 16
        ev = ("d", sem, self.dma_cnt[sem])
        for t in reads:
            t.r[sem] = ev
        for t in writes:
            t.w = ev
            t.r = {}
        fn = lambda e: e.dma_start(out=out, in_=in_, **kw)
        self.ins[eng].append([fn, waits, None, sem])

    def dma_fn(self, eng, fn, sem, reads=(), writes=()):
        sem = self._alias(sem)
        waits = self._deps(eng, reads, writes)
        self.dma_cnt[sem] = self.dma_cnt.get(sem, 0) + 16
        ev = ("d", sem, self.dma_cnt[sem])
        for t in reads:
            t.r[sem] = ev
        for t in writes:
            t.w = ev
            t.r = {}
        self.ins[eng].append([fn, waits, None, sem])

    def coll(self, kind, groups, in_ap, out_ap, sem, reads=(), writes=()):
        eng = "pool"
        sem = "coll_" + sem
        self.__dict__.setdefault("coll_sems", set()).add(sem)
        waits = self._deps(eng, reads, writes)
        self.dma_cnt[sem] = self.dma_cnt.get(sem, 0) + 1
        ev = ("d", sem, self.dma_cnt[sem])
        for t in reads:
            t.r[sem] = ev
        for t in writes:
            t.w = ev
            t.r = {}
        fn = lambda e: e.collective_compute(kind, ALU.bypass, replica_groups=groups, ins=[in_ap.opt()], outs=[out_ap.opt()])
        self.ins[eng].append([fn, waits, None, (sem, 1)])

    def barrier(self):
        last = {}
        for e in self.ENG:
            for i in range(len(self.ins[e]) - 1, -1, -1):
                if self.ins[e][i][0] is not None and self.ins[e][i][3] is None:
                    last[e] = i
                    break
        for e in self.ENG:
            waits = []
            seen = self.seen[e]
            for src, idx in last.items():
                if src == e or seen.get(src, -1) >= idx:
                    continue
                seen[src] = idx
                waits.append(("c", src, idx))
            for sem, val in self.dma_cnt.items():
                if sem in self.__dict__.get("coll_sems", ()):
                    continue
                if seen.get(sem, -1) >= val:
                    continue
                seen[sem] = val
                waits.append(("d", sem, val))
            self.ins[e].append([None, waits, False, None])
        self.sem_alias = {}

    def build(self):
        nc = self.nc
        for e in self.ENG:
            for rec in self.ins[e]:
                for ev in rec[1]:
                    if ev[0] == "c":
                        self.ins[ev[1]][ev[2]][2] = True
        semval = {}
        for e in self.ENG:
            c = 0
            for i, rec in enumerate(self.ins[e]):
                if rec[2]:
                    c += 1
                    semval[(e, i)] = c
        names = [e for e in self.ENG if e != "sp"] + sorted(self.dma_cnt)
        sems = {n: self.stack.enter_context(nc.semaphore("s_" + n)) for n in names}
        final_dma = dict(self.dma_cnt)

        def replay(ename, eobj, last=False):
            for i, (fn, waits, sig, dsem) in enumerate(self.ins[ename]):
                for ev in waits:
                    if ev[0] == "c":
                        eobj.wait_ge(sems[ev[1]], semval[(ev[1], ev[2])])
                    else:
                        eobj.wait_ge(sems[ev[1]], ev[2])
                if fn is None:
                    continue
                inst = fn(eobj)
                if isinstance(dsem, tuple):
                    inst.then_inc(sems[dsem[0]], dsem[1])
                elif dsem is not None:
                    inst.then_inc(sems[dsem], 16)
                elif sig:
                    inst.then_inc(sems[ename], 1)
            if last:
                for n, v in final_dma.items():
                    eobj.wait_ge(sems[n], v)

        with nc.Block() as block:
            @block.tensor
            def _(e):
                replay("pe", e)

            @block.scalar
            def _(e):
                replay("act", e)

            @block.vector
            def _(e):
                replay("dve", e)

            @block.gpsimd
            def _(e):
                replay("pool", e)

            @block.sync
            def _(e):
                replay("sp", e, last=True)
        self.stack.close()


def w_join(P, toks, t_out):
    wj = P.sb([128, 1], F32)
    P.op("dve", lambda e: e.memset(wj[:], 0.0), reads=toks, writes=[t_out])


def _bf(a):
    return np.ascontiguousarray(a)


class NormCtx:
    def __init__(self, P, hTv, ngs, t_ng, ones_bf, t_ones, epsb, t_eps, ps_ss, t_ps_ss, blk=512):
        self.P = P
        self.blk = blk
        self.hTv = hTv
        self.ngs, self.t_ng = ngs, t_ng
        self.ones_bf, self.t_ones = ones_bf, t_ones
        self.epsb, self.t_eps = epsb, t_eps
        self.ps_ss, self.t_ps_ss = ps_ss, t_ps_ss
        self.xt = [P.sb([128, 8, blk], F32) for _ in range(2)]
        self.t_xt = [Tok(), Tok()]
        self.t_xt2 = [Tok(), Tok()]
        self.xsq = P.sb([128, 8, blk], BF16)
        self.t_xsq = Tok()
        self.lnv = P.sb([128, blk], F32)
        self.rstd = P.sb([128, blk], F32)
        self.t_lnv, self.t_rstd = Tok(), Tok()
        self.uT = [P.sb([128, 8, blk], BF16) for _ in range(2)]
        self.t_uT = [Tok(), Tok()]

    def load(self, j):
        k = j % 2
        if self.hTv is None:
            self.loader(j, self.xt[k], f"xt{k}", (self.t_xt[k], self.t_xt2[k]))
        else:
            self.P.dma("sp", self.xt[k][:], self.hTv[:, :, j * self.blk:(j + 1) * self.blk], f"xt{k}", writes=[self.t_xt[k]])

    def norm(self, j):
        P = self.P
        k = j % 2
        xt, xsq, uT = self.xt[k], self.xsq, self.uT[k]
        P.op("act", lambda e: e.activation(xsq[:], xt[:], AF.Square), reads=[self.t_xt[k], self.t_xt2[k]], writes=[self.t_xsq])
        for c in range(8):
            P.op("pe", (lambda e, c=c: e.matmul(self.ps_ss[:, 0:self.blk], self.ones_bf[:], xsq[:, c, :], start=(c == 0), stop=(c == 7))),
                 reads=[self.t_ones, self.t_xsq], writes=[self.t_ps_ss])
        P.op("act", lambda e: e.activation(self.lnv[:], self.ps_ss[:, 0:self.blk], AF.Ln, bias=self.epsb[:, 0:1], scale=1.0 / D),
             reads=[self.t_ps_ss, self.t_eps], writes=[self.t_lnv])
        P.op("act", lambda e: e.activation(self.rstd[:], self.lnv[:], AF.Exp, scale=-0.5), reads=[self.t_lnv], writes=[self.t_rstd])
        for c in range(8):
            P.op("dve", (lambda e, c=c: e.scalar_tensor_tensor(uT[:, c, :], xt[:, c, :], self.ngs[:, c:c + 1], self.rstd[:],
                                                              ALU.mult, ALU.mult)),
                 reads=[self.t_xt[k], self.t_xt2[k], self.t_ng, self.t_rstd], writes=[self.t_uT[k]])
        return uT, self.t_uT[k]


NWC = 1792


def emit_mixer1(nc, P, PSB, t_PSB, PXg, t_PXg, io, nblk=16, do_v=True, do_g=True, sub=99):
    hgl, t_hg, w, ng, qkg, cs, cwb, ident = (io[k] for k in ("hg", "t_hg", "w", "ng", "qkg", "cs", "cwb", "ident"))
    mp, mg = io["mp"], io["mg"]
    dint = lambda n, s_, d: nc.dram_tensor(n, s_, d, kind="Internal").ap()
    qT_s = dint("qT_s", [2, 128, S], BF16)
    z_s = dint("z_s", [S, 256], F32)
    xc_s = dint("xc_s", [256, S], F32)
    gz_s = dint("gz_s", [256, S], F32)
    t_mpp = [[Tok() for _ in range(6)] for _ in range(8)]

    wb = P.sb([128, 8, NWC], BF16)
    t_wb = Tok()
    wv = w.rearrange("(c p) n -> p c n", p=128)
    _tw = []
    for c in range(8):
        _tw.append(Tok())
        P.dma("pool", wb[:, c, :], wv[:, c, :], "wb", writes=[_tw[-1]])
    w_join(P, _tw, t_wb)
    ngs = P.sb([128, 8], F32); t_ng = Tok()
    P.dma("sp", ngs[:], ng[:, :], "c_ng", writes=[t_ng])
    qg = P.sb([128, 2, 128], F32); t_qg = Tok()
    P.dma("sp", qg[:, 0, :], qkg[0:1, :].to_broadcast([128, 128]), "c_qg", writes=[t_qg])
    P.dma("sp", qg[:, 1, :], qkg[1:2, :].to_broadcast([128, 128]), "c_qg", writes=[t_qg])
    P.op("dve", lambda e: e.tensor_scalar(qg[:, 0, :], qg[:, 0, :], float(128 ** -0.5), None, ALU.mult), reads=[t_qg], writes=[t_qg])
    cw = P.sb([128, 8], F32); t_cw = Tok()
    P.dma("sp", cw[:], cwb[:, :], "c_cw", writes=[t_cw])
    idb = P.sb([128, 128], BF16); t_id = Tok()
    P.dma("sp", idb[:], ident[:, :], "c_id", writes=[t_id])
    ones_bf = P.sb([128, 128], BF16); t_ones = Tok()
    P.op("pool", lambda e: e.memset(ones_bf[:], 1.0), writes=[t_ones])
    epsb = P.sb([128, 2], F32); t_eps = Tok()
    P.op("pool", lambda e: e.memset(epsb[:], EPS), writes=[t_eps])
    kT = P.sb([128, S], BF16)
    t_kT = [Tok() for _ in range(64)]
    vx = P.sb([128, 64, 129], BF16)
    t_vx = [Tok() for _ in range(64)]
    t_vones = Tok()
    P.op("pool", lambda e: e.memset(vx[:, :, 128:129], 1.0), writes=[t_vones])
    pT = [[PSB[1], PSB[2]], [PSB[3], PSB[4]]]
    t_pT = [[t_PSB[1], t_PSB[2]], [t_PSB[3], t_PSB[4]]]
    pB = [PSB[0], PSB[5], PSB[6]]
    t_pB = [t_PSB[0], t_PSB[5], t_PSB[6]]
    pX = PXg
    t_pX = [t_PXg, t_PXg]

    nctx = NormCtx(P, None, ngs, t_ng, ones_bf, t_ones, epsb, t_eps, pB[0], t_pB[0])

    def _loader(j, xt, sem, tok):
        i, r0 = j // 2, (j % 2) * 2
        hv = hgl[i].rearrange("(r c p) t -> p r c t", r=4, p=128)
        for rr in range(2):
            P.dma("sp", xt[:, :, rr * 256:(rr + 1) * 256], hv[:, r0 + rr, :, :], sem, reads=[t_hg[i]], writes=[tok[rr]])
    nctx.loader = _loader
    csb = [P.sb([128, 4, 128], F32) for _ in range(2)]; t_cs = [Tok(), Tok()]
    st = [P.sb([128, 12], F32) for _ in range(2)]; t_st = [Tok(), Tok()]
    junk = P.sb([128, 128], F32); t_junk = Tok()
    xn = [P.sb([128, 3, 128], F32) for _ in range(2)]; t_xn = [Tok(), Tok()]
    tmp = [P.sb([128, 3, 64], F32) for _ in range(4)]; t_tmp = [Tok() for _ in range(4)]
    rot = [P.sb([128, 3, 128], BF16) for _ in range(2)]; t_rot = [Tok(), Tok()]
    qTst = [P.sb([128, 2, 512], BF16) for _ in range(2)]; t_qTst = [Tok(), Tok()]
    zst = [P.sb([128, 4, 256], F32) for _ in range(2)]; t_zst = [Tok(), Tok()]
    ccs = P.sb([128, 512], F32); t_ccs = Tok()
    sis = P.sb([128, 512], F32); t_sis = Tok()
    xcst = [P.sb([128, 2, 512], F32) for _ in range(2)]; t_xcst = [Tok(), Tok()]
    gzst = [P.sb([128, 2, 512], F32) for _ in range(2)]; t_gzst = [Tok(), Tok()]
    t_qs = [Tok() for _ in range(16)]
    t_zs = [Tok() for _ in range(16)]
    t_xcs = [Tok() for _ in range(16)]
    t_gzs = [Tok() for _ in range(16)]
    csv = cs.rearrange("(n p) f -> p n f", p=128)
    z_sv = z_s.rearrange("(n p) f -> p n f", p=128)
    qT_sv = qT_s.rearrange("h d t -> d h t")
    xc_sv = xc_s.rearrange("(c p) t -> p c t", p=128)
    gz_sv = gz_s.rearrange("(c p) t -> p c t", p=128)

    def r4(ap):
        return ap.rearrange("p h (a r i) -> p h a r i", a=2, r=2)

    jorder = [2 * p + q for p in io.get("order", list(range(8))) for q in range(2)][:nblk] if nblk == 16 else list(range(nblk))
    nctx.load(jorder[0])
    pend_tr = None
    for jn, j in enumerate(jorder):
        kb = j % 2
        jnext = jorder[jn + 1] if jn + 1 < len(jorder) else None
        if jnext is not None:
            nctx.load(jnext)
        P.dma("sp", csb[kb][:], csv[:, 4 * j:4 * j + 4, :], f"cs{kb}", writes=[t_cs[kb]])
        if jn == 0:
            pend = nctx.norm(j)
        uT, t_uT = pend
        def fm(m, pi):
            for c in range(8):
                P.op("pe", (lambda e, c=c, m=m, pi=pi, uT=uT: e.matmul(pB[pi][:], wb[:, c, 768 + m * 128:768 + (m + 1) * 128], uT[:, c, :],
                                                                 start=(c == 0), stop=(c == 7))),
                     reads=[t_uT, t_wb], writes=[t_pB[pi]])

        def fm_part(part):
            cc = part // 2
            if part % 2 == 0:
                fm(2 + cc, 1)
                P.op("act", lambda e: e.activation(ccs[:], pB[1][:], AF.Copy), reads=[t_pB[1]], writes=[t_ccs])
                fm(4 + cc, 2)
                P.op("dve", (lambda e, cc=cc, kb=kb: e.tensor_tensor(xcst[kb][:, cc, :], pB[2][:], ccs[:], ALU.mult)),
                     reads=[t_pB[2], t_ccs], writes=[t_xcst[kb]])
            else:
                fm(6 + cc, 1)
                P.op("act", lambda e: e.activation(sis[:], pB[1][:], AF.Silu), reads=[t_pB[1]], writes=[t_sis])
                fm(0 + cc, 2)
                P.op("dve", (lambda e, cc=cc, kb=kb: e.tensor_tensor(gzst[kb][:, cc, :], pB[2][:], sis[:], ALU.mult)),
                     reads=[t_pB[2], t_sis], writes=[t_gzst[kb]])
        for tc in range(4):
            ch = 4 * j + tc
            k = ch % 2
            ps = pT[k]
            if tc == 1 and jnext is not None:
                pend = nctx.norm(jnext)
            for (half, c0, c1) in ((0, 0, 512), (1, 512, 768)):
                for c in range(8):
                    P.op("pe", (lambda e, c=c, half=half, c0=c0, c1=c1, ps=ps, tc=tc, uT=uT:
                                e.matmul(ps[half][:, 0:c1 - c0], uT[:, c, tc * 128:(tc + 1) * 128], wb[:, c, c0:c1],
                                         start=(c == 0), stop=(c == 7))),
                         reads=[t_uT, t_wb], writes=[t_pT[k][half]])
            fm_part(tc)
            if pend_tr is not None:
                pend_tr()
            for h in range(3):
                P.op("act", (lambda e, h=h, ps=ps, k=k: e.activation(junk[:], ps[0][:, h * 128:(h + 1) * 128], AF.Square,
                                                                     accum_out=st[k][:, h:h + 1])),
                     reads=[t_pT[k][0]], writes=[t_junk, t_st[k]])
            P.op("act", (lambda e, k=k: e.activation(st[k][:, 4:7], st[k][:, 0:3], AF.Ln, bias=epsb[:, 0:1], scale=1.0 / 128)),
                 reads=[t_st[k], t_eps], writes=[t_st[k]])
            P.op("act", (lambda e, k=k: e.activation(st[k][:, 8:11], st[k][:, 4:7], AF.Exp, scale=-0.5)),
                 reads=[t_st[k]], writes=[t_st[k]])
            for h in range(3):
                P.op("dve", (lambda e, h=h, ps=ps, k=k: e.scalar_tensor_tensor(
                    xn[k][:, h, :], ps[0][:, h * 128:(h + 1) * 128], st[k][:, 8 + h:9 + h], qg[:, 1 if h == 2 else 0, :],
                    ALU.mult, ALU.mult)), reads=[t_pT[k][0], t_st[k], t_qg], writes=[t_xn[k]])
            P.op("act", (lambda e, ps=ps, ch=ch: e.activation(vx[:, ch, 0:128], ps[0][:, 384:512], AF.Copy)),
                 reads=[t_pT[k][0], t_vones], writes=[t_vx[ch]])
            P.op("act", (lambda e, ps=ps, tc=tc, kb=kb: e.activation(zst[kb][:, tc, :], ps[1][:, 0:256], AF.Silu)),
                 reads=[t_pT[k][1]], writes=[t_zst[kb]])
            x4 = r4(xn[k][:])
            o4 = r4(rot[k][:])
            x1, x2 = x4[:, :, :, 0, :], x4[:, :, :, 1, :]
            cosv = csb[kb][:, tc, 0:64].rearrange("p (a i) -> p a i", a=2).unsqueeze(1).to_broadcast([128, 3, 2, 32])
            sinv = csb[kb][:, tc, 64:128].rearrange("p (a i) -> p a i", a=2).unsqueeze(1).to_broadcast([128, 3, 2, 32])
            tv = [t[:].rearrange("p h (a i) -> p h a i", a=2) for t in tmp]
            P.op("dve", (lambda e, x1=x1, cosv=cosv, tv=tv: e.tensor_tensor(tv[0], x1, cosv, ALU.mult)),
                 reads=[t_xn[k], t_cs[kb]], writes=[t_tmp[0]])
            P.op("dve", (lambda e, x2=x2, sinv=sinv, tv=tv: e.tensor_tensor(tv[1], x2, sinv, ALU.mult)),
                 reads=[t_xn[k], t_cs[kb]], writes=[t_tmp[1]])
            P.op("dve", (lambda e, o4=o4, tv=tv: e.tensor_tensor(o4[:, :, :, 0, :], tv[0], tv[1], ALU.subtract)),
                 reads=[t_tmp[0], t_tmp[1]], writes=[t_rot[k]])
            P.op("dve", (lambda e, x1=x1, sinv=sinv, tv=tv: e.tensor_tensor(tv[2], x1, sinv, ALU.mult)),
                 reads=[t_xn[k], t_cs[kb]], writes=[t_tmp[2]])
            P.op("dve", (lambda e, x2=x2, cosv=cosv, tv=tv: e.tensor_tensor(tv[3], x2, cosv, ALU.mult)),
                 reads=[t_xn[k], t_cs[kb]], writes=[t_tmp[3]])
            P.op("dve", (lambda e, o4=o4, tv=tv: e.tensor_tensor(o4[:, :, :, 1, :], tv[2], tv[3], ALU.add)),
                 reads=[t_tmp[2], t_tmp[3]], writes=[t_rot[k]])
            def tr(k=k, ch=ch, kb=kb, tc=tc, j=j):
                for h in range(3):
                    P.op("pe", (lambda e, h=h, k=k: e.transpose(pX[:, k * 512 + h * 128:k * 512 + (h + 1) * 128], rot[k][:, h, :], idb[:])),
                         reads=[t_rot[k], t_id], writes=[t_pX[k]])
                P.op("act", (lambda e, k=k, ch=ch: e.activation(kT[:, ch * 128:(ch + 1) * 128], pX[:, k * 512 + 256:k * 512 + 384], AF.Copy)),
                     reads=[t_pX[k]], writes=[t_kT[ch]])
                P.op("dve", (lambda e, k=k, kb=kb, tc=tc: e.tensor_copy(
                    qTst[kb][:, :, tc * 128:(tc + 1) * 128], pX[:, k * 512:k * 512 + 256].rearrange("p (h t) -> p h t", h=2))),
                     reads=[t_pX[k]], writes=[t_qTst[kb]])
                if tc == 3:
                    P.dma("sp", qT_sv[:, :, j * 512:(j + 1) * 512], qTst[kb][:], f"qo{kb}", reads=[t_qTst[kb]], writes=[t_qs[j]])
            pend_tr = tr
        P.dma("sp", z_sv[:, 4 * j:4 * j + 4, :], zst[kb][:], f"zo{kb}", reads=[t_zst[kb]], writes=[t_zs[j]])
        P.dma("sp", xc_sv[:, :, j * 512:(j + 1) * 512], xcst[kb][:], f"xo{kb}", reads=[t_xcst[kb]], writes=[t_xcs[j]])
        P.dma("sp", gz_sv[:, :, j * 512:(j + 1) * 512], gzst[kb][:], f"go{kb}", reads=[t_gzst[kb]], writes=[t_gzs[j]])
    pend_tr()

    xcv = [P.sb([128, 2, 514], F32) for _ in range(2)]; t_xcv = [Tok(), Tok()]
    gzv = [P.sb([128, 2, 512], F32) for _ in range(2)]; t_gzv = [Tok(), Tok()]
    cv1 = P.sb([128, 512], F32); t_cv1 = Tok()
    cv2 = P.sb([128, 512], F32); t_cv2 = Tok()
    cvo = [P.sb([128, 2, 512], BF16) for _ in range(2)]; t_cvo = [Tok(), Tok()]
    for j in range(16 if do_v else 0):
        kb = j % 2
        lo = max(j * 512 - 1, 0)
        hi = min(j * 512 + 513, S)
        o0 = lo - (j * 512 - 1)
        rd = [t_xcs[j]] + ([t_xcs[j - 1]] if j > 0 else []) + ([t_xcs[j + 1]] if j < 15 else [])
        if j == 0:
            P.op("pool", lambda e: e.memset(xcv[0][:, :, 0:1], 0.0), writes=[t_xcv[0]])
        if j == 15:
            P.op("pool", lambda e: e.memset(xcv[1][:, :, 513:514], 0.0), writes=[t_xcv[1]])
        P.dma("sp", xcv[kb][:, :, o0:o0 + hi - lo], xc_sv[:, :, lo:hi], f"xv{kb}", reads=rd, writes=[t_xcv[kb]])
        P.dma("sp", gzv[kb][:], gz_sv[:, :, j * 512:(j + 1) * 512], f"gv{kb}", reads=[t_gzs[j]], writes=[t_gzv[kb]])
        for cc in range(2):
            w0, w1, w2, bb = (cw[:, cc * 4 + i:cc * 4 + i + 1] for i in range(4))
            P.op("dve", (lambda e, kb=kb, cc=cc, w0=w0, bb=bb: e.tensor_scalar(cv1[:], xcv[kb][:, cc, 0:512], w0, bb, ALU.mult, ALU.add)),
                 reads=[t_xcv[kb], t_cw], writes=[t_cv1])
            P.op("dve", (lambda e, kb=kb, cc=cc, w1=w1: e.scalar_tensor_tensor(cv2[:], xcv[kb][:, cc, 1:513], w1, cv1[:], ALU.mult, ALU.add)),
                 reads=[t_xcv[kb], t_cw, t_cv1], writes=[t_cv2])
            P.op("dve", (lambda e, kb=kb, cc=cc, w2=w2: e.scalar_tensor_tensor(cv1[:], xcv[kb][:, cc, 2:514], w2, cv2[:], ALU.mult, ALU.add)),
                 reads=[t_xcv[kb], t_cw, t_cv2], writes=[t_cv1])
            P.op("dve", (lambda e, kb=kb, cc=cc: e.tensor_tensor(cvo[kb][:, cc, :], cv1[:], gzv[kb][:, cc, :], ALU.mult)),
                 reads=[t_cv1, t_gzv[kb]], writes=[t_cvo[kb]])
        P.dma("sp", mp[j // 2].rearrange("(a p) t -> p a t", p=128)[:, 2:4, (j % 2) * 512:(j % 2 + 1) * 512], cvo[kb][:], f"co{kb}",
              reads=[t_cvo[kb]], writes=[t_mpp[j // 2][4 + j % 2]])

    qTb = [P.sb([128, 512], BF16) for _ in range(2)]; t_qTb = [Tok(), Tok()]
    szb = [P.sb([128, 4, 128], F32) for _ in range(2)]; t_szb = [Tok(), Tok()]
    pt = [P.sb([128, 512], BF16) for _ in range(3)]; t_pt = [Tok() for _ in range(3)]
    rden = P.sb([128, 4], F32); t_rden = Tok()
    aout = [P.sb([128, 4, 128], BF16) for _ in range(2)]; t_aout = [Tok(), Tok()]
    pS = [pT[0][0][:], pT[0][1][:]]
    t_pS = t_pT[0]
    pO = [pT[1][0][:], pT[1][1][:], pB[1][:], pB[2][:]]
    t_pO = [t_pT[1][0], t_pT[1][1], t_pB[1], t_pB[2]]
    aoT = [P.sb([128, 4, 128], BF16) for _ in range(2)]; t_aoT = [Tok(), Tok()]
    it = 0
    for hd in range(2 if do_g else 0):
        for qb in range(min(16, sub)):
            kq = it % 2
            it += 1
            P.dma("sp", qTb[kq][:], qT_s[hd][:, qb * 512:(qb + 1) * 512], f"ql{kq}", reads=[t_qs[qb]], writes=[t_qTb[kq]])
            P.dma("sp", szb[kq][:], z_sv[:, 4 * qb:4 * qb + 4, hd * 128:(hd + 1) * 128], f"zl{kq}", reads=[t_zs[qb]], writes=[t_szb[kq]])

            def smm(kc):
                P.op("pe", (lambda e, kc=kc, kq=kq: e.matmul(pS[kc % 2], kT[:, kc * 128:(kc + 1) * 128], qTb[kq][:], start=True, stop=True)),
                     reads=[t_kT[kc], t_qTb[kq]], writes=[t_pS[kc % 2]])
            smm(0)
            for kc in range(64):
                if kc + 1 < 64:
                    smm(kc + 1)
                pk = kc % 3
                P.op("act", (lambda e, kc=kc, pk=pk: e.activation(pt[pk][:], pS[kc % 2], AF.Exp)),
                     reads=[t_pS[kc % 2]], writes=[t_pt[pk]])
                for qs in range(4):
                    P.op("pe", (lambda e, kc=kc, pk=pk, qs=qs: e.matmul(pO[qs][:, 0:129], pt[pk][:, qs * 128:(qs + 1) * 128], vx[:, kc, :],
                                                                       start=(kc == 0), stop=(kc == 63))),
                         reads=[t_pt[pk], t_vx[kc], t_vones], writes=[t_pO[qs]])
            for qs in range(4):
                P.op("dve", (lambda e, qs=qs: e.reciprocal(rden[:, qs:qs + 1], pO[qs][:, 128:129])), reads=[t_pO[qs]], writes=[t_rden])
                P.op("dve", (lambda e, qs=qs, kq=kq: e.scalar_tensor_tensor(aout[kq][:, qs, :], pO[qs][:, 0:128], rden[:, qs:qs + 1],
                                                                          szb[kq][:, qs, :], ALU.mult, ALU.mult)),
                     reads=[t_pO[qs], t_rden, t_szb[kq]], writes=[t_aout[kq]])
            for qs in range(4):
                P.op("pe", (lambda e, qs=qs, kq=kq: e.transpose(pX[:, qs * 128:(qs + 1) * 128], aout[kq][:, qs, :], idb[:])),
                     reads=[t_aout[kq], t_id], writes=[t_pX[0]])
            P.op("act", (lambda e, kq=kq: e.activation(aoT[kq][:], pX[:, 0:512].rearrange("p (a t) -> p a t", a=4), AF.Copy)),
                 reads=[t_pX[0]], writes=[t_aoT[kq]])
            P.dma("sp", mp[qb // 2][hd * 128:(hd + 1) * 128, (qb % 2) * 512:(qb % 2 + 1) * 512], aoT[kq][:].rearrange("p a t -> p (a t)"), f"ao{kq}",
                  reads=[t_aoT[kq]], writes=[t_mpp[qb // 2][hd * 2 + qb % 2]])
            if hd == 1 and qb % 2 == 1:
                P.coll("AllGather", GROUPS, mp[qb // 2], mg[qb // 2], f"cc{(qb // 2) % 4}", reads=t_mpp[qb // 2], writes=[io["t_mg"][qb // 2]])


def rope_tables():
    t = np.arange(S)
    pos = np.stack([t // 64, t % 64], axis=-1).astype(np.float32)
    inv = (np.float32(10000.0) ** (-np.arange(32, dtype=np.float32) / np.float32(32))).astype(np.float32)
    ang = (pos[:, :, None] * inv).astype(np.float32)
    return np.concatenate([np.cos(ang).reshape(S, 64), np.sin(ang).reshape(S, 64)], axis=1).astype(np.float32)


NWA = 2308
NA_CFG_TILES = (0, 1, 2, 62, 63)


def na_tile_cfg(i):
    if i < 2:
        return i, 0, 4
    if i >= 62:
        return 3 + (i - 62), 120, 4
    return 2, 2 * i - 4, 5


def emit_mixer0(nc, P, PSB, t_PSB, PXg, t_PXg, io, do_m=True, do_n=True):
    BLK = 256
    NB = S // BLK
    hT, w, ng, gateb, mlg, tri, nab, nam, ident = (io[k] for k in ("hT", "w", "ng", "gateb", "mlg", "tri", "nab", "nam", "ident"))
    mp, mg = io["mp"], io["mg"]
    dint = lambda n, s, d: nc.dram_tensor(n, s, d, kind="Internal").ap()
    fm_s = dint("fm_s", [64, 128, 768], BF16)
    tm_s = dint("tm_s", [64, 128, 769], BF16)
    g_s = dint("g_s", [64, 128, 768], F32)
    na_s = dint("na_s", [4, 128, S], BF16)
    vna_s = dint("vna_s", [64, 128, 260], BF16)
    h_s = dint("h_s", [2, 64, 128, 256], F32)
    t_mpp = [[Tok() for _ in range(16)] for _ in range(8)]

    wb = P.sb([128, 8, NWA], BF16); t_wb = Tok()
    wv = w.rearrange("(c p) n -> p c n", p=128)
    _tw = []
    for c in range(8):
        for (a, b_) in ((0, 1024), (1024, 2048), (2048, NWA)):
            _tw.append(Tok())
            P.dma("pool", wb[:, c, a:b_], wv[:, c, a:b_], "wb", writes=[_tw[-1]])
    w_join(P, _tw, t_wb)
    ngs = P.sb([128, 8], F32); t_ng = Tok()
    P.dma("sp", ngs[:], ng[:, :], "c_ng", writes=[t_ng])
    gbb = P.sb([128, 4], F32); t_gbb = Tok()
    P.dma("sp", gbb[:], gateb[0:1, :].to_broadcast([128, 4]), "c_gb", writes=[t_gbb])
    mlgb = P.sb([128, 256], F32); t_mlg = Tok()
    P.dma("sp", mlgb[:], mlg[0:1, :].to_broadcast([128, 256]), "c_mlg", writes=[t_mlg])
    trs = P.sb([128, 2, 128], F32); t_tri = Tok()
    P.dma("sp", trs[:], tri.rearrange("a s t -> s a t"), "c_tri", writes=[t_tri])
    idb = P.sb([128, 128], BF16); t_id = Tok()
    P.dma("sp", idb[:], ident[:, :], "c_id", writes=[t_id])
    ones_bf = P.sb([128, 128], BF16); t_ones = Tok()
    P.op("pool", lambda e: e.memset(ones_bf[:], 1.0), writes=[t_ones])
    ones32 = P.sb([128, 128], F32); t_ones32 = Tok()
    P.op("pool", lambda e: e.memset(ones32[:], 1.0), writes=[t_ones32])
    epsb = P.sb([128, 2], F32); t_eps = Tok()
    P.op("pool", lambda e: e.memset(epsb[:, 0:1], EPS), writes=[t_eps])
    P.op("pool", lambda e: e.memset(epsb[:, 1:2], 1.0), writes=[t_eps])
    EA = P.sb([128, 64, 2], F32); t_EA = [Tok() for _ in range(64)]
    EG = P.sb([128, 64, 2], F32); t_EG = [Tok() for _ in range(64)]
    pb, t_pb, pX, t_pX = PSB, t_PSB, PXg, t_PXg

    nctx = NormCtx(P, hT.rearrange("(c p) t -> p c t", p=128), ngs, t_ng, ones_bf, t_ones, epsb, t_eps, pb[0], t_pb[0], blk=BLK)
    g4 = [P.sb([128, 4], F32) for _ in range(2)]; t_g4 = [Tok(), Tok()]
    sm = [P.sb([128, 16], F32) for _ in range(2)]; t_sm = [Tok(), Tok()]
    X5 = [P.sb([128, 5, 256], BF16) for _ in range(2)]; t_X5 = [Tok(), Tok()]
    FMst = [P.sb([128, 768], BF16) for _ in range(2)]; t_FMst = [Tok(), Tok()]
    vml = [P.sb([128, 257], BF16) for _ in range(2)]; t_vml = [Tok(), Tok()]
    GS = [P.sb([128, 3, 256], F32) for _ in range(2)]; t_GS = [Tok(), Tok()]
    Y4 = [P.sb([128, 512], BF16) for _ in range(2)]; t_Y4 = [Tok(), Tok()]
    NAst = [P.sb([128, 4, 128], BF16) for _ in range(2)]; t_NAst = [Tok(), Tok()]
    vst = [P.sb([128, 4, 65], BF16) for _ in range(2)]; t_vst = [Tok(), Tok()]
    t_fm = [Tok() for _ in range(64)]
    t_tm = [Tok() for _ in range(64)]
    t_tm2 = [Tok() for _ in range(64)]
    t_gs = [Tok() for _ in range(64)]
    t_nas = [Tok() for _ in range(64)]
    t_vna = [Tok() for _ in range(64)]
    for k in range(2):
        P.op("pool", (lambda e, k=k: e.memset(vml[k][:, 256:257], 1.0)), writes=[t_vml[k]])
        P.op("pool", (lambda e, k=k: e.memset(vst[k][:, :, 64:65], 1.0)), writes=[t_vst[k]])
    na_sv = na_s.rearrange("a p t -> p a t")

    def grp(uT, t_uT, tc, c0, c1, bank):
        for c in range(8):
            P.op("pe", (lambda e, c=c: e.matmul(pb[bank][:, 0:c1 - c0], uT[:, c, tc * 128:(tc + 1) * 128], wb[:, c, c0:c1],
                                                start=(c == 0), stop=(c == 7))),
                 reads=[t_uT, t_wb], writes=[t_pb[bank]])

    nctx.load(0)
    pend_nat = None
    for j in range(NB):
        if j + 1 < NB:
            nctx.load(j + 1)
        if j == 0:
            pend = nctx.norm(0)
        uT, t_uT = pend
        for tc in range(BLK // 128):
            ch = j * (BLK // 128) + tc
            k = ch % 2
            if tc == 1 and j + 1 < NB:
                pend = nctx.norm(j + 1)
            if pend_nat is not None:
                pend_nat()
            grp(uT, t_uT, tc, 2048, 2308, 5)
            P.op("dve", (lambda e, k=k: e.tensor_tensor(g4[k][:], pb[5][:, 256:260], gbb[:], ALU.add)),
                 reads=[t_pb[5], t_gbb], writes=[t_g4[k]])
            P.op("act", (lambda e, k=k: e.activation(vst[k][:, :, 0:64], pb[5][:, 0:256].rearrange("p (h d) -> p h d", h=4), AF.Copy)),
                 reads=[t_pb[5]], writes=[t_vst[k]])
            P.dma("sp", vna_s[ch], vst[k][:].rearrange("p h d -> p (h d)"), f"vn{k}", reads=[t_vst[k]], writes=[t_vna[ch]])
            P.op("act", (lambda e, k=k: e.activation(sm[k][:, 0:2], g4[k][:, 1:4:2], AF.Exp, scale=-1.0)),
                 reads=[t_g4[k]], writes=[t_sm[k]])
            P.op("act", (lambda e, k=k: e.activation(sm[k][:, 2:4], sm[k][:, 0:2], AF.Ln, bias=epsb[:, 1:2])),
                 reads=[t_sm[k], t_eps], writes=[t_sm[k]])
            grp(uT, t_uT, tc, 0, 512, 1)
            grp(uT, t_uT, tc, 512, 1024, 2)
            P.op("act", (lambda e, k=k: e.activation(vml[k][:, 0:256], pb[2][:, 0:256], AF.Copy)), reads=[t_pb[2]], writes=[t_vml[k]])
            P.dma("sp", tm_s[ch][:, 512:769], vml[k][:], f"vo{k}", reads=[t_vml[k]], writes=[t_tm2[ch]])
            P.op("pe", (lambda e, k=k: e.matmul(pb[6][:, 0:1], trs[:, 0, :], sm[k][:, 2:3], start=True, stop=True)),
                 reads=[t_tri, t_sm[k]], writes=[t_pb[6]])
            P.op("pe", (lambda e, k=k: e.matmul(pb[6][:, 1:2], trs[:, 1, :], sm[k][:, 3:4], start=True, stop=True)),
                 reads=[t_tri, t_sm[k]], writes=[t_pb[6]])
            P.op("pe", (lambda e, k=k: e.matmul(pb[6][:, 2:4], ones32[:], sm[k][:, 2:4], start=True, stop=True)),
                 reads=[t_ones32, t_sm[k]], writes=[t_pb[6]])
            P.op("act", (lambda e, k=k: e.activation(sm[k][:, 4:6], pb[6][:, 0:2], AF.Exp, scale=-1.0)),
                 reads=[t_pb[6]], writes=[t_sm[k]])
            P.op("act", (lambda e, ch=ch: e.activation(EG[:, ch, :], pb[6][:, 2:4], AF.Exp, scale=-1.0)),
                 reads=[t_pb[6]], writes=[t_EG[ch]])
            P.op("dve", (lambda e, k=k: e.tensor_tensor(sm[k][:, 6:8], pb[6][:, 0:2], g4[k][:, 0:4:2], ALU.add)),
                 reads=[t_pb[6], t_g4[k]], writes=[t_sm[k]])
            P.op("act", (lambda e, k=k: e.activation(sm[k][:, 8:10], sm[k][:, 6:8], AF.Exp)), reads=[t_sm[k]], writes=[t_sm[k]])
            P.op("dve", (lambda e, k=k, ch=ch: e.tensor_scalar(EA[:, ch, :], sm[k][:, 8:10], 1.0 / 16.0, None, ALU.mult)),
                 reads=[t_sm[k]], writes=[t_EA[ch]])
            grp(uT, t_uT, tc, 1024, 1536, 3)
            P.op("dve", (lambda e, k=k: e.tensor_scalar(X5[k][:, 0, :], pb[1][:, 0:256], sm[k][:, 4:5], None, ALU.mult)),
                 reads=[t_pb[1], t_sm[k]], writes=[t_X5[k]])
            P.op("dve", (lambda e, k=k: e.tensor_scalar(X5[k][:, 1, :], pb[1][:, 0:256], sm[k][:, 5:6], None, ALU.mult)),
                 reads=[t_pb[1], t_sm[k]], writes=[t_X5[k]])
            P.op("act", (lambda e, k=k: e.activation(X5[k][:, 2, :], pb[1][:, 256:512], AF.Copy)), reads=[t_pb[1]], writes=[t_X5[k]])
            P.op("dve", (lambda e, k=k, ch=ch: e.tensor_scalar(X5[k][:, 3, :], pb[1][:, 256:512], EA[:, ch, 0:1], None, ALU.mult)),
                 reads=[t_pb[1], t_EA[ch]], writes=[t_X5[k]])
            P.op("dve", (lambda e, k=k, ch=ch: e.tensor_scalar(X5[k][:, 4, :], pb[1][:, 256:512], EA[:, ch, 1:2], None, ALU.mult)),
                 reads=[t_pb[1], t_EA[ch]], writes=[t_X5[k]])
            P.dma("sp", tm_s[ch][:, 0:512], X5[k][:, 3:5, :].rearrange("p a d -> p (a d)"), f"to{k}", reads=[t_X5[k]], writes=[t_tm[ch]])
            grp(uT, t_uT, tc, 1536, 2048, 4)
            P.op("act", (lambda e, k=k: e.activation(GS[k][:, 0, :], pb[2][:, 256:512], AF.Tanh, scale=0.5)), reads=[t_pb[2]], writes=[t_GS[k]])
            P.op("dve", (lambda e, k=k: e.tensor_scalar(GS[k][:, 0, :], GS[k][:, 0, :], 0.5, 0.5, ALU.mult, ALU.add)), reads=[], writes=[t_GS[k]])
            P.op("act", (lambda e, k=k: e.activation(GS[k][:, 1:3, :], pb[3][:].rearrange("p (a d) -> p a d", a=2), AF.Silu)),
                 reads=[t_pb[3]], writes=[t_GS[k]])
            P.dma("sp", g_s[ch], GS[k][:].rearrange("p a d -> p (a d)"), f"go{k}", reads=[t_GS[k]], writes=[t_gs[ch]])
            P.op("act", (lambda e, k=k: e.activation(Y4[k][:], pb[4][:], AF.Copy)), reads=[t_pb[4]], writes=[t_Y4[k]])
            for s_ in range(3):
                for hf in range(2):
                    P.op("pe", (lambda e, k=k, s_=s_, hf=hf: e.transpose(pX[:, (s_ * 2 + hf) * 128:(s_ * 2 + hf + 1) * 128],
                                                                        X5[k][:, s_, hf * 128:(hf + 1) * 128], idb[:])),
                         reads=[t_X5[k], t_id], writes=[t_pX])
            P.op("act", (lambda e, k=k: e.activation(FMst[k][:], pX[:, 0:768], AF.Copy)), reads=[t_pX], writes=[t_FMst[k]])
            P.dma("sp", fm_s[ch], FMst[k][:], f"fo{k}", reads=[t_FMst[k]], writes=[t_fm[ch]])

            def nat(k=k, ch=ch):
                for a in range(4):
                    P.op("pe", (lambda e, k=k, a=a: e.transpose(pX[:, a * 128:(a + 1) * 128], Y4[k][:, a * 128:(a + 1) * 128], idb[:])),
                         reads=[t_Y4[k], t_id], writes=[t_pX])
                P.op("dve", (lambda e, k=k: e.tensor_copy(NAst[k][:], pX[:, 0:512].rearrange("p (a t) -> p a t", a=4))),
                     reads=[t_pX], writes=[t_NAst[k]])
                P.dma("sp", na_sv[:, :, ch * 128:(ch + 1) * 128], NAst[k][:], f"no{k}", reads=[t_NAst[k]], writes=[t_nas[ch]])
            pend_nat = nat
    pend_nat()

    if True:
        fmB = [[P.sb([128, 768], BF16) for _ in range(2)] for _ in range(2)]
        tmB = [[P.sb([128, 769], BF16) for _ in range(2)] for _ in range(2)]
        t_fmB = [[Tok(), Tok()], [Tok(), Tok()]]
        t_tmB = [[Tok(), Tok()], [Tok(), Tok()]]
        wT = [[P.sb([128, 128], BF16) for _ in range(2)] for _ in range(2)]; t_wT = [[Tok(), Tok()], [Tok(), Tok()]]
        Cf = [P.sb([128, 2, 257], F32) for _ in range(2)]; t_Cf = [Tok(), Tok()]
        Cb = [P.sb([128, 2, 257], BF16) for _ in range(2)]; t_Cb = [Tok(), Tok()]
        ctmp = [[P.sb([128, 2, 257], F32) for _ in range(2)] for _ in range(2)]; t_ctmp = [[Tok(), Tok()], [Tok(), Tok()]]
        dn = [P.sb([128, 4], F32) for _ in range(2)]; t_dn = [Tok(), Tok()]
        hst = [[P.sb([128, 256], F32) for _ in range(2)] for _ in range(2)]
        t_hst = [[Tok(), Tok()], [Tok(), Tok()]]
        t_hs = [[Tok() for _ in range(64)] for _ in range(2)]
        for d_ in range(2):
            P.op("pool", (lambda e, d_=d_: e.memset(Cf[d_][:], 0.0)), writes=[t_Cf[d_]])
        def m_front(c):
            kk = c % 2
            chs = (c, 63 - c)
            for d_ in range(2):
                ch = chs[d_]
                P.dma("sp", fmB[d_][kk][:], fm_s[ch], f"fl{d_}{kk}", reads=[t_fm[ch]], writes=[t_fmB[d_][kk]])
                P.dma("sp", tmB[d_][kk][:], tm_s[ch], f"tl{d_}{kk}", reads=[t_tm[ch], t_tm2[ch]], writes=[t_tmB[d_][kk]])
            for d_ in range(2):
                ch = chs[d_]
                fmv = fmB[d_][kk][:].rearrange("p (a h t) -> p a h t", a=3, h=2)
                for hf in range(2):
                    P.op("pe", (lambda e, d_=d_, hf=hf, fmv=fmv: e.matmul(pb[0][:, d_ * 128:(d_ + 1) * 128], fmv[:, 2, hf, :], fmv[:, d_, hf, :],
                                                                         start=(hf == 0), stop=(hf == 1))),
                         reads=[t_fmB[d_][kk]], writes=[t_pb[0]])
                P.op("dve", (lambda e, d_=d_, ch=ch, kk=kk: e.scalar_tensor_tensor(wT[d_][kk][:], pb[0][:, d_ * 128:(d_ + 1) * 128], EA[:, ch, d_:d_ + 1], trs[:, d_, :],
                                                                           ALU.mult, ALU.mult)),
                     reads=[t_pb[0], t_EA[ch], t_tri], writes=[t_wT[d_][kk]])
            for d_ in range(2):
                ch = chs[d_]
                tmv = tmB[d_][kk]
                for hf in range(2):
                    P.op("pe", (lambda e, d_=d_, hf=hf, tmv=tmv: e.matmul(pb[3 + hf][:, 0:257], tmv[:, d_ * 256 + hf * 128:d_ * 256 + (hf + 1) * 128],
                                                                         tmv[:, 512:769], start=True, stop=True)),
                         reads=[t_tmB[d_][kk]], writes=[t_pb[3 + hf]])
                    P.op("act", (lambda e, d_=d_, hf=hf, ch=ch, kk=kk: e.activation(ctmp[d_][kk][:, hf, :], pb[3 + hf][:, 0:257], AF.Copy, scale=EG[:, ch, d_:d_ + 1])),
                         reads=[t_pb[3 + hf], t_EG[ch]], writes=[t_ctmp[d_][kk]])

        def m_back(c):
            kk = c % 2
            chs = (c, 63 - c)
            for d_ in range(2):
                ch = chs[d_]
                fmv = fmB[d_][kk][:].rearrange("p (a h t) -> p a h t", a=3, h=2)
                tmv = tmB[d_][kk]
                if c > 0:
                    for hf in range(2):
                        P.op("pe", (lambda e, d_=d_, hf=hf, fmv=fmv: e.matmul(pb[1 + d_][:, 0:257], fmv[:, d_, hf, :], Cb[d_][:, hf, :],
                                                                             start=(hf == 0), stop=False)),
                             reads=[t_fmB[d_][kk], t_Cb[d_]], writes=[t_pb[1 + d_]])
                P.op("pe", (lambda e, d_=d_, tmv=tmv, c=c, kk=kk: e.matmul(pb[1 + d_][:, 0:257], wT[d_][kk][:], tmv[:, 512:769], start=(c == 0), stop=True)),
                     reads=[t_wT[d_][kk], t_tmB[d_][kk]], writes=[t_pb[1 + d_]])
                P.op("act", (lambda e, d_=d_: e.activation(dn[d_][:, 0:1], pb[1 + d_][:, 256:257], AF.Abs)), reads=[t_pb[1 + d_]], writes=[t_dn[d_]])
                P.op("dve", (lambda e, d_=d_: e.tensor_scalar(dn[d_][:, 1:2], dn[d_][:, 0:1], 1.0, None, ALU.max)), reads=[t_dn[d_]], writes=[t_dn[d_]])
                P.op("dve", (lambda e, d_=d_: e.reciprocal(dn[d_][:, 2:3], dn[d_][:, 1:2])), reads=[t_dn[d_]], writes=[t_dn[d_]])
                P.op("dve", (lambda e, d_=d_, kk=kk: e.tensor_scalar(hst[d_][kk][:], pb[1 + d_][:, 0:256], dn[d_][:, 2:3], None, ALU.mult)),
                     reads=[t_pb[1 + d_], t_dn[d_]], writes=[t_hst[d_][kk]])
                P.dma("sp", h_s[d_][ch], hst[d_][kk][:], f"ho{d_}{kk}", reads=[t_hst[d_][kk]], writes=[t_hs[d_][ch]])
                P.op("dve", (lambda e, d_=d_, ch=ch, kk=kk: e.scalar_tensor_tensor(Cf[d_][:], Cf[d_][:], EG[:, ch, d_:d_ + 1], ctmp[d_][kk][:], ALU.mult, ALU.add)),
                     reads=[t_EG[ch], t_ctmp[d_][kk]], writes=[t_Cf[d_]])
                P.op("dve", (lambda e, d_=d_: e.tensor_copy(Cb[d_][:], Cf[d_][:])), reads=[t_Cf[d_]], writes=[t_Cb[d_]])

        hfb = P.sb([128, 2, 4, 256], F32); t_hfb = Tok(); t_hfb2 = [Tok(), Tok()]
        gfb = P.sb([128, 4, 512], F32); t_gfb = Tok()
        hsum = P.sb([128, 4, 256], F32); t_hsum = Tok()
        junk = P.sb([128, 256], F32); t_junk = Tok()
        fst = P.sb([128, 12], F32); t_fst = Tok()
        mo = P.sb([128, 4, 256], BF16); t_mo = Tok()
        moT = P.sb([128, 2, 4, 128], BF16); t_moT = Tok()

        def m_final4(g):
            c0 = 4 * g
            for d_ in range(2):
                P.dma("sp", hfb[:, d_, :, :], h_s[d_][c0:c0 + 4].rearrange("j p f -> p j f"), f"hl{d_}",
                      reads=[t_hs[d_][c0 + j] for j in range(4)], writes=[t_hfb2[d_]])
            P.dma("sp", gfb[:], g_s[c0:c0 + 4].rearrange("j p f -> p j f")[:, :, 0:512], "gl0", reads=[t_gs[c0 + j] for j in range(4)], writes=[t_gfb])
            P.op("dve", (lambda e: e.tensor_tensor(hsum[:], hfb[:, 0, :, :], hfb[:, 1, :, :], ALU.add)), reads=t_hfb2, writes=[t_hsum])
            for j in range(4):
                P.op("act", (lambda e, j=j: e.activation(junk[:], hsum[:, j, :], AF.Square, accum_out=fst[:, j:j + 1])),
                     reads=[t_hsum], writes=[t_junk, t_fst])
            P.op("act", (lambda e: e.activation(fst[:, 4:8], fst[:, 0:4], AF.Ln, bias=epsb[:, 0:1], scale=1.0 / 256)), reads=[t_fst, t_eps], writes=[t_fst])
            P.op("act", (lambda e: e.activation(fst[:, 8:12], fst[:, 4:8], AF.Exp, scale=-0.5)), reads=[t_fst], writes=[t_fst])
            for j in range(4):
                P.op("dve", (lambda e, j=j: e.scalar_tensor_tensor(hsum[:, j, :], hsum[:, j, :], fst[:, 8 + j:9 + j], mlgb[:], ALU.mult, ALU.mult)),
                     reads=[t_fst, t_mlg], writes=[t_hsum])
            P.op("dve", (lambda e: e.tensor_tensor(hsum[:], hsum[:], gfb[:, :, 0:256], ALU.mult)), reads=[t_gfb], writes=[t_hsum])
            P.op("dve", (lambda e: e.tensor_tensor(mo[:], hsum[:], gfb[:, :, 256:512], ALU.mult)), reads=[t_gfb, t_hsum], writes=[t_mo])
            for hf in range(2):
                for j in range(4):
                    P.op("pe", (lambda e, hf=hf, j=j: e.transpose(pX[:, (hf * 4 + j) * 128:(hf * 4 + j + 1) * 128], mo[:, j, hf * 128:(hf + 1) * 128], idb[:])),
                         reads=[t_mo, t_id], writes=[t_pX])
            P.op("act", (lambda e: e.activation(moT[:].rearrange("p a j t -> p (a j t)"), pX[:, 0:1024], AF.Copy)), reads=[t_pX], writes=[t_moT])
            p_, q_ = c0 // 8, (c0 % 8) * 128
            P.dma("sp", mp[p_].rearrange("(a p) t -> p a t", p=128)[:, 0:2, q_:q_ + 512], moT[:].rearrange("p a j t -> p a (j t)"), "mo0",
                  reads=[t_moT], writes=[t_mpp[p_][(c0 % 8) + j] for j in range(4)])

    if True:
        Et = P.sb([128, 5, 20, 128], BF16); t_E = [Tok() for _ in range(5)]
        btmp = P.sb([128, 20, 128], F32); t_btmp = Tok()
        mtmp = P.sb([128, 5, 128], F32); t_mtmp = Tok()
        for cf in range(5):
            P.dma("sp", btmp[:], nab[cf], "bl", writes=[t_btmp])
            P.dma("sp", mtmp[:], nam[cf], "ml", writes=[t_mtmp])
            P.op("act", (lambda e: e.activation(btmp[:], btmp[:], AF.Exp)), writes=[t_btmp])
            for h in range(4):
                P.op("dve", (lambda e, cf=cf, h=h: e.tensor_tensor(Et[:, cf, h * 5:(h + 1) * 5, :], btmp[:, h * 5:(h + 1) * 5, :], mtmp[:], ALU.mult)),
                     reads=[t_btmp, t_mtmp], writes=[t_E[cf]])
        qn = [P.sb([128, 2, 128], BF16) for _ in range(2)]; t_qn = [Tok(), Tok()]
        kn_ = [P.sb([128, 2, 640], BF16) for _ in range(2)]; t_kn = [Tok(), Tok()]
        vn = [P.sb([128, 5, 260], BF16) for _ in range(2)]; t_vn = [Tok(), Tok()]
        gzn = [P.sb([128, 256], F32) for _ in range(2)]; t_gzn = [Tok(), Tok()]
        pp = [P.sb([128, 5, 128], BF16) for _ in range(3)]; t_pp = [Tok() for _ in range(3)]
        rdn = P.sb([128, 4], F32); t_rdn = Tok()
        no = [P.sb([128, 256], BF16) for _ in range(2)]; t_no = [Tok(), Tok()]
        noT = [P.sb([128, 2, 128], BF16) for _ in range(2)]; t_noT = [Tok(), Tok()]
        vna_v = vna_s.rearrange("n p f -> p n f")
        cnt = [0]

        def n_tile(i):
            k = i % 2
            cf, r0, nb = na_tile_cfg(i)
            c0 = r0 // 2
            nk = 512 if nb == 4 else 576
            P.dma("sp", qn[k][:], na_sv[:, 0:2, i * 128:(i + 1) * 128], f"nq{k}", reads=[t_nas[i]], writes=[t_qn[k]])
            P.dma("sp", kn_[k][:, :, 0:nk], na_sv[:, 2:4, r0 * 64:r0 * 64 + nk], f"nk{k}",
                  reads=[t_nas[cc] for cc in range(c0, c0 + nb)], writes=[t_kn[k]])
            P.dma("sp", vn[k][:, 0:nb, :], vna_v[:, c0:c0 + nb, :], f"nv{k}", reads=[t_vna[cc] for cc in range(c0, c0 + nb)], writes=[t_vn[k]])
            P.dma("sp", gzn[k][:], g_s[i][:, 512:768], f"ng{k}", reads=[t_gs[i]], writes=[t_gzn[k]])
            ob = 6
            sbanks = (5, 7)

            def s_mm(h):
                pr, off = h // 2, (h % 2) * 64
                sb_ = sbanks[h % 2]
                for bl in range(nb):
                    kn = 128 if bl < 4 else 64
                    if bl < 4:
                        dst, tk = pb[sb_][0:kn, bl * 128:(bl + 1) * 128], t_pb[sb_]
                    else:
                        dst, tk = pb[0][0:kn, 256 + 128 * (h % 2):384 + 128 * (h % 2)], t_pb[0]
                    P.op("pe", (lambda e, k=k, pr=pr, off=off, bl=bl, kn=kn, dst=dst: e.matmul(
                        dst, kn_[k][off:off + 64, pr, bl * 128:bl * 128 + kn], qn[k][off:off + 64, pr, :],
                        start=True, stop=True)), reads=[t_kn[k], t_qn[k]], writes=[tk])
            s_mm(0)
            for h in range(4):
                sb_ = sbanks[h % 2]
                pk = cnt[0] % 3
                cnt[0] += 1
                if h + 1 < 4:
                    s_mm(h + 1)
                P.op("act", (lambda e, pk=pk, sb_=sb_: e.activation(pp[pk][:, 0:4, :], pb[sb_][:].rearrange("p (b q) -> p b q", b=4), AF.Exp, scale=0.125)),
                     reads=[t_pb[sb_]], writes=[t_pp[pk]])
                if nb == 5:
                    P.op("act", (lambda e, pk=pk, h=h: e.activation(pp[pk][0:64, 4, :], pb[0][0:64, 256 + 128 * (h % 2):384 + 128 * (h % 2)], AF.Exp, scale=0.125)),
                         reads=[t_pb[0]], writes=[t_pp[pk]])
                P.op("dve", (lambda e, pk=pk, cf=cf, h=h: e.tensor_tensor(pp[pk][:, 0:4, :], pp[pk][:, 0:4, :], Et[:, cf, h * 5:h * 5 + 4, :], ALU.mult)),
                     reads=[t_E[cf]], writes=[t_pp[pk]])
                if nb == 5:
                    P.op("dve", (lambda e, pk=pk, cf=cf, h=h: e.tensor_tensor(pp[pk][0:64, 4, :], pp[pk][0:64, 4, :], Et[0:64, cf, h * 5 + 4, :], ALU.mult)),
                         reads=[t_E[cf]], writes=[t_pp[pk]])
                for bl in range(nb):
                    kn = 128 if bl < 4 else 64
                    P.op("pe", (lambda e, k=k, pk=pk, h=h, bl=bl, kn=kn, ob=ob: e.matmul(
                        pb[ob][:, h * 65:(h + 1) * 65], pp[pk][0:kn, bl, :], vn[k][0:kn, bl, h * 65:(h + 1) * 65],
                        start=(bl == 0), stop=(bl == nb - 1))), reads=[t_pp[pk], t_vn[k]], writes=[t_pb[ob]])
            ov = pb[ob][:, 0:260].rearrange("p (h d) -> p h d", h=4)
            P.op("dve", (lambda e, ov=ov: e.reciprocal(rdn[:], ov[:, :, 64])), reads=[t_pb[ob]], writes=[t_rdn])
            for h in range(4):
                P.op("dve", (lambda e, k=k, h=h, ov=ov: e.scalar_tensor_tensor(no[k][:, h * 64:(h + 1) * 64], ov[:, h, 0:64], rdn[:, h:h + 1],
                                                                             gzn[k][:, h * 64:(h + 1) * 64], ALU.mult, ALU.mult)),
                     reads=[t_pb[ob], t_rdn, t_gzn[k]], writes=[t_no[k]])
            for hf in range(2):
                P.op("pe", (lambda e, k=k, hf=hf: e.transpose(pX[:, hf * 128:(hf + 1) * 128], no[k][:, hf * 128:(hf + 1) * 128], idb[:])),
                     reads=[t_no[k], t_id], writes=[t_pX])
            P.op("act", (lambda e, k=k: e.activation(noT[k][:], pX[:, 0:256].rearrange("p (a t) -> p a t", a=2), AF.Copy)),
                 reads=[t_pX], writes=[t_noT[k]])
            P.dma("sp", mp[i // 8].rearrange("(a p) t -> p a t", p=128)[:, 2:4, (i % 8) * 128:(i % 8 + 1) * 128], noT[k][:], f"nn{k}",
                  reads=[t_noT[k]], writes=[t_mpp[i // 8][8 + i % 8]])


    def ag(p):
        P.coll("AllGather", GROUPS, mp[p], mg[p], f"cc{p % 4}", reads=t_mpp[p], writes=[io["t_mg"][p]])
    m_front(0)
    for c in range(64):
        if c + 1 < 64:
            m_front(c + 1)
        m_back(c)
        n_tile(c)
        if c >= 35 and c % 4 == 3:
            m_final4((c - 3) // 4)
            m_final4((63 - c) // 4)
            if c % 8 == 7:
                q = (c - 39) // 8
                ag(3 - q)
                ag(4 + q)


def na_bias_tables(rpb4):
    kp = np.arange(128)
    q = np.arange(128)
    bias = np.zeros((5, 128, 20, 128), np.float32)
    mask = np.zeros((5, 128, 5, 128), np.float32)
    for ci, i in enumerate(NA_CFG_TILES):
        _, r0k, nb = na_tile_cfg(i)
        qr = 2 * i + q // 64
        qc = q % 64
        rr0 = np.clip(qr - 4, 0, 120)
        cc0 = np.clip(qc - 8, 0, 48)
        for bl in range(nb):
            npart = 128 if bl < 4 else 64
            kr = r0k + bl * 2 + kp // 64
            kc = kp % 64
            valid = ((kr[:, None] >= rr0[None, :]) & (kr[:, None] < rr0[None, :] + 8) &
                     (kc[:, None] >= cc0[None, :]) & (kc[:, None] < cc0[None, :] + 16) & (kp[:, None] < npart))
            dr = np.clip(kr[:, None] - qr[None, :] + 7, 0, 14)
            dc = np.clip(kc[:, None] - qc[None, :] + 15, 0, 30)
            mask[ci, :, bl, :] = valid
            for h in range(4):
                bias[ci, :, h * 5 + bl, :] = rpb4[h][dr, dc]
    return bias, mask


def emit_outproj(nc, P, PSB, t_PSB, io, final):
    mg, t_mg, wout = io["mg"], io["t_mg"], io["wout"]
    m0 = P.mark()
    wb = P.sb([128, 16, D], BF16); t_wb = Tok()
    _tw = []
    for c in range(16):
        r, q = c // 4, c % 4
        row0 = (r * 256 + q * 128) if q < 2 else (1024 + r * 256 + (q - 2) * 128)
        _tw.append(Tok())
        P.dma("pool", wb[:, c, :], wout[row0:row0 + 128, :], "wb", writes=[_tw[-1]])
    w_join(P, _tw, t_wb)
    NBUF = 4
    mt = [P.sb([128, 16, 256], BF16) for _ in range(NBUF)]; t_mt = [Tok() for _ in range(NBUF)]
    xr = [P.sb([128, 8, 256], F32) for _ in range(NBUF)]; t_xr = [Tok() for _ in range(NBUF)]
    hT = [P.sb([128, 8, 256], F32) for _ in range(2)]; t_hT = [Tok(), Tok()]
    if final:
        ones_bf = P.sb([128, 128], BF16); t_ones = Tok()
        P.op("pool", lambda e: e.memset(ones_bf[:], 1.0), writes=[t_ones])
        epsb = P.sb([128, 1], F32); t_eps = Tok()
        P.op("pool", lambda e: e.memset(epsb[:], EPS), writes=[t_eps])
        gfs = P.sb([128, 8], F32); t_gf = Tok()
        P.dma("sp", gfs[:], io["gfin"][:, :], "c_ng", writes=[t_gf])
        xsq = P.sb([128, 8, 256], BF16); t_xsq = Tok()
        lnv = P.sb([128, 256], F32); t_lnv = Tok()
        rstd = P.sb([128, 256], F32); t_rstd = Tok()
        ob = [P.sb([128, 8, 256], F32) for _ in range(2)]; t_ob = [Tok(), Tok()]
    order = io.get("order", list(range(8)))
    for n_, i in enumerate(order):
        k = n_ % 2
        kl = n_ % NBUF

        def ld(e, i=i, k=kl):
            if "rank" not in P.__dict__:
                P.rank = e.partition_id() % 4
            r = P.rank
            return e.dma_start(out=mt[k][:], in_=mg[i].rearrange("(c p) t -> p c t", p=128)[:, :, bass.ts(r, 256)])
        P.dma_fn("sp", ld, f"ml{kl}", reads=[t_mg[i]], writes=[t_mt[kl]])
        if "xres" in io:
            P.dma("sp", xr[kl][:], io["xres"][i].rearrange("(c p) t -> p c t", p=128), f"xr{kl}", writes=[t_xr[kl]])
        else:
            P.dma("sp", xr[kl][:], io["hp_in"][i].rearrange("(c p) t -> p c t", p=128), f"xr{kl}", reads=[io["t_hp_in"][i]], writes=[t_xr[kl]])
        for n in range(8):
            bank = 1 + n % 4
            for c in range(16):
                P.op("pe", (lambda e, c=c, n=n, kl=kl, bank=bank: e.matmul(PSB[bank][:, 0:256], wb[:, c, n * 128:(n + 1) * 128], mt[kl][:, c, :],
                                                                      start=(c == 0), stop=(c == 15))),
                     reads=[t_wb, t_mt[kl]], writes=[t_PSB[bank]])
            P.op("dve", (lambda e, n=n, k=k, kl=kl, bank=bank: e.tensor_tensor(hT[k][:, n, :], PSB[bank][:, 0:256], xr[kl][:, n, :], ALU.add)),
                 reads=[t_PSB[bank], t_xr[kl]], writes=[t_hT[k]])
        if not final:
            hp, t_hp, hgl, t_hg = io["hp"], io["t_hp"], io["hg"], io["t_hg"]
            P.dma("sp", hp[i].rearrange("(c p) t -> p c t", p=128), hT[k][:], f"ho{k}", reads=[t_hT[k]], writes=[t_hp[i]])
            P.coll("AllGather", GROUPS, hp[i], hgl[i], f"cc{4 + i % 4}", reads=[t_hp[i]], writes=[t_hg[i]])
        else:
            P.op("act", (lambda e, k=k: e.activation(xsq[:], hT[k][:], AF.Square)), reads=[t_hT[k]], writes=[t_xsq])
            for c in range(8):
                P.op("pe", (lambda e, c=c: e.matmul(PSB[0][:, 0:256], ones_bf[:], xsq[:, c, :], start=(c == 0), stop=(c == 7))),
                     reads=[t_ones, t_xsq], writes=[t_PSB[0]])
            P.op("act", (lambda e: e.activation(lnv[:], PSB[0][:, 0:256], AF.Ln, bias=epsb[:, 0:1], scale=1.0 / D)),
                 reads=[t_PSB[0], t_eps], writes=[t_lnv])
            P.op("act", (lambda e: e.activation(rstd[:], lnv[:], AF.Exp, scale=-0.5)), reads=[t_lnv], writes=[t_rstd])
            for c in range(8):
                P.op("dve", (lambda e, c=c, k=k: e.scalar_tensor_tensor(ob[k][:, c, :], hT[k][:, c, :], gfs[:, c:c + 1], rstd[:], ALU.mult, ALU.mult)),
                     reads=[t_hT[k], t_gf, t_rstd], writes=[t_ob[k]])
            P.dma("sp", io["outT"][i].rearrange("(c p) t -> p c t", p=128), ob[k][:], f"oo{k}", reads=[t_ob[k]])
    P.barrier()
    P.release(m0)


def build_fused(nstages=4, a_kw=None, c_kw=None):
    nc = bass.Bass("TRN2", target_bir_lowering=False)
    din = lambda n, s_, d=F32: nc.dram_tensor(n, s_, d, kind="ExternalInput").ap()
    dint = lambda n, s_, d: nc.dram_tensor(n, s_, d, kind="Internal").ap()
    a_hT = din("a_hT", [D, S]); a_w = din("a_w", [D, NWA]); a_ng = din("a_ng", [128, 8])
    gateb = din("gateb", [1, 4]); mlg = din("mlg", [1, 256]); tri = din("tri", [2, 128, 128])
    nab = din("nab", [5, 128, 20, 128]); nam = din("nam", [5, 128, 5, 128]); ident = din("ident", [128, 128], BF16)
    b_wout = din("b_wout", [2048, D]); b_xres = din("b_xres", [8, D, 256])
    c_w = din("c_w", [D, NWC]); c_ng = din("c_ng", [128, 8]); qkg = din("qkg", [2, 128]); cs = din("cs", [S, 128]); cwb = din("cwb", [128, 8])
    d_wout = din("d_wout", [2048, D]); gfin = din("gfin", [128, 8])
    outT = nc.dram_tensor("outT", [8, D, 256], F32, kind="ExternalOutput").ap()
    mp0 = [dint(f"mp0_{i}", [512, 1024], BF16) for i in range(8)]
    mg0 = [dint(f"mg0_{i}", [2048, 1024], BF16) for i in range(8)]
    mp1 = [dint(f"mp1_{i}", [512, 1024], BF16) for i in range(8)]
    mg1 = [dint(f"mg1_{i}", [2048, 1024], BF16) for i in range(8)]
    hp = [dint(f"hp_{i}", [D, 256], F32) for i in range(8)]
    hgl = [dint(f"hg_{i}", [4 * D, 256], F32) for i in range(8)]
    t_mg0 = [Tok() for _ in range(8)]; t_mg1 = [Tok() for _ in range(8)]
    t_hp = [Tok() for _ in range(8)]; t_hg = [Tok() for _ in range(8)]

    P = Prog(nc)
    PSB = [P.ps([128, 512]) for _ in range(8)]
    t_PSB = [XTok() for _ in range(8)]
    PX = PSB[7][:].bitcast(BF16); t_PX = t_PSB[7]

    m = P.mark()
    emit_mixer0(nc, P, PSB, t_PSB, PX, t_PX, dict(hT=a_hT, w=a_w, ng=a_ng, gateb=gateb, mlg=mlg, tri=tri, nab=nab, nam=nam, ident=ident,
                                                 mp=mp0, mg=mg0, t_mg=t_mg0), **(a_kw or {}))
    P.barrier(); P.release(m)
    if nstages >= 2:
        emit_outproj(nc, P, PSB, t_PSB, dict(mg=mg0, t_mg=t_mg0, wout=b_wout, xres=b_xres, hp=hp, t_hp=t_hp, hg=hgl, t_hg=t_hg,
                                             order=[3, 4, 2, 5, 1, 6, 0, 7]), final=False)
    m = P.mark()
    if nstages >= 3:
        emit_mixer1(nc, P, PSB, t_PSB, PX, t_PX, dict(hg=hgl, t_hg=t_hg, w=c_w, ng=c_ng, qkg=qkg, cs=cs, cwb=cwb, ident=ident,
                                                     mp=mp1, mg=mg1, t_mg=t_mg1, order=[3, 4, 2, 5, 1, 6, 0, 7]), **(c_kw or {}))
    P.barrier(); P.release(m)
    if nstages >= 4:
        emit_outproj(nc, P, PSB, t_PSB, dict(mg=mg1, t_mg=t_mg1, wout=d_wout, hp_in=hp, t_hp_in=t_hp, gfin=gfin, outT=outT), final=True)
    P.build()
    return nc


def kernel(x, norm_g, final_g, ev_w_in, ev_gate_b, ev_w_out, ev_ml_norm_g, ev_na_rpb,
           od_w_in, od_w_out, od_q_norm_g, od_k_norm_g, od_conv_w, od_conv_b):
    f = lambda a: np.asarray(a, dtype=np.float32)
    x, norm_g, final_g = f(x), f(norm_g), f(final_g)
    ev_w_in, ev_gate_b, ev_w_out, ev_ml_norm_g, ev_na_rpb = f(ev_w_in)[0], f(ev_gate_b)[0], f(ev_w_out)[0], f(ev_ml_norm_g)[0], f(ev_na_rpb)[0]
    od_w_in, od_w_out, qg, kg, conv_w, conv_b = f(od_w_in)[0], f(od_w_out)[0], f(od_q_norm_g)[0], f(od_k_norm_g)[0], f(od_conv_w)[0], f(od_conv_b)[0]
    nc = build_fused()
    ident = np.eye(128, dtype=np.float32).astype(ml_dtypes.bfloat16)
    s_ = np.arange(128)
    tri = np.stack([(s_[:, None] <= s_[None, :]), (s_[:, None] >= s_[None, :])]).astype(np.float32)
    cs = rope_tables()
    xT = [np.ascontiguousarray(x[b].T) for b in range(2)]
    r256 = np.arange(256)
    in_maps = []
    for core in range(NCORES):
        b, hg = core // 4, core % 4
        cols0 = np.concatenate([
            0 + hg * 256 + r256, 1024 + hg * 256 + r256, 2048 + hg * 256 + r256, 3072 + hg * 256 + r256,
            4096 + hg * 256 + r256, 5136 + 3072 + hg * 256 + r256, 5136 + hg * 256 + r256, 5136 + 1024 + hg * 256 + r256,
            5136 + 2048 + hg * 256 + r256, 5120 + np.array([hg, 4 + hg, 8 + hg, 12 + hg])])
        gb = ev_gate_b[np.array([hg, 4 + hg, 8 + hg, 12 + hg])].reshape(1, 4)
        bias, mask = na_bias_tables(ev_na_rpb[4 * hg:4 * hg + 4])
        kvh = hg // 2
        cols1 = np.concatenate([
            hg * 256 + r256, 1024 + kvh * 128 + np.arange(128), 1280 + kvh * 128 + np.arange(128),
            1536 + hg * 256 + r256, 2560 + hg * 256 + r256, 3584 + hg * 256 + r256, 4608 + hg * 256 + r256, 5632 + hg * 256 + r256])
        cwb = np.zeros((128, 8), np.float32)
        for cc in range(2):
            ch = hg * 256 + cc * 128 + np.arange(128)
            for i in range(3):
                cwb[:, cc * 4 + i] = conv_w[i, ch]
            cwb[:, cc * 4 + 3] = conv_b[ch]
        xres = np.stack([xT[b][:, (4 * i + hg) * 256:(4 * i + hg + 1) * 256] for i in range(8)])
        in_maps.append({
            "a_hT": xT[b], "a_w": np.ascontiguousarray(ev_w_in[:, cols0]), "a_ng": np.ascontiguousarray(norm_g[0].reshape(8, 128).T),
            "gateb": np.ascontiguousarray(gb), "mlg": np.ascontiguousarray(ev_ml_norm_g[hg * 256:(hg + 1) * 256].reshape(1, 256)),
            "tri": tri, "nab": bias, "nam": mask, "ident": ident,
            "b_wout": np.ascontiguousarray(ev_w_out), "b_xres": np.ascontiguousarray(xres),
            "c_w": np.ascontiguousarray(od_w_in[:, cols1]), "c_ng": np.ascontiguousarray(norm_g[1].reshape(8, 128).T),
            "qkg": np.stack([qg, kg]), "cs": cs, "cwb": cwb,
            "d_wout": np.ascontiguousarray(od_w_out), "gfin": np.ascontiguousarray(final_g.reshape(8, 128).T)})
    res = run_bass_kernel_spmd(nc, in_maps, core_ids=list(range(NCORES)))
    out = np.empty((2, S, D), np.float32)
    for core in range(NCORES):
        b, tq = core // 4, core % 4
        o = res.results[core]["outT"]
        for i in range(8):
            out[b, (4 * i + tq) * 256:(4 * i + tq + 1) * 256, :] = o[i].T
    return out
```

```python
import numpy as np
import ml_dtypes
from contextlib import ExitStack
import concourse.bass as bass
import concourse.mybir as mybir
from concourse.bass_utils import run_bass_kernel_spmd

F32 = mybir.dt.float32
BF16 = mybir.dt.bfloat16
AF = mybir.ActivationFunctionType
ALU = mybir.AluOpType
AX = mybir.AxisListType

S = 8192
D = 1024
NCORES = 8
GROUPS = [[0, 1, 2, 3], [4, 5, 6, 7]]
EPS = 1e-6


class Tok:
    __slots__ = ("w", "r", "x")

    def __init__(self, x=False):
        self.w = None
        self.r = {}
        self.x = x


def XTok():
    return Tok(True)


class Prog:
    ENG = ("pe", "act", "dve", "pool", "sp")

    def __init__(self, nc):
        self.nc = nc
        self.ins = {e: [] for e in self.ENG}
        self.seen = {e: {} for e in self.ENG}
        self.dma_cnt = {}
        self.stack = ExitStack()
        self._n = 0

    SB_BASE = 16512
    SB_TOP = 229376

    def sb(self, shape, dt, name=None):
        self._n += 1
        nb = int(np.prod(shape[1:])) * (4 if dt == F32 else 2)
        nb = (nb + 31) // 32 * 32
        off = getattr(self, "sbp", self.SB_BASE)
        assert off + nb <= self.SB_TOP, f"SBUF overflow: {off + nb}"
        self.sbp = off + nb
        return self.nc.alloc_sbuf_tensor_at(name or f"sb{self._n}", list(shape), dt, offset=off)

    def mark(self):
        return getattr(self, "sbp", self.SB_BASE)

    def release(self, m):
        self.sbp = m

    def ps(self, shape, dt=F32, name=None):
        self._n += 1
        return self.stack.enter_context(self.nc.psum_tensor(name or f"ps{self._n}", list(shape), dt))

    def _deps(self, eng, reads, writes):
        deps = {}
        def add(ev):
            key = ev[1]
            if key not in deps or deps[key][2] < ev[2]:
                deps[key] = ev
        for t in reads:
            if t.w is not None:
                add(t.w)
        for t in writes:
            if t.w is not None:
                add(t.w)
            for ev in t.r.values():
                add(ev)
        waits = []
        seen = self.seen[eng]
        for key, ev in deps.items():
            if ev[0] == "c" and key == eng and eng == "pe":
                continue
            if seen.get(key, -1) >= ev[2]:
                continue
            seen[key] = ev[2]
            waits.append(ev)
        return waits

    def op(self, eng, fn, reads=(), writes=()):
        xs = [t for t in reads if t.x]
        if xs:
            reads = [t for t in reads if not t.x]
            writes = list(writes) + xs
        waits = self._deps(eng, reads, writes)
        idx = len(self.ins[eng])
        ev = ("c", eng, idx)
        for t in reads:
            t.r[eng] = ev
        for t in writes:
            t.w = ev
            t.r = {}
        self.ins[eng].append([fn, waits, False, None])

    def _alias(self, sem):
        al = self.__dict__.setdefault("sem_alias", {})
        if sem not in al:
            al[sem] = f"d{len(al)}"
        return al[sem]

    def dma(self, eng, out, in_, sem, reads=(), writes=(), **kw):
        if eng == "sp" and type(out.tensor).__name__ == "DRamTensorHandle":
            eng = "pool"
        sem = self._alias(sem)
        waits = self._deps(eng, reads, writes)
        self.dma_cnt[sem] = self.dma_cnt.get(sem, 0) + 16
        ev = ("d", sem, self.dma_cnt[sem])
        for t in reads:
            t.r[sem] = ev
        for t in writes:
            t.w = ev
            t.r = {}
        fn = lambda e: e.dma_start(out=out, in_=in_, **kw)
        self.ins[eng].append([fn, waits, None, sem])

    def dma_fn(self, eng, fn, sem, reads=(), writes=()):
        sem = self._alias(sem)
        waits = self._deps(eng, reads, writes)
        self.dma_cnt[sem] = self.dma_cnt.get(sem, 0) + 16
        ev = ("d", sem, self.dma_cnt[sem])
        for t in reads:
            t.r[sem] = ev
        for t in writes:
            t.w = ev
            t.r = {}
        self.ins[eng].append([fn, waits, None, sem])

    def coll(self, kind, groups, in_ap, out_ap, sem, reads=(), writes=()):
        eng = "pool"
        sem = "coll_" + sem
        self.__dict__.setdefault("coll_sems", set()).add(sem)
        waits = self._deps(eng, reads, writes)
        self.dma_cnt[sem] = self.dma_cnt.get(sem, 0) + 1
        ev = ("d", sem, self.dma_cnt[sem])
        for t in reads:
            t.r[sem] = ev
        for t in writes:
            t.w = ev
            t.r = {}
        fn = lambda e: e.collective_compute(kind, ALU.bypass, replica_groups=groups, ins=[in_ap.opt()], outs=[out_ap.opt()])
        self.ins[eng].append([fn, waits, None, (sem, 1)])

    def barrier(self):
        last = {}
        for e in self.ENG:
            for i in range(len(self.ins[e]) - 1, -1, -1):
                if self.ins[e][i][0] is not None and self.ins[e][i][3] is None:
                    last[e] = i
                    break
        for e in self.ENG:
            waits = []
            seen = self.seen[e]
            for src, idx in last.items():
                if src == e or seen.get(src, -1) >= idx:
                    continue
                seen[src] = idx
                waits.append(("c", src, idx))
            for sem, val in self.dma_cnt.items():
                if sem in self.__dict__.get("coll_sems", ()):
                    continue
                if seen.get(sem, -1) >= val:
                    continue
                seen[sem] = val
                waits.append(("d", sem, val))
            self.ins[e].append([None, waits, False, None])
        self.sem_alias = {}

    def build(self):
        nc = self.nc
        for e in self.ENG:
            for rec in self.ins[e]:
                for ev in rec[1]:
                    if ev[0] == "c":
                        self.ins[ev[1]][ev[2]][2] = True
        semval = {}
        for e in self.ENG:
            c = 0
            for i, rec in enumerate(self.ins[e]):
                if rec[2]:
                    c += 1
                    semval[(e, i)] = c
        names = [e for e in self.ENG if e != "sp"] + sorted(self.dma_cnt)
        sems = {n: self.stack.enter_context(nc.semaphore("s_" + n)) for n in names}
        final_dma = dict(self.dma_cnt)

        def replay(ename, eobj, last=False):
            for i, (fn, waits, sig, dsem) in enumerate(self.ins[ename]):
                for ev in waits:
                    if ev[0] == "c":
                        eobj.wait_ge(sems[ev[1]], semval[(ev[1], ev[2])])
                    else:
                        eobj.wait_ge(sems[ev[1]], ev[2])
                if fn is None:
                    continue
                inst = fn(eobj)
                if isinstance(dsem, tuple):
                    inst.then_inc(sems[dsem[0]], dsem[1])
                elif dsem is not None:
                    inst.then_inc(sems[dsem], 16)
                elif sig:
                    inst.then_inc(sems[ename], 1)
            if last:
                for n, v in final_dma.items():
                    eobj.wait_ge(sems[n], v)

        with nc.Block() as block:
            @block.tensor
            def _(e):
                replay("pe", e)

            @block.scalar
            def _(e):
                replay("act", e)

            @block.vector
            def _(e):
                replay("dve", e)

            @block.gpsimd
            def _(e):
                replay("pool", e)

            @block.sync
            def _(e):
                replay("sp", e, last=True)
        self.stack.close()


def w_join(P, toks, t_out):
    wj = P.sb([128, 1], F32)
    P.op("dve", lambda e: e.memset(wj[:], 0.0), reads=toks, writes=[t_out])


def _bf(a):
    return np.ascontiguousarray(a)


class NormCtx:
    def __init__(self, P, hTv, ngs, t_ng, ones_bf, t_ones, epsb, t_eps, ps_ss, t_ps_ss, blk=512):
        self.P = P
        self.blk = blk
        self.hTv = hTv
        self.ngs, self.t_ng = ngs, t_ng
        self.ones_bf, self.t_ones = ones_bf, t_ones
        self.epsb, self.t_eps = epsb, t_eps
        self.ps_ss, self.t_ps_ss = ps_ss, t_ps_ss
        self.xt = [P.sb([128, 8, blk], F32) for _ in range(2)]
        self.t_xt = [Tok(), Tok()]
        self.xsq = P.sb([128, 8, blk], BF16)
        self.t_xsq = Tok()
        self.lnv = P.sb([128, blk], F32)
        self.rstd = P.sb([128, blk], F32)
        self.t_lnv, self.t_rstd = Tok(), Tok()
        self.uT = [P.sb([128, 8, blk], BF16) for _ in range(2)]
        self.t_uT = [Tok(), Tok()]

    def load(self, j):
        k = j % 2
        if self.hTv is None:
            self.loader(j, self.xt[k], f"xt{k}", self.t_xt[k])
        else:
            self.P.dma("sp", self.xt[k][:], self.hTv[:, :, j * self.blk:(j + 1) * self.blk], f"xt{k}", writes=[self.t_xt[k]])

    def norm(self, j):
        P = self.P
        k = j % 2
        xt, xsq, uT = self.xt[k], self.xsq, self.uT[k]
        P.op("act", lambda e: e.activation(xsq[:], xt[:], AF.Square), reads=[self.t_xt[k]], writes=[self.t_xsq])
        for c in range(8):
            P.op("pe", (lambda e, c=c: e.matmul(self.ps_ss[:, 0:self.blk], self.ones_bf[:], xsq[:, c, :], start=(c == 0), stop=(c == 7))),
                 reads=[self.t_ones, self.t_xsq], writes=[self.t_ps_ss])
        P.op("act", lambda e: e.activation(self.lnv[:], self.ps_ss[:, 0:self.blk], AF.Ln, bias=self.epsb[:, 0:1], scale=1.0 / D),
             reads=[self.t_ps_ss, self.t_eps], writes=[self.t_lnv])
        P.op("act", lambda e: e.activation(self.rstd[:], self.lnv[:], AF.Exp, scale=-0.5), reads=[self.t_lnv], writes=[self.t_rstd])
        for c in range(8):
            P.op("dve", (lambda e, c=c: e.scalar_tensor_tensor(uT[:, c, :], xt[:, c, :], self.ngs[:, c:c + 1], self.rstd[:],
                                                              ALU.mult, ALU.mult)),
                 reads=[self.t_xt[k], self.t_ng, self.t_rstd], writes=[self.t_uT[k]])
        return uT, self.t_uT[k]


NWC = 1792


def emit_mixer1(nc, P, PSB, t_PSB, PXg, t_PXg, io, nblk=16, do_v=True, do_g=True, sub=99):
    hgl, t_hg, w, ng, qkg, cs, cwb, ident = (io[k] for k in ("hg", "t_hg", "w", "ng", "qkg", "cs", "cwb", "ident"))
    mp, mg = io["mp"], io["mg"]
    dint = lambda n, s_, d: nc.dram_tensor(n, s_, d, kind="Internal").ap()
    qT_s = dint("qT_s", [2, 128, S], BF16)
    z_s = dint("z_s", [S, 256], F32)
    xc_s = dint("xc_s", [256, S], F32)
    gz_s = dint("gz_s", [256, S], F32)
    t_mpp = [[Tok() for _ in range(6)] for _ in range(8)]

    wb = P.sb([128, 8, NWC], BF16)
    t_wb = Tok()
    wv = w.rearrange("(c p) n -> p c n", p=128)
    _tw = []
    for c in range(8):
        _tw.append(Tok())
        P.dma("pool", wb[:, c, :], wv[:, c, :], "wb", writes=[_tw[-1]])
    w_join(P, _tw, t_wb)
    ngs = P.sb([128, 8], F32); t_ng = Tok()
    P.dma("sp", ngs[:], ng[:, :], "c_ng", writes=[t_ng])
    qg = P.sb([128, 2, 128], F32); t_qg = Tok()
    P.dma("sp", qg[:, 0, :], qkg[0:1, :].to_broadcast([128, 128]), "c_qg", writes=[t_qg])
    P.dma("sp", qg[:, 1, :], qkg[1:2, :].to_broadcast([128, 128]), "c_qg", writes=[t_qg])
    P.op("dve", lambda e: e.tensor_scalar(qg[:, 0, :], qg[:, 0, :], float(128 ** -0.5), None, ALU.mult), reads=[t_qg], writes=[t_qg])
    cw = P.sb([128, 8], F32); t_cw = Tok()
    P.dma("sp", cw[:], cwb[:, :], "c_cw", writes=[t_cw])
    idb = P.sb([128, 128], BF16); t_id = Tok()
    P.dma("sp", idb[:], ident[:, :], "c_id", writes=[t_id])
    ones_bf = P.sb([128, 128], BF16); t_ones = Tok()
    P.op("pool", lambda e: e.memset(ones_bf[:], 1.0), writes=[t_ones])
    epsb = P.sb([128, 2], F32); t_eps = Tok()
    P.op("pool", lambda e: e.memset(epsb[:], EPS), writes=[t_eps])
    kT = P.sb([128, S], BF16)
    t_kT = [Tok() for _ in range(64)]
    vx = P.sb([128, 64, 129], BF16)
    t_vx = [Tok() for _ in range(64)]
    t_vones = Tok()
    P.op("pool", lambda e: e.memset(vx[:, :, 128:129], 1.0), writes=[t_vones])
    pT = [[PSB[1], PSB[2]], [PSB[3], PSB[4]]]
    t_pT = [[t_PSB[1], t_PSB[2]], [t_PSB[3], t_PSB[4]]]
    pB = [PSB[0], PSB[5], PSB[6]]
    t_pB = [t_PSB[0], t_PSB[5], t_PSB[6]]
    pX = PXg
    t_pX = [t_PXg, t_PXg]

    nctx = NormCtx(P, None, ngs, t_ng, ones_bf, t_ones, epsb, t_eps, pB[0], t_pB[0])

    def _loader(j, xt, sem, tok):
        i, r0 = j // 2, (j % 2) * 2
        hv = hgl[i].rearrange("(r c p) t -> p r c t", r=4, p=128)
        for rr in range(2):
            P.dma("sp", xt[:, :, rr * 256:(rr + 1) * 256], hv[:, r0 + rr, :, :], sem, reads=[t_hg[i]], writes=[tok])
    nctx.loader = _loader
    csb = [P.sb([128, 4, 128], F32) for _ in range(2)]; t_cs = [Tok(), Tok()]
    st = [P.sb([128, 12], F32) for _ in range(2)]; t_st = [Tok(), Tok()]
    junk = P.sb([128, 128], F32); t_junk = Tok()
    xn = [P.sb([128, 3, 128], F32) for _ in range(2)]; t_xn = [Tok(), Tok()]
    tmp = [P.sb([128, 3, 64], F32) for _ in range(4)]; t_tmp = [Tok() for _ in range(4)]
    rot = [P.sb([128, 3, 128], BF16) for _ in range(2)]; t_rot = [Tok(), Tok()]
    qTst = [P.sb([128, 2, 512], BF16) for _ in range(2)]; t_qTst = [Tok(), Tok()]
    zst = [P.sb([128, 4, 256], F32) for _ in range(2)]; t_zst = [Tok(), Tok()]
    ccs = P.sb([128, 512], F32); t_ccs = Tok()
    sis = P.sb([128, 512], F32); t_sis = Tok()
    xcst = [P.sb([128, 2, 512], F32) for _ in range(2)]; t_xcst = [Tok(), Tok()]
    gzst = [P.sb([128, 2, 512], F32) for _ in range(2)]; t_gzst = [Tok(), Tok()]
    t_qs = [Tok() for _ in range(16)]
    t_zs = [Tok() for _ in range(16)]
    t_xcs = [Tok() for _ in range(16)]
    t_gzs = [Tok() for _ in range(16)]
    csv = cs.rearrange("(n p) f -> p n f", p=128)
    z_sv = z_s.rearrange("(n p) f -> p n f", p=128)
    qT_sv = qT_s.rearrange("h d t -> d h t")
    xc_sv = xc_s.rearrange("(c p) t -> p c t", p=128)
    gz_sv = gz_s.rearrange("(c p) t -> p c t", p=128)

    def r4(ap):
        return ap.rearrange("p h (a r i) -> p h a r i", a=2, r=2)

    jorder = [2 * p + q for p in io.get("order", list(range(8))) for q in range(2)][:nblk] if nblk == 16 else list(range(nblk))
    nctx.load(jorder[0])
    pend_tr = None
    for jn, j in enumerate(jorder):
        kb = j % 2
        jnext = jorder[jn + 1] if jn + 1 < len(jorder) else None
        if jnext is not None:
            nctx.load(jnext)
        P.dma("sp", csb[kb][:], csv[:, 4 * j:4 * j + 4, :], f"cs{kb}", writes=[t_cs[kb]])
        if jn == 0:
            pend = nctx.norm(j)
        uT, t_uT = pend
        def fm(m, pi):
            for c in range(8):
                P.op("pe", (lambda e, c=c, m=m, pi=pi, uT=uT: e.matmul(pB[pi][:], wb[:, c, 768 + m * 128:768 + (m + 1) * 128], uT[:, c, :],
                                                                 start=(c == 0), stop=(c == 7))),
                     reads=[t_uT, t_wb], writes=[t_pB[pi]])

        def fm_part(part):
            cc = part // 2
            if part % 2 == 0:
                fm(2 + cc, 1)
                P.op("act", lambda e: e.activation(ccs[:], pB[1][:], AF.Copy), reads=[t_pB[1]], writes=[t_ccs])
                fm(4 + cc, 2)
                P.op("dve", (lambda e, cc=cc, kb=kb: e.tensor_tensor(xcst[kb][:, cc, :], pB[2][:], ccs[:], ALU.mult)),
                     reads=[t_pB[2], t_ccs], writes=[t_xcst[kb]])
            else:
                fm(6 + cc, 1)
                P.op("act", lambda e: e.activation(sis[:], pB[1][:], AF.Silu), reads=[t_pB[1]], writes=[t_sis])
                fm(0 + cc, 2)
                P.op("dve", (lambda e, cc=cc, kb=kb: e.tensor_tensor(gzst[kb][:, cc, :], pB[2][:], sis[:], ALU.mult)),
                     reads=[t_pB[2], t_sis], writes=[t_gzst[kb]])
        for tc in range(4):
            ch = 4 * j + tc
            k = ch % 2
            ps = pT[k]
            if tc == 1 and jnext is not None:
                pend = nctx.norm(jnext)
            for (half, c0, c1) in ((0, 0, 512), (1, 512, 768)):
                for c in range(8):
                    P.op("pe", (lambda e, c=c, half=half, c0=c0, c1=c1, ps=ps, tc=tc, uT=uT:
                                e.matmul(ps[half][:, 0:c1 - c0], uT[:, c, tc * 128:(tc + 1) * 128], wb[:, c, c0:c1],
                                         start=(c == 0), stop=(c == 7))),
                         reads=[t_uT, t_wb], writes=[t_pT[k][half]])
            fm_part(tc)
            if pend_tr is not None:
                pend_tr()
            for h in range(3):
                P.op("act", (lambda e, h=h, ps=ps, k=k: e.activation(junk[:], ps[0][:, h * 128:(h + 1) * 128], AF.Square,
                                                                     accum_out=st[k][:, h:h + 1])),
                     reads=[t_pT[k][0]], writes=[t_junk, t_st[k]])
            P.op("act", (lambda e, k=k: e.activation(st[k][:, 4:7], st[k][:, 0:3], AF.Ln, bias=epsb[:, 0:1], scale=1.0 / 128)),
                 reads=[t_st[k], t_eps], writes=[t_st[k]])
            P.op("act", (lambda e, k=k: e.activation(st[k][:, 8:11], st[k][:, 4:7], AF.Exp, scale=-0.5)),
                 reads=[t_st[k]], writes=[t_st[k]])
            for h in range(3):
                P.op("dve", (lambda e, h=h, ps=ps, k=k: e.scalar_tensor_tensor(
                    xn[k][:, h, :], ps[0][:, h * 128:(h + 1) * 128], st[k][:, 8 + h:9 + h], qg[:, 1 if h == 2 else 0, :],
                    ALU.mult, ALU.mult)), reads=[t_pT[k][0], t_st[k], t_qg], writes=[t_xn[k]])
            P.op("act", (lambda e, ps=ps, ch=ch: e.activation(vx[:, ch, 0:128], ps[0][:, 384:512], AF.Copy)),
                 reads=[t_pT[k][0], t_vones], writes=[t_vx[ch]])
            P.op("act", (lambda e, ps=ps, tc=tc, kb=kb: e.activation(zst[kb][:, tc, :], ps[1][:, 0:256], AF.Silu)),
                 reads=[t_pT[k][1]], writes=[t_zst[kb]])
            x4 = r4(xn[k][:])
            o4 = r4(rot[k][:])
            x1, x2 = x4[:, :, :, 0, :], x4[:, :, :, 1, :]
            cosv = csb[kb][:, tc, 0:64].rearrange("p (a i) -> p a i", a=2).unsqueeze(1).to_broadcast([128, 3, 2, 32])
            sinv = csb[kb][:, tc, 64:128].rearrange("p (a i) -> p a i", a=2).unsqueeze(1).to_broadcast([128, 3, 2, 32])
            tv = [t[:].rearrange("p h (a i) -> p h a i", a=2) for t in tmp]
            P.op("dve", (lambda e, x1=x1, cosv=cosv, tv=tv: e.tensor_tensor(tv[0], x1, cosv, ALU.mult)),
                 reads=[t_xn[k], t_cs[kb]], writes=[t_tmp[0]])
            P.op("dve", (lambda e, x2=x2, sinv=sinv, tv=tv: e.tensor_tensor(tv[1], x2, sinv, ALU.mult)),
                 reads=[t_xn[k], t_cs[kb]], writes=[t_tmp[1]])
            P.op("dve", (lambda e, o4=o4, tv=tv: e.tensor_tensor(o4[:, :, :, 0, :], tv[0], tv[1], ALU.subtract)),
                 reads=[t_tmp[0], t_tmp[1]], writes=[t_rot[k]])
            P.op("dve", (lambda e, x1=x1, sinv=sinv, tv=tv: e.tensor_tensor(tv[2], x1, sinv, ALU.mult)),
                 reads=[t_xn[k], t_cs[kb]], writes=[t_tmp[2]])
            P.op("dve", (lambda e, x2=x2, cosv=cosv, tv=tv: e.tensor_tensor(tv[3], x2, cosv, ALU.mult)),
                 reads=[t_xn[k], t_cs[kb]], writes=[t_tmp[3]])
            P.op("dve", (lambda e, o4=o4, tv=tv: e.tensor_tensor(o4[:, :, :, 1, :], tv[2], tv[3], ALU.add)),
                 reads=[t_tmp[2], t_tmp[3]], writes=[t_rot[k]])
            def tr(k=k, ch=ch, kb=kb, tc=tc, j=j):
                for h in range(3):
                    P.op("pe", (lambda e, h=h, k=k: e.transpose(pX[:, k * 512 + h * 128:k * 512 + (h + 1) * 128], rot[k][:, h, :], idb[:])),
                         reads=[t_rot[k], t_id], writes=[t_pX[k]])
                P.op("act", (lambda e, k=k, ch=ch: e.activation(kT[:, ch * 128:(ch + 1) * 128], pX[:, k * 512 + 256:k * 512 + 384], AF.Copy)),
                     reads=[t_pX[k]], writes=[t_kT[ch]])
                P.op("dve", (lambda e, k=k, kb=kb, tc=tc: e.tensor_copy(
                    qTst[kb][:, :, tc * 128:(tc + 1) * 128], pX[:, k * 512:k * 512 + 256].rearrange("p (h t) -> p h t", h=2))),
                     reads=[t_pX[k]], writes=[t_qTst[kb]])
                if tc == 3:
                    P.dma("sp", qT_sv[:, :, j * 512:(j + 1) * 512], qTst[kb][:], f"qo{kb}", reads=[t_qTst[kb]], writes=[t_qs[j]])
            pend_tr = tr
        P.dma("sp", z_sv[:, 4 * j:4 * j + 4, :], zst[kb][:], f"zo{kb}", reads=[t_zst[kb]], writes=[t_zs[j]])
        P.dma("sp", xc_sv[:, :, j * 512:(j + 1) * 512], xcst[kb][:], f"xo{kb}", reads=[t_xcst[kb]], writes=[t_xcs[j]])
        P.dma("sp", gz_sv[:, :, j * 512:(j + 1) * 512], gzst[kb][:], f"go{kb}", reads=[t_gzst[kb]], writes=[t_gzs[j]])
    pend_tr()

    xcv = [P.sb([128, 2, 514], F32) for _ in range(2)]; t_xcv = [Tok(), Tok()]
    gzv = [P.sb([128, 2, 512], F32) for _ in range(2)]; t_gzv = [Tok(), Tok()]
    cv1 = P.sb([128, 512], F32); t_cv1 = Tok()
    cv2 = P.sb([128, 512], F32); t_cv2 = Tok()
    cvo = [P.sb([128, 2, 512], BF16) for _ in range(2)]; t_cvo = [Tok(), Tok()]
    for j in range(16 if do_v else 0):
        kb = j % 2
        lo = max(j * 512 - 1, 0)
        hi = min(j * 512 + 513, S)
        o0 = lo - (j * 512 - 1)
        rd = [t_xcs[j]] + ([t_xcs[j - 1]] if j > 0 else []) + ([t_xcs[j + 1]] if j < 15 else [])
        if j == 0:
            P.op("pool", lambda e: e.memset(xcv[0][:, :, 0:1], 0.0), writes=[t_xcv[0]])
        if j == 15:
            P.op("pool", lambda e: e.memset(xcv[1][:, :, 513:514], 0.0), writes=[t_xcv[1]])
        P.dma("sp", xcv[kb][:, :, o0:o0 + hi - lo], xc_sv[:, :, lo:hi], f"xv{kb}", reads=rd, writes=[t_xcv[kb]])
        P.dma("sp", gzv[kb][:], gz_sv[:, :, j * 512:(j + 1) * 512], f"gv{kb}", reads=[t_gzs[j]], writes=[t_gzv[kb]])
        for cc in range(2):
            w0, w1, w2, bb = (cw[:, cc * 4 + i:cc * 4 + i + 1] for i in range(4))
            P.op("dve", (lambda e, kb=kb, cc=cc, w0=w0, bb=bb: e.tensor_scalar(cv1[:], xcv[kb][:, cc, 0:512], w0, bb, ALU.mult, ALU.add)),
                 reads=[t_xcv[kb], t_cw], writes=[t_cv1])
            P.op("dve", (lambda e, kb=kb, cc=cc, w1=w1: e.scalar_tensor_tensor(cv2[:], xcv[kb][:, cc, 1:513], w1, cv1[:], ALU.mult, ALU.add)),
                 reads=[t_xcv[kb], t_cw, t_cv1], writes=[t_cv2])
            P.op("dve", (lambda e, kb=kb, cc=cc, w2=w2: e.scalar_tensor_tensor(cv1[:], xcv[kb][:, cc, 2:514], w2, cv2[:], ALU.mult, ALU.add)),
                 reads=[t_xcv[kb], t_cw, t_cv2], writes=[t_cv1])
            P.op("dve", (lambda e, kb=kb, cc=cc: e.tensor_tensor(cvo[kb][:, cc, :], cv1[:], gzv[kb][:, cc, :], ALU.mult)),
                 reads=[t_cv1, t_gzv[kb]], writes=[t_cvo[kb]])
        P.dma("sp", mp[j // 2].rearrange("(a p) t -> p a t", p=128)[:, 2:4, (j % 2) * 512:(j % 2 + 1) * 512], cvo[kb][:], f"co{kb}",
              reads=[t_cvo[kb]], writes=[t_mpp[j // 2][4 + j % 2]])

    qTb = [P.sb([128, 512], BF16) for _ in range(2)]; t_qTb = [Tok(), Tok()]
    szb = [P.sb([128, 4, 128], F32) for _ in range(2)]; t_szb = [Tok(), Tok()]
    pt = [P.sb([128, 512], BF16) for _ in range(3)]; t_pt = [Tok() for _ in range(3)]
    rden = P.sb([128, 4], F32); t_rden = Tok()
    aout = [P.sb([128, 4, 128], BF16) for _ in range(2)]; t_aout = [Tok(), Tok()]
    pS = [pT[0][0][:], pT[0][1][:]]
    t_pS = t_pT[0]
    pO = [pT[1][0][:], pT[1][1][:], pB[1][:], pB[2][:]]
    t_pO = [t_pT[1][0], t_pT[1][1], t_pB[1], t_pB[2]]
    aoT = [P.sb([128, 4, 128], BF16) for _ in range(2)]; t_aoT = [Tok(), Tok()]
    it = 0
    for hd in range(2 if do_g else 0):
        for qb in range(min(16, sub)):
            kq = it % 2
            it += 1
            P.dma("sp", qTb[kq][:], qT_s[hd][:, qb * 512:(qb + 1) * 512], f"ql{kq}", reads=[t_qs[qb]], writes=[t_qTb[kq]])
            P.dma("sp", szb[kq][:], z_sv[:, 4 * qb:4 * qb + 4, hd * 128:(hd + 1) * 128], f"zl{kq}", reads=[t_zs[qb]], writes=[t_szb[kq]])

            def smm(kc):
                P.op("pe", (lambda e, kc=kc, kq=kq: e.matmul(pS[kc % 2], kT[:, kc * 128:(kc + 1) * 128], qTb[kq][:], start=True, stop=True)),
                     reads=[t_kT[kc], t_qTb[kq]], writes=[t_pS[kc % 2]])
            smm(0)
            for kc in range(64):
                if kc + 1 < 64:
                    smm(kc + 1)
                pk = kc % 3
                P.op("act", (lambda e, kc=kc, pk=pk: e.activation(pt[pk][:], pS[kc % 2], AF.Exp)),
                     reads=[t_pS[kc % 2]], writes=[t_pt[pk]])
                for qs in range(4):
                    P.op("pe", (lambda e, kc=kc, pk=pk, qs=qs: e.matmul(pO[qs][:, 0:129], pt[pk][:, qs * 128:(qs + 1) * 128], vx[:, kc, :],
                                                                       start=(kc == 0), stop=(kc == 63))),
                         reads=[t_pt[pk], t_vx[kc], t_vones], writes=[t_pO[qs]])
            for qs in range(4):
                P.op("dve", (lambda e, qs=qs: e.reciprocal(rden[:, qs:qs + 1], pO[qs][:, 128:129])), reads=[t_pO[qs]], writes=[t_rden])
                P.op("dve", (lambda e, qs=qs, kq=kq: e.scalar_tensor_tensor(aout[kq][:, qs, :], pO[qs][:, 0:128], rden[:, qs:qs + 1],
                                                                          szb[kq][:, qs, :], ALU.mult, ALU.mult)),
                     reads=[t_pO[qs], t_rden, t_szb[kq]], writes=[t_aout[kq]])
            for qs in range(4):
                P.op("pe", (lambda e, qs=qs, kq=kq: e.transpose(pX[:, qs * 128:(qs + 1) * 128], aout[kq][:, qs, :], idb[:])),
                     reads=[t_aout[kq], t_id], writes=[t_pX[0]])
            P.op("act", (lambda e, kq=kq: e.activation(aoT[kq][:], pX[:, 0:512].rearrange("p (a t) -> p a t", a=4), AF.Copy)),
                 reads=[t_pX[0]], writes=[t_aoT[kq]])
            P.dma("sp", mp[qb // 2][hd * 128:(hd + 1) * 128, (qb % 2) * 512:(qb % 2 + 1) * 512], aoT[kq][:].rearrange("p a t -> p (a t)"), f"ao{kq}",
                  reads=[t_aoT[kq]], writes=[t_mpp[qb // 2][hd * 2 + qb % 2]])
            if hd == 1 and qb % 2 == 1:
                P.coll("AllGather", GROUPS, mp[qb // 2], mg[qb // 2], f"cc{(qb // 2) % 4}", reads=t_mpp[qb // 2], writes=[io["t_mg"][qb // 2]])


def rope_tables():
    t = np.arange(S)
    pos = np.stack([t // 64, t % 64], axis=-1).astype(np.float32)
    inv = (np.float32(10000.0) ** (-np.arange(32, dtype=np.float32) / np.float32(32))).astype(np.float32)
    ang = (pos[:, :, None] * inv).astype(np.float32)
    return np.concatenate([np.cos(ang).reshape(S, 64), np.sin(ang).reshape(S, 64)], axis=1).astype(np.float32)


NWA = 2308
NA_CFG_TILES = (0, 1, 2, 62, 63)


def na_tile_cfg(i):
    if i < 2:
        return i, 0, 4
    if i >= 62:
        return 3 + (i - 62), 120, 4
    return 2, 2 * i - 4, 5


def emit_mixer0(nc, P, PSB, t_PSB, PXg, t_PXg, io, do_m=True, do_n=True):
    BLK = 256
    NB = S // BLK
    hT, w, ng, gateb, mlg, tri, nab, nam, ident = (io[k] for k in ("hT", "w", "ng", "gateb", "mlg", "tri", "nab", "nam", "ident"))
    mp, mg = io["mp"], io["mg"]
    dint = lambda n, s, d: nc.dram_tensor(n, s, d, kind="Internal").ap()
    fm_s = dint("fm_s", [64, 128, 768], BF16)
    tm_s = dint("tm_s", [64, 128, 769], BF16)
    g_s = dint("g_s", [64, 128, 768], F32)
    na_s = dint("na_s", [4, 128, S], BF16)
    vna_s = dint("vna_s", [64, 128, 260], BF16)
    h_s = dint("h_s", [2, 64, 128, 256], F32)
    t_mpp = [[Tok() for _ in range(16)] for _ in range(8)]

    wb = P.sb([128, 8, NWA], BF16); t_wb = Tok()
    wv = w.rearrange("(c p) n -> p c n", p=128)
    _tw = []
    for c in range(8):
        for (a, b_) in ((0, 1024), (1024, 2048), (2048, NWA)):
            _tw.append(Tok())
            P.dma("pool", wb[:, c, a:b_], wv[:, c, a:b_], "wb", writes=[_tw[-1]])
    w_join(P, _tw, t_wb)
    ngs = P.sb([128, 8], F32); t_ng = Tok()
    P.dma("sp", ngs[:], ng[:, :], "c_ng", writes=[t_ng])
    gbb = P.sb([128, 4], F32); t_gbb = Tok()
    P.dma("sp", gbb[:], gateb[0:1, :].to_broadcast([128, 4]), "c_gb", writes=[t_gbb])
    mlgb = P.sb([128, 256], F32); t_mlg = Tok()
    P.dma("sp", mlgb[:], mlg[0:1, :].to_broadcast([128, 256]), "c_mlg", writes=[t_mlg])
    trs = P.sb([128, 2, 128], F32); t_tri = Tok()
    P.dma("sp", trs[:], tri.rearrange("a s t -> s a t"), "c_tri", writes=[t_tri])
    idb = P.sb([128, 128], BF16); t_id = Tok()
    P.dma("sp", idb[:], ident[:, :], "c_id", writes=[t_id])
    ones_bf = P.sb([128, 128], BF16); t_ones = Tok()
    P.op("pool", lambda e: e.memset(ones_bf[:], 1.0), writes=[t_ones])
    ones32 = P.sb([128, 128], F32); t_ones32 = Tok()
    P.op("pool", lambda e: e.memset(ones32[:], 1.0), writes=[t_ones32])
    epsb = P.sb([128, 2], F32); t_eps = Tok()
    P.op("pool", lambda e: e.memset(epsb[:, 0:1], EPS), writes=[t_eps])
    P.op("pool", lambda e: e.memset(epsb[:, 1:2], 1.0), writes=[t_eps])
    EA = P.sb([128, 64, 2], F32); t_EA = [Tok() for _ in range(64)]
    EG = P.sb([128, 64, 2], F32); t_EG = [Tok() for _ in range(64)]
    pb, t_pb, pX, t_pX = PSB, t_PSB, PXg, t_PXg

    nctx = NormCtx(P, hT.rearrange("(c p) t -> p c t", p=128), ngs, t_ng, ones_bf, t_ones, epsb, t_eps, pb[0], t_pb[0], blk=BLK)
    g4 = [P.sb([128, 4], F32) for _ in range(2)]; t_g4 = [Tok(), Tok()]
    sm = [P.sb([128, 16], F32) for _ in range(2)]; t_sm = [Tok(), Tok()]
    X5 = [P.sb([128, 5, 256], BF16) for _ in range(2)]; t_X5 = [Tok(), Tok()]
    FMst = [P.sb([128, 768], BF16) for _ in range(2)]; t_FMst = [Tok(), Tok()]
    vml = [P.sb([128, 257], BF16) for _ in range(2)]; t_vml = [Tok(), Tok()]
    GS = [P.sb([128, 3, 256], F32) for _ in range(2)]; t_GS = [Tok(), Tok()]
    Y4 = [P.sb([128, 512], BF16) for _ in range(2)]; t_Y4 = [Tok(), Tok()]
    NAst = [P.sb([128, 4, 128], BF16) for _ in range(2)]; t_NAst = [Tok(), Tok()]
    vst = [P.sb([128, 4, 65], BF16) for _ in range(2)]; t_vst = [Tok(), Tok()]
    t_fm = [Tok() for _ in range(64)]
    t_tm = [Tok() for _ in range(64)]
    t_gs = [Tok() for _ in range(64)]
    t_nas = [Tok() for _ in range(64)]
    t_vna = [Tok() for _ in range(64)]
    for k in range(2):
        P.op("pool", (lambda e, k=k: e.memset(vml[k][:, 256:257], 1.0)), writes=[t_vml[k]])
        P.op("pool", (lambda e, k=k: e.memset(vst[k][:, :, 64:65], 1.0)), writes=[t_vst[k]])
    na_sv = na_s.rearrange("a p t -> p a t")

    def grp(uT, t_uT, tc, c0, c1, bank):
        for c in range(8):
            P.op("pe", (lambda e, c=c: e.matmul(pb[bank][:, 0:c1 - c0], uT[:, c, tc * 128:(tc + 1) * 128], wb[:, c, c0:c1],
                                                start=(c == 0), stop=(c == 7))),
                 reads=[t_uT, t_wb], writes=[t_pb[bank]])

    Et = P.sb([128, 5, 20, 128], BF16); t_E = [Tok() for _ in range(5)]
    btmp = P.sb([128, 20, 128], F32); t_btmp = Tok()
    mtmp = P.sb([128, 5, 128], F32); t_mtmp = Tok()
    for cf in range(5):
        P.dma("sp", btmp[:], nab[cf], "bl", writes=[t_btmp])
        P.dma("sp", mtmp[:], nam[cf], "ml", writes=[t_mtmp])
        P.op("act", (lambda e: e.activation(btmp[:], btmp[:], AF.Exp)), writes=[t_btmp])
        for h in range(4):
            P.op("dve", (lambda e, cf=cf, h=h: e.tensor_tensor(Et[:, cf, h * 5:(h + 1) * 5, :], btmp[:, h * 5:(h + 1) * 5, :], mtmp[:], ALU.mult)),
                 reads=[t_btmp, t_mtmp], writes=[t_E[cf]])
    nctx.load(0)
    pend_nat = None
    for j in range(NB):
        if j + 1 < NB:
            nctx.load(j + 1)
        if j == 0:
            pend = nctx.norm(0)
        uT, t_uT = pend
        for tc in range(BLK // 128):
            ch = j * (BLK // 128) + tc
            k = ch % 2
            if tc == 1 and j + 1 < NB:
                pend = nctx.norm(j + 1)
            if pend_nat is not None:
                pend_nat()
            grp(uT, t_uT, tc, 2048, 2308, 5)
            P.op("dve", (lambda e, k=k: e.tensor_tensor(g4[k][:], pb[5][:, 256:260], gbb[:], ALU.add)),
                 reads=[t_pb[5], t_gbb], writes=[t_g4[k]])
            P.op("act", (lambda e, k=k: e.activation(vst[k][:, :, 0:64], pb[5][:, 0:256].rearrange("p (h d) -> p h d", h=4), AF.Copy)),
                 reads=[t_pb[5]], writes=[t_vst[k]])
            P.dma("sp", vna_s[ch], vst[k][:].rearrange("p h d -> p (h d)"), f"vn{k}", reads=[t_vst[k]], writes=[t_vna[ch]])
            P.op("act", (lambda e, k=k: e.activation(sm[k][:, 0:2], g4[k][:, 1:4:2], AF.Exp, scale=-1.0)),
                 reads=[t_g4[k]], writes=[t_sm[k]])
            P.op("act", (lambda e, k=k: e.activation(sm[k][:, 2:4], sm[k][:, 0:2], AF.Ln, bias=epsb[:, 1:2])),
                 reads=[t_sm[k], t_eps], writes=[t_sm[k]])
            grp(uT, t_uT, tc, 0, 512, 1)
            grp(uT, t_uT, tc, 512, 1024, 2)
            P.op("act", (lambda e, k=k: e.activation(vml[k][:, 0:256], pb[2][:, 0:256], AF.Copy)), reads=[t_pb[2]], writes=[t_vml[k]])
            P.dma("sp", tm_s[ch][:, 512:769], vml[k][:], f"vo{k}", reads=[t_vml[k]], writes=[t_tm[ch]])
            P.op("pe", (lambda e, k=k: e.matmul(pb[6][:, 0:1], trs[:, 0, :], sm[k][:, 2:3], start=True, stop=True)),
                 reads=[t_tri, t_sm[k]], writes=[t_pb[6]])
            P.op("pe", (lambda e, k=k: e.matmul(pb[6][:, 1:2], trs[:, 1, :], sm[k][:, 3:4], start=True, stop=True)),
                 reads=[t_tri, t_sm[k]], writes=[t_pb[6]])
            P.op("pe", (lambda e, k=k: e.matmul(pb[6][:, 2:4], ones32[:], sm[k][:, 2:4], start=True, stop=True)),
                 reads=[t_ones32, t_sm[k]], writes=[t_pb[6]])
            P.op("act", (lambda e, k=k: e.activation(sm[k][:, 4:6], pb[6][:, 0:2], AF.Exp, scale=-1.0)),
                 reads=[t_pb[6]], writes=[t_sm[k]])
            P.op("act", (lambda e, ch=ch: e.activation(EG[:, ch, :], pb[6][:, 2:4], AF.Exp, scale=-1.0)),
                 reads=[t_pb[6]], writes=[t_EG[ch]])
            P.op("dve", (lambda e, k=k: e.tensor_tensor(sm[k][:, 6:8], pb[6][:, 0:2], g4[k][:, 0:4:2], ALU.add)),
                 reads=[t_pb[6], t_g4[k]], writes=[t_sm[k]])
            P.op("act", (lambda e, k=k: e.activation(sm[k][:, 8:10], sm[k][:, 6:8], AF.Exp)), reads=[t_sm[k]], writes=[t_sm[k]])
            P.op("dve", (lambda e, k=k, ch=ch: e.tensor_scalar(EA[:, ch, :], sm[k][:, 8:10], 1.0 / 16.0, None, ALU.mult)),
                 reads=[t_sm[k]], writes=[t_EA[ch]])
            grp(uT, t_uT, tc, 1024, 1536, 3)
            P.op("dve", (lambda e, k=k: e.tensor_scalar(X5[k][:, 0, :], pb[1][:, 0:256], sm[k][:, 4:5], None, ALU.mult)),
                 reads=[t_pb[1], t_sm[k]], writes=[t_X5[k]])
            P.op("dve", (lambda e, k=k: e.tensor_scalar(X5[k][:, 1, :], pb[1][:, 0:256], sm[k][:, 5:6], None, ALU.mult)),
                 reads=[t_pb[1], t_sm[k]], writes=[t_X5[k]])
            P.op("act", (lambda e, k=k: e.activation(X5[k][:, 2, :], pb[1][:, 256:512], AF.Copy)), reads=[t_pb[1]], writes=[t_X5[k]])
            P.op("dve", (lambda e, k=k, ch=ch: e.tensor_scalar(X5[k][:, 3, :], pb[1][:, 256:512], EA[:, ch, 0:1], None, ALU.mult)),
                 reads=[t_pb[1], t_EA[ch]], writes=[t_X5[k]])
            P.op("dve", (lambda e, k=k, ch=ch: e.tensor_scalar(X5[k][:, 4, :], pb[1][:, 256:512], EA[:, ch, 1:2], None, ALU.mult)),
                 reads=[t_pb[1], t_EA[ch]], writes=[t_X5[k]])
            P.dma("sp", tm_s[ch][:, 0:512], X5[k][:, 3:5, :].rearrange("p a d -> p (a d)"), f"to{k}", reads=[t_X5[k]], writes=[t_tm[ch]])
            grp(uT, t_uT, tc, 1536, 2048, 4)
            P.op("act", (lambda e, k=k: e.activation(GS[k][:, 0, :], pb[2][:, 256:512], AF.Tanh, scale=0.5)), reads=[t_pb[2]], writes=[t_GS[k]])
            P.op("dve", (lambda e, k=k: e.tensor_scalar(GS[k][:, 0, :], GS[k][:, 0, :], 0.5, 0.5, ALU.mult, ALU.add)), reads=[], writes=[t_GS[k]])
            P.op("act", (lambda e, k=k: e.activation(GS[k][:, 1:3, :], pb[3][:].rearrange("p (a d) -> p a d", a=2), AF.Silu)),
                 reads=[t_pb[3]], writes=[t_GS[k]])
            P.dma("sp", g_s[ch], GS[k][:].rearrange("p a d -> p (a d)"), f"go{k}", reads=[t_GS[k]], writes=[t_gs[ch]])
            P.op("act", (lambda e, k=k: e.activation(Y4[k][:], pb[4][:], AF.Copy)), reads=[t_pb[4]], writes=[t_Y4[k]])
            for s_ in range(3):
                for hf in range(2):
                    P.op("pe", (lambda e, k=k, s_=s_, hf=hf: e.transpose(pX[:, (s_ * 2 + hf) * 128:(s_ * 2 + hf + 1) * 128],
                                                                        X5[k][:, s_, hf * 128:(hf + 1) * 128], idb[:])),
                         reads=[t_X5[k], t_id], writes=[t_pX])
            P.op("act", (lambda e, k=k: e.activation(FMst[k][:], pX[:, 0:768], AF.Copy)), reads=[t_pX], writes=[t_FMst[k]])
            P.dma("sp", fm_s[ch], FMst[k][:], f"fo{k}", reads=[t_FMst[k]], writes=[t_fm[ch]])

            def nat(k=k, ch=ch):
                for a in range(4):
                    P.op("pe", (lambda e, k=k, a=a: e.transpose(pX[:, a * 128:(a + 1) * 128], Y4[k][:, a * 128:(a + 1) * 128], idb[:])),
                         reads=[t_Y4[k], t_id], writes=[t_pX])
                P.op("dve", (lambda e, k=k: e.tensor_copy(NAst[k][:], pX[:, 0:512].rearrange("p (a t) -> p a t", a=4))),
                     reads=[t_pX], writes=[t_NAst[k]])
                P.dma("sp", na_sv[:, :, ch * 128:(ch + 1) * 128], NAst[k][:], f"no{k}", reads=[t_NAst[k]], writes=[t_nas[ch]])
            pend_nat = nat
    pend_nat()

    if True:
        fmB = [[P.sb([128, 768], BF16) for _ in range(2)] for _ in range(2)]
        tmB = [[P.sb([128, 769], BF16) for _ in range(2)] for _ in range(2)]
        t_fmB = [[Tok(), Tok()], [Tok(), Tok()]]
        t_tmB = [[Tok(), Tok()], [Tok(), Tok()]]
        wT = [[P.sb([128, 128], BF16) for _ in range(2)] for _ in range(2)]; t_wT = [[Tok(), Tok()], [Tok(), Tok()]]
        Cf = [P.sb([128, 2, 257], F32) for _ in range(2)]; t_Cf = [Tok(), Tok()]
        Cb = [P.sb([128, 2, 257], BF16) for _ in range(2)]; t_Cb = [Tok(), Tok()]
        ctmp = [[P.sb([128, 2, 257], F32) for _ in range(2)] for _ in range(2)]; t_ctmp = [[Tok(), Tok()], [Tok(), Tok()]]
        dn = [P.sb([128, 4], F32) for _ in range(2)]; t_dn = [Tok(), Tok()]
        hst = [[P.sb([128, 256], F32) for _ in range(2)] for _ in range(2)]
        t_hst = [[Tok(), Tok()], [Tok(), Tok()]]
        t_hs = [[Tok() for _ in range(64)] for _ in range(2)]
        for d_ in range(2):
            P.op("pool", (lambda e, d_=d_: e.memset(Cf[d_][:], 0.0)), writes=[t_Cf[d_]])
        def m_front(c):
            kk = c % 2
            chs = (c, 63 - c)
            for d_ in range(2):
                ch = chs[d_]
                P.dma("sp", fmB[d_][kk][:], fm_s[ch], f"fl{d_}{kk}", reads=[t_fm[ch]], writes=[t_fmB[d_][kk]])
                P.dma("sp", tmB[d_][kk][:], tm_s[ch], f"tl{d_}{kk}", reads=[t_tm[ch]], writes=[t_tmB[d_][kk]])
            for d_ in range(2):
                ch = chs[d_]
                fmv = fmB[d_][kk][:].rearrange("p (a h t) -> p a h t", a=3, h=2)
                for hf in range(2):
                    P.op("pe", (lambda e, d_=d_, hf=hf, fmv=fmv: e.matmul(pb[0][:, d_ * 128:(d_ + 1) * 128], fmv[:, 2, hf, :], fmv[:, d_, hf, :],
                                                                         start=(hf == 0), stop=(hf == 1))),
                         reads=[t_fmB[d_][kk]], writes=[t_pb[0]])
                P.op("dve", (lambda e, d_=d_, ch=ch, kk=kk: e.scalar_tensor_tensor(wT[d_][kk][:], pb[0][:, d_ * 128:(d_ + 1) * 128], EA[:, ch, d_:d_ + 1], trs[:, d_, :],
                                                                           ALU.mult, ALU.mult)),
                     reads=[t_pb[0], t_EA[ch], t_tri], writes=[t_wT[d_][kk]])
            for d_ in range(2):
                ch = chs[d_]
                tmv = tmB[d_][kk]
                for hf in range(2):
                    P.op("pe", (lambda e, d_=d_, hf=hf, tmv=tmv: e.matmul(pb[3 + hf][:, 0:257], tmv[:, d_ * 256 + hf * 128:d_ * 256 + (hf + 1) * 128],
                                                                         tmv[:, 512:769], start=True, stop=True)),
                         reads=[t_tmB[d_][kk]], writes=[t_pb[3 + hf]])
                    P.op("act", (lambda e, d_=d_, hf=hf, ch=ch, kk=kk: e.activation(ctmp[d_][kk][:, hf, :], pb[3 + hf][:, 0:257], AF.Copy, scale=EG[:, ch, d_:d_ + 1])),
                         reads=[t_pb[3 + hf], t_EG[ch]], writes=[t_ctmp[d_][kk]])

        def m_back(c):
            kk = c % 2
            chs = (c, 63 - c)
            for d_ in range(2):
                ch = chs[d_]
                fmv = fmB[d_][kk][:].rearrange("p (a h t) -> p a h t", a=3, h=2)
                tmv = tmB[d_][kk]
                if c > 0:
                    for hf in range(2):
                        P.op("pe", (lambda e, d_=d_, hf=hf, fmv=fmv: e.matmul(pb[1 + d_][:, 0:257], fmv[:, d_, hf, :], Cb[d_][:, hf, :],
                                                                             start=(hf == 0), stop=False)),
                             reads=[t_fmB[d_][kk], t_Cb[d_]], writes=[t_pb[1 + d_]])
                P.op("pe", (lambda e, d_=d_, tmv=tmv, c=c, kk=kk: e.matmul(pb[1 + d_][:, 0:257], wT[d_][kk][:], tmv[:, 512:769], start=(c == 0), stop=True)),
                     reads=[t_wT[d_][kk], t_tmB[d_][kk]], writes=[t_pb[1 + d_]])
                P.op("act", (lambda e, d_=d_: e.activation(dn[d_][:, 0:1], pb[1 + d_][:, 256:257], AF.Abs)), reads=[t_pb[1 + d_]], writes=[t_dn[d_]])
                P.op("dve", (lambda e, d_=d_: e.tensor_scalar(dn[d_][:, 1:2], dn[d_][:, 0:1], 1.0, None, ALU.max)), reads=[t_dn[d_]], writes=[t_dn[d_]])
                P.op("dve", (lambda e, d_=d_: e.reciprocal(dn[d_][:, 2:3], dn[d_][:, 1:2])), reads=[t_dn[d_]], writes=[t_dn[d_]])
                P.op("dve", (lambda e, d_=d_, kk=kk: e.tensor_scalar(hst[d_][kk][:], pb[1 + d_][:, 0:256], dn[d_][:, 2:3], None, ALU.mult)),
                     reads=[t_pb[1 + d_], t_dn[d_]], writes=[t_hst[d_][kk]])
                P.dma("sp", h_s[d_][ch], hst[d_][kk][:], f"ho{d_}{kk}", reads=[t_hst[d_][kk]], writes=[t_hs[d_][ch]])
                P.op("dve", (lambda e, d_=d_, ch=ch, kk=kk: e.scalar_tensor_tensor(Cf[d_][:], Cf[d_][:], EG[:, ch, d_:d_ + 1], ctmp[d_][kk][:], ALU.mult, ALU.add)),
                     reads=[t_EG[ch], t_ctmp[d_][kk]], writes=[t_Cf[d_]])
                P.op("dve", (lambda e, d_=d_: e.tensor_copy(Cb[d_][:], Cf[d_][:])), reads=[t_Cf[d_]], writes=[t_Cb[d_]])

        hfb = P.sb([128, 2, 4, 256], F32); t_hfb = Tok()
        gfb = P.sb([128, 4, 512], F32); t_gfb = Tok()
        hsum = P.sb([128, 4, 256], F32); t_hsum = Tok()
        junk = P.sb([128, 256], F32); t_junk = Tok()
        fst = P.sb([128, 12], F32); t_fst = Tok()
        mo = P.sb([128, 4, 256], BF16); t_mo = Tok()
        moT = P.sb([128, 2, 4, 128], BF16); t_moT = Tok()

        def m_final4(g):
            c0 = 4 * g
            for d_ in range(2):
                P.dma("sp", hfb[:, d_, :, :], h_s[d_][c0:c0 + 4].rearrange("j p f -> p j f"), f"hl{d_}",
                      reads=[t_hs[d_][c0 + j] for j in range(4)], writes=[t_hfb])
            P.dma("sp", gfb[:], g_s[c0:c0 + 4].rearrange("j p f -> p j f")[:, :, 0:512], "gl0", reads=[t_gs[c0 + j] for j in range(4)], writes=[t_gfb])
            P.op("dve", (lambda e: e.tensor_tensor(hsum[:], hfb[:, 0, :, :], hfb[:, 1, :, :], ALU.add)), reads=[t_hfb], writes=[t_hsum])
            for j in range(4):
                P.op("act", (lambda e, j=j: e.activation(junk[:], hsum[:, j, :], AF.Square, accum_out=fst[:, j:j + 1])),
                     reads=[t_hsum], writes=[t_junk, t_fst])
            P.op("act", (lambda e: e.activation(fst[:, 4:8], fst[:, 0:4], AF.Ln, bias=epsb[:, 0:1], scale=1.0 / 256)), reads=[t_fst, t_eps], writes=[t_fst])
            P.op("act", (lambda e: e.activation(fst[:, 8:12], fst[:, 4:8], AF.Exp, scale=-0.5)), reads=[t_fst], writes=[t_fst])
            for j in range(4):
                P.op("dve", (lambda e, j=j: e.scalar_tensor_tensor(hsum[:, j, :], hsum[:, j, :], fst[:, 8 + j:9 + j], mlgb[:], ALU.mult, ALU.mult)),
                     reads=[t_fst, t_mlg], writes=[t_hsum])
            P.op("dve", (lambda e: e.tensor_tensor(hsum[:], hsum[:], gfb[:, :, 0:256], ALU.mult)), reads=[t_gfb], writes=[t_hsum])
            P.op("dve", (lambda e: e.tensor_tensor(mo[:], hsum[:], gfb[:, :, 256:512], ALU.mult)), reads=[t_gfb, t_hsum], writes=[t_mo])
            for hf in range(2):
                for j in range(4):
                    P.op("pe", (lambda e, hf=hf, j=j: e.transpose(pX[:, (hf * 4 + j) * 128:(hf * 4 + j + 1) * 128], mo[:, j, hf * 128:(hf + 1) * 128], idb[:])),
                         reads=[t_mo, t_id], writes=[t_pX])
            P.op("act", (lambda e: e.activation(moT[:].rearrange("p a j t -> p (a j t)"), pX[:, 0:1024], AF.Copy)), reads=[t_pX], writes=[t_moT])
            p_, q_ = c0 // 8, (c0 % 8) * 128
            P.dma("sp", mp[p_].rearrange("(a p) t -> p a t", p=128)[:, 0:2, q_:q_ + 512], moT[:].rearrange("p a j t -> p a (j t)"), "mo0",
                  reads=[t_moT], writes=[t_mpp[p_][(c0 % 8) + j] for j in range(4)])

    if True:
        qn = [P.sb([128, 2, 128], BF16) for _ in range(2)]; t_qn = [Tok(), Tok()]
        kn_ = [P.sb([128, 2, 640], BF16) for _ in range(2)]; t_kn = [Tok(), Tok()]
        vn = [P.sb([128, 5, 260], BF16) for _ in range(2)]; t_vn = [Tok(), Tok()]
        gzn = [P.sb([128, 256], F32) for _ in range(2)]; t_gzn = [Tok(), Tok()]
        pp = [P.sb([128, 5, 128], BF16) for _ in range(3)]; t_pp = [Tok() for _ in range(3)]
        rdn = P.sb([128, 4], F32); t_rdn = Tok()
        no = [P.sb([128, 256], BF16) for _ in range(2)]; t_no = [Tok(), Tok()]
        noT = [P.sb([128, 2, 128], BF16) for _ in range(2)]; t_noT = [Tok(), Tok()]
        vna_v = vna_s.rearrange("n p f -> p n f")
        cnt = [0]

        def n_tile(i):
            k = i % 2
            cf, r0, nb = na_tile_cfg(i)
            c0 = r0 // 2
            nk = 512 if nb == 4 else 576
            P.dma("sp", qn[k][:], na_sv[:, 0:2, i * 128:(i + 1) * 128], f"nq{k}", reads=[t_nas[i]], writes=[t_qn[k]])
            P.dma("sp", kn_[k][:, :, 0:nk], na_sv[:, 2:4, r0 * 64:r0 * 64 + nk], f"nk{k}",
                  reads=[t_nas[cc] for cc in range(c0, c0 + nb)], writes=[t_kn[k]])
            P.dma("sp", vn[k][:, 0:nb, :], vna_v[:, c0:c0 + nb, :], f"nv{k}", reads=[t_vna[cc] for cc in range(c0, c0 + nb)], writes=[t_vn[k]])
            P.dma("sp", gzn[k][:], g_s[i][:, 512:768], f"ng{k}", reads=[t_gs[i]], writes=[t_gzn[k]])
            ob = 6
            sbanks = (5, 7)

            def s_mm(h):
                pr, off = h // 2, (h % 2) * 64
                sb_ = sbanks[h % 2]
                for bl in range(nb):
                    kn = 128 if bl < 4 else 64
                    if bl < 4:
                        dst, tk = pb[sb_][0:kn, bl * 128:(bl + 1) * 128], t_pb[sb_]
                    else:
                        dst, tk = pb[0][0:kn, 256 + 128 * (h % 2):384 + 128 * (h % 2)], t_pb[0]
                    P.op("pe", (lambda e, k=k, pr=pr, off=off, bl=bl, kn=kn, dst=dst: e.matmul(
                        dst, kn_[k][off:off + 64, pr, bl * 128:bl * 128 + kn], qn[k][off:off + 64, pr, :],
                        start=True, stop=True)), reads=[t_kn[k], t_qn[k]], writes=[tk])
            s_mm(0)
            for h in range(4):
                sb_ = sbanks[h % 2]
                pk = cnt[0] % 3
                cnt[0] += 1
                if h + 1 < 4:
                    s_mm(h + 1)
                P.op("act", (lambda e, pk=pk, sb_=sb_: e.activation(pp[pk][:, 0:4, :], pb[sb_][:].rearrange("p (b q) -> p b q", b=4), AF.Exp, scale=0.125)),
                     reads=[t_pb[sb_]], writes=[t_pp[pk]])
                if nb == 5:
                    P.op("act", (lambda e, pk=pk, h=h: e.activation(pp[pk][0:64, 4, :], pb[0][0:64, 256 + 128 * (h % 2):384 + 128 * (h % 2)], AF.Exp, scale=0.125)),
                         reads=[t_pb[0]], writes=[t_pp[pk]])
                P.op("dve", (lambda e, pk=pk, cf=cf, h=h: e.tensor_tensor(pp[pk][:, 0:4, :], pp[pk][:, 0:4, :], Et[:, cf, h * 5:h * 5 + 4, :], ALU.mult)),
                     reads=[t_E[cf]], writes=[t_pp[pk]])
                if nb == 5:
                    P.op("dve", (lambda e, pk=pk, cf=cf, h=h: e.tensor_tensor(pp[pk][0:64, 4, :], pp[pk][0:64, 4, :], Et[0:64, cf, h * 5 + 4, :], ALU.mult)),
                         reads=[t_E[cf]], writes=[t_pp[pk]])
                for bl in range(nb):
                    kn = 128 if bl < 4 else 64
                    P.op("pe", (lambda e, k=k, pk=pk, h=h, bl=bl, kn=kn, ob=ob: e.matmul(
                        pb[ob][:, h * 65:(h + 1) * 65], pp[pk][0:kn, bl, :], vn[k][0:kn, bl, h * 65:(h + 1) * 65],
                        start=(bl == 0), stop=(bl == nb - 1))), reads=[t_pp[pk], t_vn[k]], writes=[t_pb[ob]])
            ov = pb[ob][:, 0:260].rearrange("p (h d) -> p h d", h=4)
            P.op("dve", (lambda e, ov=ov: e.reciprocal(rdn[:], ov[:, :, 64])), reads=[t_pb[ob]], writes=[t_rdn])
            for h in range(4):
                P.op("dve", (lambda e, k=k, h=h, ov=ov: e.scalar_tensor_tensor(no[k][:, h * 64:(h + 1) * 64], ov[:, h, 0:64], rdn[:, h:h + 1],
                                                                             gzn[k][:, h * 64:(h + 1) * 64], ALU.mult, ALU.mult)),
                     reads=[t_pb[ob], t_rdn, t_gzn[k]], writes=[t_no[k]])
            for hf in range(2):
                P.op("pe", (lambda e, k=k, hf=hf: e.transpose(pX[:, hf * 128:(hf + 1) * 128], no[k][:, hf * 128:(hf + 1) * 128], idb[:])),
                     reads=[t_no[k], t_id], writes=[t_pX])
            P.op("act", (lambda e, k=k: e.activation(noT[k][:], pX[:, 0:256].rearrange("p (a t) -> p a t", a=2), AF.Copy)),
                 reads=[t_pX], writes=[t_noT[k]])
            P.dma("sp", mp[i // 8].rearrange("(a p) t -> p a t", p=128)[:, 2:4, (i % 8) * 128:(i % 8 + 1) * 128], noT[k][:], f"nn{k}",
                  reads=[t_noT[k]], writes=[t_mpp[i // 8][8 + i % 8]])


    def ag(p):
        P.coll("AllGather", GROUPS, mp[p], mg[p], f"cc{p % 4}", reads=t_mpp[p], writes=[io["t_mg"][p]])
    m_front(0)
    for c in range(64):
        if c + 1 < 64:
            m_front(c + 1)
        m_back(c)
        n_tile(c)
        if c >= 35 and c % 4 == 3:
            m_final4((c - 3) // 4)
            m_final4((63 - c) // 4)
            if c % 8 == 7:
                q = (c - 39) // 8
                ag(3 - q)
                ag(4 + q)


def na_bias_tables(rpb4):
    kp = np.arange(128)
    q = np.arange(128)
    bias = np.zeros((5, 128, 20, 128), np.float32)
    mask = np.zeros((5, 128, 5, 128), np.float32)
    for ci, i in enumerate(NA_CFG_TILES):
        _, r0k, nb = na_tile_cfg(i)
        qr = 2 * i + q // 64
        qc = q % 64
        rr0 = np.clip(qr - 4, 0, 120)
        cc0 = np.clip(qc - 8, 0, 48)
        for bl in range(nb):
            npart = 128 if bl < 4 else 64
            kr = r0k + bl * 2 + kp // 64
            kc = kp % 64
            valid = ((kr[:, None] >= rr0[None, :]) & (kr[:, None] < rr0[None, :] + 8) &
                     (kc[:, None] >= cc0[None, :]) & (kc[:, None] < cc0[None, :] + 16) & (kp[:, None] < npart))
            dr = np.clip(kr[:, None] - qr[None, :] + 7, 0, 14)
            dc = np.clip(kc[:, None] - qc[None, :] + 15, 0, 30)
            mask[ci, :, bl, :] = valid
            for h in range(4):
                bias[ci, :, h * 5 + bl, :] = rpb4[h][dr, dc]
    return bias, mask


def emit_outproj(nc, P, PSB, t_PSB, io, final):
    mg, t_mg, wout = io["mg"], io["t_mg"], io["wout"]
    m0 = P.mark()
    wb = P.sb([128, 16, D], BF16); t_wb = Tok()
    _tw = []
    for c in range(16):
        r, q = c // 4, c % 4
        row0 = (r * 256 + q * 128) if q < 2 else (1024 + r * 256 + (q - 2) * 128)
        _tw.append(Tok())
        P.dma("pool", wb[:, c, :], wout[row0:row0 + 128, :], "wb", writes=[_tw[-1]])
    w_join(P, _tw, t_wb)
    NBUF = 4
    mt = [P.sb([128, 16, 256], BF16) for _ in range(NBUF)]; t_mt = [Tok() for _ in range(NBUF)]
    xr = [P.sb([128, 8, 256], F32) for _ in range(NBUF)]; t_xr = [Tok() for _ in range(NBUF)]
    hT = [P.sb([128, 8, 256], F32) for _ in range(2)]; t_hT = [Tok(), Tok()]
    if final:
        ones_bf = P.sb([128, 128], BF16); t_ones = Tok()
        P.op("pool", lambda e: e.memset(ones_bf[:], 1.0), writes=[t_ones])
        epsb = P.sb([128, 1], F32); t_eps = Tok()
        P.op("pool", lambda e: e.memset(epsb[:], EPS), writes=[t_eps])
        gfs = P.sb([128, 8], F32); t_gf = Tok()
        P.dma("sp", gfs[:], io["gfin"][:, :], "c_ng", writes=[t_gf])
        xsq = P.sb([128, 8, 256], BF16); t_xsq = Tok()
        lnv = P.sb([128, 256], F32); t_lnv = Tok()
        rstd = P.sb([128, 256], F32); t_rstd = Tok()
        ob = [P.sb([128, 8, 256], F32) for _ in range(2)]; t_ob = [Tok(), Tok()]
    order = io.get("order", list(range(8)))
    for n_, i in enumerate(order):
        k = n_ % 2
        kl = n_ % NBUF

        def ld(e, i=i, k=kl):
            if "rank" not in P.__dict__:
                P.rank = e.partition_id() % 4
            r = P.rank
            return e.dma_start(out=mt[k][:], in_=mg[i].rearrange("(c p) t -> p c t", p=128)[:, :, bass.ts(r, 256)])
        P.dma_fn("sp", ld, f"ml{kl}", reads=[t_mg[i]], writes=[t_mt[kl]])
        if "xres" in io:
            P.dma("sp", xr[kl][:], io["xres"][i].rearrange("(c p) t -> p c t", p=128), f"xr{kl}", writes=[t_xr[kl]])
        else:
            P.dma("sp", xr[kl][:], io["hp_in"][i].rearrange("(c p) t -> p c t", p=128), f"xr{kl}", reads=[io["t_hp_in"][i]], writes=[t_xr[kl]])
        for n in range(8):
            bank = 1 + n % 4
            for c in range(16):
                P.op("pe", (lambda e, c=c, n=n, kl=kl, bank=bank: e.matmul(PSB[bank][:, 0:256], wb[:, c, n * 128:(n + 1) * 128], mt[kl][:, c, :],
                                                                      start=(c == 0), stop=(c == 15))),
                     reads=[t_wb, t_mt[kl]], writes=[t_PSB[bank]])
            P.op("dve", (lambda e, n=n, k=k, kl=kl, bank=bank: e.tensor_tensor(hT[k][:, n, :], PSB[bank][:, 0:256], xr[kl][:, n, :], ALU.add)),
                 reads=[t_PSB[bank], t_xr[kl]], writes=[t_hT[k]])
        if not final:
            hp, t_hp, hgl, t_hg = io["hp"], io["t_hp"], io["hg"], io["t_hg"]
            P.dma("sp", hp[i].rearrange("(c p) t -> p c t", p=128), hT[k][:], f"ho{k}", reads=[t_hT[k]], writes=[t_hp[i]])
            P.coll("AllGather", GROUPS, hp[i], hgl[i], f"cc{4 + i % 4}", reads=[t_hp[i]], writes=[t_hg[i]])
        else:
            P.op("act", (lambda e, k=k: e.activation(xsq[:], hT[k][:], AF.Square)), reads=[t_hT[k]], writes=[t_xsq])
            for c in range(8):
                P.op("pe", (lambda e, c=c: e.matmul(PSB[0][:, 0:256], ones_bf[:], xsq[:, c, :], start=(c == 0), stop=(c == 7))),
                     reads=[t_ones, t_xsq], writes=[t_PSB[0]])
            P.op("act", (lambda e: e.activation(lnv[:], PSB[0][:, 0:256], AF.Ln, bias=epsb[:, 0:1], scale=1.0 / D)),
                 reads=[t_PSB[0], t_eps], writes=[t_lnv])
            P.op("act", (lambda e: e.activation(rstd[:], lnv[:], AF.Exp, scale=-0.5)), reads=[t_lnv], writes=[t_rstd])
            for c in range(8):
                P.op("dve", (lambda e, c=c, k=k: e.scalar_tensor_tensor(ob[k][:, c, :], hT[k][:, c, :], gfs[:, c:c + 1], rstd[:], ALU.mult, ALU.mult)),
                     reads=[t_hT[k], t_gf, t_rstd], writes=[t_ob[k]])
            P.dma("sp", io["outT"][i].rearrange("(c p) t -> p c t", p=128), ob[k][:], f"oo{k}", reads=[t_ob[k]])
    P.barrier()
    P.release(m0)


def build_fused(nstages=4, a_kw=None, c_kw=None):
    nc = bass.Bass("TRN2", target_bir_lowering=False)
    din = lambda n, s_, d=F32: nc.dram_tensor(n, s_, d, kind="ExternalInput").ap()
    dint = lambda n, s_, d: nc.dram_tensor(n, s_, d, kind="Internal").ap()
    a_hT = din("a_hT", [D, S]); a_w = din("a_w", [D, NWA]); a_ng = din("a_ng", [128, 8])
    gateb = din("gateb", [1, 4]); mlg = din("mlg", [1, 256]); tri = din("tri", [2, 128, 128])
    nab = din("nab", [5, 128, 20, 128]); nam = din("nam", [5, 128, 5, 128]); ident = din("ident", [128, 128], BF16)
    b_wout = din("b_wout", [2048, D]); b_xres = din("b_xres", [8, D, 256])
    c_w = din("c_w", [D, NWC]); c_ng = din("c_ng", [128, 8]); qkg = din("qkg", [2, 128]); cs = din("cs", [S, 128]); cwb = din("cwb", [128, 8])
    d_wout = din("d_wout", [2048, D]); gfin = din("gfin", [128, 8])
    outT = nc.dram_tensor("outT", [8, D, 256], F32, kind="ExternalOutput").ap()
    mp0 = [dint(f"mp0_{i}", [512, 1024], BF16) for i in range(8)]
    mg0 = [dint(f"mg0_{i}", [2048, 1024], BF16) for i in range(8)]
    mp1 = [dint(f"mp1_{i}", [512, 1024], BF16) for i in range(8)]
    mg1 = [dint(f"mg1_{i}", [2048, 1024], BF16) for i in range(8)]
    hp = [dint(f"hp_{i}", [D, 256], F32) for i in range(8)]
    hgl = [dint(f"hg_{i}", [4 * D, 256], F32) for i in range(8)]
    t_mg0 = [Tok() for _ in range(8)]; t_mg1 = [Tok() for _ in range(8)]
    t_hp = [Tok() for _ in range(8)]; t_hg = [Tok() for _ in range(8)]

    P = Prog(nc)
    PSB = [P.ps([128, 512]) for _ in range(8)]
    t_PSB = [XTok() for _ in range(8)]
    PX = PSB[7][:].bitcast(BF16); t_PX = t_PSB[7]

    m = P.mark()
    emit_mixer0(nc, P, PSB, t_PSB, PX, t_PX, dict(hT=a_hT, w=a_w, ng=a_ng, gateb=gateb, mlg=mlg, tri=tri, nab=nab, nam=nam, ident=ident,
                                                 mp=mp0, mg=mg0, t_mg=t_mg0), **(a_kw or {}))
    P.barrier(); P.release(m)
    if nstages >= 2:
        emit_outproj(nc, P, PSB, t_PSB, dict(mg=mg0, t_mg=t_mg0, wout=b_wout, xres=b_xres, hp=hp, t_hp=t_hp, hg=hgl, t_hg=t_hg,
                                             order=[3, 4, 2, 5, 1, 6, 0, 7]), final=False)
    m = P.mark()
    if nstages >= 3:
        emit_mixer1(nc, P, PSB, t_PSB, PX, t_PX, dict(hg=hgl, t_hg=t_hg, w=c_w, ng=c_ng, qkg=qkg, cs=cs, cwb=cwb, ident=ident,
                                                     mp=mp1, mg=mg1, t_mg=t_mg1, order=[3, 4, 2, 5, 1, 6, 0, 7]), **(c_kw or {}))
    P.barrier(); P.release(m)
    if nstages >= 4:
        emit_outproj(nc, P, PSB, t_PSB, dict(mg=mg1, t_mg=t_mg1, wout=d_wout, hp_in=hp, t_hp_in=t_hp, gfin=gfin, outT=outT), final=True)
    P.build()
    return nc


def kernel(x, norm_g, final_g, ev_w_in, ev_gate_b, ev_w_out, ev_ml_norm_g, ev_na_rpb,
           od_w_in, od_w_out, od_q_norm_g, od_k_norm_g, od_conv_w, od_conv_b):
    f = lambda a: np.asarray(a, dtype=np.float32)
    x, norm_g, final_g = f(x), f(norm_g), f(final_g)
    ev_w_in, ev_gate_b, ev_w_out, ev_ml_norm_g, ev_na_rpb = f(ev_w_in)[0], f(ev_gate_b)[0], f(ev_w_out)[0], f(ev_ml_norm_g)[0], f(ev_na_rpb)[0]
    od_w_in, od_w_out, qg, kg, conv_w, conv_b = f(od_w_in)[0], f(od_w_out)[0], f(od_q_norm_g)[0], f(od_k_norm_g)[0], f(od_conv_w)[0], f(od_conv_b)[0]
    nc = build_fused()
    ident = np.eye(128, dtype=np.float32).astype(ml_dtypes.bfloat16)
    s_ = np.arange(128)
    tri = np.stack([(s_[:, None] <= s_[None, :]), (s_[:, None] >= s_[None, :])]).astype(np.float32)
    cs = rope_tables()
    xT = [np.ascontiguousarray(x[b].T) for b in range(2)]
    r256 = np.arange(256)
    in_maps = []
    for core in range(NCORES):
        b, hg = core // 4, core % 4
        cols0 = np.concatenate([
            0 + hg * 256 + r256, 1024 + hg * 256 + r256, 2048 + hg * 256 + r256, 3072 + hg * 256 + r256,
            4096 + hg * 256 + r256, 5136 + 3072 + hg * 256 + r256, 5136 + hg * 256 + r256, 5136 + 1024 + hg * 256 + r256,
            5136 + 2048 + hg * 256 + r256, 5120 + np.array([hg, 4 + hg, 8 + hg, 12 + hg])])
        gb = ev_gate_b[np.array([hg, 4 + hg, 8 + hg, 12 + hg])].reshape(1, 4)
        bias, mask = na_bias_tables(ev_na_rpb[4 * hg:4 * hg + 4])
        kvh = hg // 2
        cols1 = np.concatenate([
            hg * 256 + r256, 1024 + kvh * 128 + np.arange(128), 1280 + kvh * 128 + np.arange(128),
            1536 + hg * 256 + r256, 2560 + hg * 256 + r256, 3584 + hg * 256 + r256, 4608 + hg * 256 + r256, 5632 + hg * 256 + r256])
        cwb = np.zeros((128, 8), np.float32)
        for cc in range(2):
            ch = hg * 256 + cc * 128 + np.arange(128)
            for i in range(3):
                cwb[:, cc * 4 + i] = conv_w[i, ch]
            cwb[:, cc * 4 + 3] = conv_b[ch]
        xres = np.stack([xT[b][:, (4 * i + hg) * 256:(4 * i + hg + 1) * 256] for i in range(8)])
        in_maps.append({
            "a_hT": xT[b], "a_w": np.ascontiguousarray(ev_w_in[:, cols0]), "a_ng": np.ascontiguousarray(norm_g[0].reshape(8, 128).T),
            "gateb": np.ascontiguousarray(gb), "mlg": np.ascontiguousarray(ev_ml_norm_g[hg * 256:(hg + 1) * 256].reshape(1, 256)),
            "tri": tri, "nab": bias, "nam": mask, "ident": ident,
            "b_wout": np.ascontiguousarray(ev_w_out), "b_xres": np.ascontiguousarray(xres),
            "c_w": np.ascontiguousarray(od_w_in[:, cols1]), "c_ng": np.ascontiguousarray(norm_g[1].reshape(8, 128).T),
            "qkg": np.stack([qg, kg]), "cs": cs, "cwb": cwb,
            "d_wout": np.ascontiguousarray(od_w_out), "gfin": np.ascontiguousarray(final_g.reshape(8, 128).T)})
    res = run_bass_kernel_spmd(nc, in_maps, core_ids=list(range(NCORES)))
    out = np.empty((2, S, D), np.float32)
    for core in range(NCORES):
        b, tq = core // 4, core % 4
        o = res.results[core]["outT"]
        for i in range(8):
            out[b, (4 * i + tq) * 256:(4 * i + tq + 1) * 256, :] = o[i].T
    return out
```
